# Optimizing a Trainium2 kernel written in Bass

```python
import math
import jax, jax.numpy as jnp
from jax import lax
import numpy as np

D_MODEL = 1024
BATCH = 8
SEQ = 4096
DEPTH = 2

D_PLE = 256
A_HEADS = 4
A_DK = 128
A_DV = 128
A_WIDTH = A_HEADS * A_DK
B_HEADS = 8
B_DH = 64
B_WIDTH = B_HEADS * B_DH
D_MIX = A_WIDTH + B_WIDTH
IN_WIDTHS = (A_WIDTH, A_WIDTH, A_HEADS * A_DV, A_HEADS * A_DV,
             B_WIDTH, B_WIDTH, B_WIDTH, B_WIDTH)
D_IN = sum(IN_WIDTHS)
CHUNK = 64
Q_BLOCK = 128
EPS = 1e-6

kernel_name = "hymba_hgrn2_stickbreaking_trunk"


def rmsnorm(x, g):
    xf = x.astype(jnp.float32)
    y = xf * lax.rsqrt(jnp.mean(xf * xf, axis=-1, keepdims=True) + EPS)
    return (y * g.astype(jnp.float32)).astype(x.dtype)


def head_rmsnorm(o, g):
    B_, S_, H, d = o.shape
    y = o * lax.rsqrt(jnp.mean(o * o, axis=-1, keepdims=True) + EPS)
    return y.reshape(B_, S_, H * d) * g.astype(jnp.float32)


def hgrn2_chunkwise(q, k, v, log_f):
    B_, S_, H, dk = q.shape
    dv = v.shape[-1]
    n = S_ // CHUNK

    def to_chunks(t):
        return t.reshape(B_, n, CHUNK, H, t.shape[-1]).transpose(1, 0, 3, 2, 4)

    causal = jnp.tril(jnp.ones((CHUNK, CHUNK), dtype=bool))[:, :, None]

    def step(state, inp):
        qc, kc, vc, gc = inp
        b = jnp.cumsum(gc, axis=2)
        diff = b[:, :, :, None, :] - b[:, :, None, :, :]
        decay = jnp.where(causal, jnp.exp(jnp.where(causal, diff, 0.0)), 0.0)
        scores = jnp.einsum('bhtk,bhtsk,bhsk->bhts', qc, decay, kc)
        o = (jnp.einsum('bhts,bhsv->bhtv', scores, vc)
             + jnp.einsum('bhtk,bhkv->bhtv', qc * jnp.exp(b), state))
        b_last = b[:, :, -1:, :]
        state = (jnp.exp(b_last[:, :, 0, :, None]) * state
                 + jnp.einsum('bhsk,bhsv->bhkv', kc * jnp.exp(b_last - b), vc))
        return state, o

    s0 = jnp.zeros((B_, H, dk, dv), jnp.float32)
    _, o = lax.scan(step, s0, (to_chunks(q), to_chunks(k), to_chunks(v), to_chunks(log_f)))
    return o.transpose(1, 0, 3, 2, 4).reshape(B_, S_, H, dv)


def stick_breaking(q, k, v):
    S_ = q.shape[2]
    scale = B_DH ** -0.5
    outs = []
    for blk in range(S_ // Q_BLOCK):
        t0 = blk * Q_BLOCK
        t1 = t0 + Q_BLOCK
        z = jnp.einsum('bhtd,bhsd->bhts', q[:, :, t0:t1], k[:, :, :t1]) * scale
        mask = jnp.arange(t1)[None, :] < (t0 + jnp.arange(Q_BLOCK))[:, None]
        log_1m = jnp.where(mask, -jax.nn.softplus(z), 0.0)
        log_rest = lax.cumsum(log_1m, axis=3, reverse=True) - log_1m
        w = jnp.where(mask, jnp.exp(jax.nn.log_sigmoid(z) + log_rest), 0.0)
        outs.append(jnp.einsum('bhts,bhsd->bhtd', w, v[:, :, :t1]))
    return jnp.concatenate(outs, axis=2)


def mixer_layer(h, norm_g, w_in, a_norm_g, b_norm_g, w_out, lb):
    B_, S_, _ = h.shape
    f32 = jnp.float32
    u = rmsnorm(h, norm_g)
    proj = jnp.einsum('bsd,de->bse', u, w_in)
    a_q, a_f, a_i, a_g, b_q, b_k, b_v, b_g = jnp.split(
        proj, [int(c) for c in np.cumsum(IN_WIDTHS)[:-1]], axis=-1)

    lb = lb.astype(f32)
    k_a = (1.0 - lb) * jax.nn.sigmoid(-a_f.astype(f32))
    log_f = jnp.log1p(-k_a)
    q_a = jax.nn.silu(a_q.astype(f32))
    hd = lambda t, d: t.reshape(B_, S_, A_HEADS, d)
    o_a = hgrn2_chunkwise(hd(q_a, A_DK), hd(k_a, A_DK), hd(a_i.astype(f32), A_DV), hd(log_f, A_DK))
    o_a = head_rmsnorm(o_a, a_norm_g) * jax.nn.silu(a_g.astype(f32))

    to_heads = lambda t: t.astype(f32).reshape(B_, S_, B_HEADS, B_DH).transpose(0, 2, 1, 3)
    o_b = stick_breaking(to_heads(b_q), to_heads(b_k), to_heads(b_v)).transpose(0, 2, 1, 3)
    o_b = head_rmsnorm(o_b, b_norm_g) * jax.nn.silu(b_g.astype(f32))

    y = jnp.concatenate([o_a, o_b], axis=-1).astype(h.dtype)
    return h + jnp.einsum('bse,ed->bsd', y, w_out)


def setup_inputs(seed: int = 0) -> dict:
    key = jax.random.key(seed)
    ks = jax.random.split(key, 14)
    f32 = jnp.float32
    nrm = lambda k, shape, s: jax.random.normal(k, shape, f32) * s
    return {
        "x": nrm(ks[0], (BATCH, SEQ, D_MODEL), 1.0),
        "p": nrm(ks[1], (DEPTH, BATCH, SEQ, D_PLE), 1.0),
        "norm_mix": 1.0 + nrm(ks[2], (DEPTH, D_MODEL), 0.02),
        "w_in": nrm(ks[3], (DEPTH, D_MODEL, D_IN), D_MODEL ** -0.5),
        "a_out_norm": 1.0 + nrm(ks[4], (DEPTH, A_HEADS * A_DV), 0.02),
        "b_out_norm": 1.0 + nrm(ks[5], (DEPTH, B_WIDTH), 0.02),
        "w_out": nrm(ks[6], (DEPTH, D_MIX, D_MODEL), 0.5 * D_MIX ** -0.5),
        "lb_logits": nrm(ks[7], (DEPTH, A_WIDTH), 0.1),
        "ple_gate_norm": 1.0 + nrm(ks[8], (DEPTH, D_MODEL), 0.02),
        "w_ple_gate": nrm(ks[9], (DEPTH, D_MODEL, D_MODEL), D_MODEL ** -0.5),
        "w_ple_proj": nrm(ks[10], (DEPTH, D_PLE, D_MODEL), D_PLE ** -0.5),
        "ple_post_norm": 1.0 + nrm(ks[11], (DEPTH, D_MODEL), 0.02),
        "final_norm": 1.0 + nrm(ks[12], (D_MODEL,), 0.02),
    }


def reference(x, p, norm_mix, w_in, a_out_norm, b_out_norm, w_out, lb_logits,
              ple_gate_norm, w_ple_gate, w_ple_proj, ple_post_norm, final_norm):
    sm = jax.nn.softmax(lb_logits.astype(jnp.float32), axis=0)
    lower_bounds = jnp.cumsum(sm, axis=0) - sm[0:1]

    h = x
    for i in range(DEPTH):
        h = mixer_layer(h, norm_mix[i], w_in[i], a_out_norm[i], b_out_norm[i], w_out[i], lower_bounds[i])
        pe = rmsnorm(jnp.einsum('bsc,cd->bsd', p[i], w_ple_proj[i]), ple_post_norm[i])
        gate = jax.nn.sigmoid(jnp.einsum('bsd,de->bse', rmsnorm(h, ple_gate_norm[i]), w_ple_gate[i]))
        h = h + gate * pe
    return rmsnorm(h, final_norm)
```

```python
from contextlib import ExitStack
import numpy as np
import concourse.bass as bass
import concourse.mybir as mybir
from concourse.bass_utils import run_bass_kernel_spmd

F32 = mybir.dt.float32
BF16 = mybir.dt.bfloat16
AF = mybir.ActivationFunctionType
ALU = mybir.AluOpType

S = 4096
D = 1024
DEPTH = 2
NT = 8
NB = 32
EPS = 1e-6
DMA_RING = 8

C_ID, C_TRI, C_ONE, C_BA, C_BB, C_LT, C_LE, C_MR, C_END = 0, 128, 256, 384, 512, 640, 768, 832, 1344


class Tile:
    __slots__ = ("ap", "lw", "rd", "name")

    def __init__(self, ap, name=""):
        self.ap = ap
        self.lw = None
        self.rd = {}
        self.name = name


class Eng:
    def __init__(self, name, is_dma=False):
        self.name = name
        self.is_dma = is_dma
        self.instrs = []
        self.pending = set()
        self.gfirst = None


class Instr:
    __slots__ = ("fn", "deps", "signal", "val")

    def __init__(self, fn, deps):
        self.fn = fn
        self.deps = deps
        self.signal = False
        self.val = 0


class Plan:
    def __init__(self):
        self.pe = Eng("pe")
        self.act = Eng("act")
        self.dve = Eng("dve")
        self.pool = Eng("pool")
        self.sp = Eng("sp", is_dma=True)
        self.engs = [self.sp, self.pe, self.act, self.dve, self.pool]

    def op(self, eng, fn, reads=(), writes=()):
        deps = set()
        for t in reads:
            if t.lw is not None:
                deps.add(t.lw)
        for t in writes:
            if t.lw is not None:
                deps.add(t.lw)
            for e, s in t.rd.items():
                if e.is_dma:
                    for ss in s:
                        deps.add((e, ss))
                else:
                    deps.add((e, s))
        if eng is self.pe:
            deps = {d for d in deps if d[0] is not self.pe}
        if eng.pending:
            deps |= eng.pending
            eng.pending = set()
        seq = len(eng.instrs)
        if eng.gfirst is not None:
            if eng.gfirst < 0:
                eng.gfirst = seq
            else:
                eng.instrs[eng.gfirst].deps |= deps
                deps = set()
        eng.instrs.append(Instr(fn, deps))
        for t in reads:
            if eng.is_dma:
                t.rd.setdefault(eng, []).append(seq)
            else:
                t.rd[eng] = seq
        for t in writes:
            t.lw = (eng, seq)
            t.rd = {}
        return seq

    def group(self, eng):
        plan = self

        class _G:
            def __enter__(self_g):
                eng.gfirst = -1

            def __exit__(self_g, *a):
                eng.gfirst = None
        return _G()

    def barrier(self):
        deps = set()
        for e in self.engs:
            n = len(e.instrs)
            if n == 0:
                continue
            if e.is_dma:
                for s in range(max(0, n - DMA_RING), n):
                    deps.add((e, s))
            else:
                deps.add((e, n - 1))
        for e in self.engs:
            e.pending |= {d for d in deps if d[0] is not e or e.is_dma}

    def emit(self, nc, stack):
        for e in self.engs:
            for ins in e.instrs:
                for (de, ds) in ins.deps:
                    if not de.is_dma:
                        de.instrs[ds].signal = True
        for e in self.engs:
            if e.is_dma:
                continue
            c = 0
            for ins in e.instrs:
                if ins.signal:
                    c += 1
                    ins.val = c
        sems = {}
        for e in self.engs:
            if e.is_dma:
                sems[e] = [stack.enter_context(nc.semaphore(f"s_{e.name}{i}")) for i in range(DMA_RING)]
            else:
                sems[e] = stack.enter_context(nc.semaphore(f"s_{e.name}"))
        block = stack.enter_context(nc.Block())

        def resolve(dep):
            de, ds = dep
            if de.is_dma:
                return (sems[de][ds % DMA_RING], 16 * (ds // DMA_RING + 1))
            return (sems[de], de.instrs[ds].val)

        def run(e, beng):
            known = {}
            n = len(e.instrs)
            for seq, ins in enumerate(e.instrs):
                waits = {}
                if e.is_dma and seq >= DMA_RING:
                    s, v = resolve((e, seq - DMA_RING))
                    waits[s] = max(waits.get(s, 0), v)
                for dep in ins.deps:
                    s, v = resolve(dep)
                    if v > waits.get(s, 0):
                        waits[s] = v
                for s, v in waits.items():
                    if known.get(s, 0) >= v:
                        continue
                    beng.wait_ge(s, v)
                    known[s] = v
                bi = ins.fn(beng)
                if e.is_dma:
                    bi.then_inc(sems[e][seq % DMA_RING], 16)
                elif ins.signal:
                    bi.then_inc(sems[e], 1)
            if e.is_dma:
                for seq in range(max(0, n - DMA_RING), n):
                    s, v = resolve((e, seq))
                    if known.get(s, 0) < v:
                        beng.wait_ge(s, v)
                        known[s] = v

        @block.sync
        def _(sync):
            run(self.sp, sync)

        @block.tensor
        def _(tensor):
            run(self.pe, tensor)

        @block.scalar
        def _(scalar):
            run(self.act, scalar)

        @block.vector
        def _(vector):
            run(self.dve, vector)

        @block.gpsimd
        def _(gpsimd):
            run(self.pool, gpsimd)


class RPool:
    def __init__(self, tiles):
        self.tiles = tiles
        self.i = 0

    def next(self):
        t = self.tiles[self.i % len(self.tiles)]
        self.i += 1
        return t


def build_program(debug=False, n_layers=DEPTH, stop=None):
    nc = bass.Bass("TRN2", target_bir_lowering=False)
    dram = lambda name, shape, dt=F32, kind="ExternalInput": nc.dram_tensor(name, shape, dt, kind=kind).ap()
    x = dram("x", [S, D])
    p_in = dram("p", [DEPTH, S, 256])
    w_in = dram("w_in", [DEPTH, D, 4096])
    w_out = dram("w_out", [DEPTH, D, D])
    w_pg = dram("w_pg", [DEPTH, D, D])
    w_pp = dram("w_pp", [DEPTH, 256, D])
    vecs = dram("vecs", [7, 128, D])
    colv = dram("colv", [128, 24])
    cst = dram("cst", [128, C_END])
    out = dram("out", [S, D], F32, "ExternalOutput")
    hres = dram("hres", [S, D], F32, "Internal")
    yT = dram("yT", [D, S], BF16, "Internal")

    P = Plan()
    pe, act, dve, pool, sp = P.pe, P.act, P.dve, P.pool, P.sp
    st = ExitStack()
    sbt = lambda name, shape, dt=F32: st.enter_context(nc.sbuf_tensor(name, shape, dt))

    uT_sb = sbt("uT", [128, 8, S], BF16)
    uT_t = [Tile(uT_sb[:, :, j * 512:(j + 1) * 512], f"uT{j}") for j in range(NT)]
    cst_sb = sbt("cst_sb", [128, C_END])
    cbf_sb = sbt("cbf_sb", [128, C_LE], BF16)
    colv_sb = sbt("colv_sb", [128, 24])
    lbw_sb = sbt("lbw_sb", [128, 32])
    t_cst = Tile(cst_sb[:])
    t_cbf = Tile(cbf_sb[:])
    t_colv = Tile(colv_sb[:])
    t_lbw = Tile(lbw_sb[:])
    OVW = 34800
    ov = sbt("ov", [128, OVW])
    ovp = [0]

    ovmax = [0]

    def ov_reset():
        ovmax[0] = max(ovmax[0], ovp[0])
        ovp[0] = 0

    def ov_alloc(shape, dt=F32, name=""):
        n = int(np.prod(shape[1:]))
        words = n if dt == F32 else (n + 1) // 2
        a = ov[:, ovp[0]:ovp[0] + words]
        ovp[0] += words
        assert ovp[0] <= OVW, ("overlay overflow", ovp[0])
        if dt != F32:
            a = a.bitcast(dt)
        if len(shape) == 3:
            a = a.rearrange("p (a b) -> p a b", a=shape[1])
        return Tile(a, name)

    def ov_pool(n, shape, dt=F32, name=""):
        return RPool([ov_alloc(shape, dt, f"{name}{i}") for i in range(n)])

    psbig = [st.enter_context(nc.psum_tensor(f"ps{i}", [128, 1024], F32)) for i in range(4)]
    bank = [Tile(psbig[i // 2][:, (i % 2) * 512:(i % 2) * 512 + 512], f"bank{i}") for i in range(8)]

    ident_bf = cbf_sb[:, C_ID:C_ID + 128]
    tri_bf = cbf_sb[:, C_TRI:C_TRI + 128]
    ones_bf = cbf_sb[:, C_ONE:C_ONE + 128]
    blkA_bf = cbf_sb[:, C_BA:C_BA + 128]
    blkB_bf = cbf_sb[:, C_BB:C_BB + 128]
    mlt_bf = cbf_sb[:, C_LT:C_LT + 128]
    mle_f = cst_sb[:, C_LE:C_LE + 64]
    mreset_f = cst_sb[:, C_MR:C_MR + 512]

    P.op(sp, lambda e: e.dma_start(out=cst_sb[:], in_=cst), [], [t_cst])
    P.op(sp, lambda e: e.dma_start(out=colv_sb[:], in_=colv), [], [t_colv])
    P.op(dve, lambda e: e.tensor_copy(out=cbf_sb[:], in_=cst_sb[:, 0:C_LE]), [t_cst], [t_cbf])
    L0 = colv_sb[:, 16:20]
    L1 = colv_sb[:, 20:24]
    w = lambda a, b: lbw_sb[:, a:b]
    P.op(dve, lambda e: e.tensor_tensor(out=w(0, 4), in0=L0, in1=L1, op=ALU.max), [t_colv], [t_lbw])
    P.op(dve, lambda e: e.tensor_tensor(out=w(4, 8), in0=L0, in1=w(0, 4), op=ALU.subtract), [t_colv, t_lbw], [t_lbw])
    P.op(dve, lambda e: e.tensor_tensor(out=w(8, 12), in0=L1, in1=w(0, 4), op=ALU.subtract), [t_colv, t_lbw], [t_lbw])
    P.op(act, lambda e: e.activation(out=w(4, 12), in_=w(4, 12), func=AF.Exp), [t_lbw], [t_lbw])
    P.op(dve, lambda e: e.tensor_tensor(out=w(0, 4), in0=w(4, 8), in1=w(8, 12), op=ALU.add), [t_lbw], [t_lbw])
    P.op(dve, lambda e: e.reciprocal(out=w(0, 4), in_=w(0, 4)), [t_lbw], [t_lbw])
    P.op(dve, lambda e: e.tensor_tensor(out=w(4, 8), in0=w(4, 8), in1=w(0, 4), op=ALU.mult), [t_lbw], [t_lbw])
    P.op(dve, lambda e: e.tensor_tensor(out=w(8, 12), in0=w(8, 12), in1=w(0, 4), op=ALU.mult), [t_lbw], [t_lbw])
    P.op(dve, lambda e: e.tensor_tensor(out=w(12, 16), in0=w(4, 8), in1=w(8, 12), op=ALU.add), [t_lbw], [t_lbw])
    P.op(dve, lambda e: e.tensor_tensor(out=w(0, 4), in0=w(4, 8), in1=w(4, 8), op=ALU.subtract), [t_lbw], [t_lbw])
    P.op(dve, lambda e: e.tensor_tensor(out=w(12, 16), in0=w(12, 16), in1=w(4, 8), op=ALU.subtract), [t_lbw], [t_lbw])
    P.op(dve, lambda e: e.tensor_scalar(out=w(16, 20), in0=w(0, 4), scalar1=-1.0, scalar2=1.0, op0=ALU.mult, op1=ALU.add), [t_lbw], [t_lbw])
    P.op(dve, lambda e: e.tensor_scalar(out=w(20, 24), in0=w(12, 16), scalar1=-1.0, scalar2=1.0, op0=ALU.mult, op1=ALU.add), [t_lbw], [t_lbw])

    P.op(act, lambda e: e.activation(out=w(24, 32), in_=w(16, 24), func=AF.Ln), [t_lbw], [t_lbw])

    dbg = {}
    if debug:
        dbg["uT0"] = dram("d_uT0", [128, 8, S], BF16, "ExternalOutput")
        dbg["yT0"] = dram("d_yT0", [D, S], BF16, "ExternalOutput")
        dbg["h1"] = dram("d_h1", [S, D], F32, "ExternalOutput")

    def emit_u1(hb, nm, W, on_dve=False):
        junk = W["junk"].next()
        ssq = W["small"].next()
        if on_dve:
            P.op(dve, lambda e: e.scalar_tensor_tensor(out=junk.ap, in0=hb.ap, scalar=1.0, in1=hb.ap, op0=ALU.mult, op1=ALU.mult, accum_out=ssq.ap[:, 0:1]), [hb], [junk, ssq])
        else:
            P.op(act, lambda e: e.activation(out=junk.ap, in_=hb.ap, func=AF.Square, accum_out=ssq.ap[:, 0:1]), [hb], [junk, ssq])
        P.op(act, lambda e: e.activation(out=ssq.ap[:, 1:2], in_=ssq.ap[:, 0:1], func=AF.Ln, scale=1.0 / D, bias=W["eps"].ap), [ssq, W["eps"]], [ssq])
        P.op(act, lambda e: e.activation(out=ssq.ap[:, 2:3], in_=ssq.ap[:, 1:2], func=AF.Exp, scale=-0.5), [ssq], [ssq])
        ub = W["ub"].next()
        P.op(dve, lambda e: e.scalar_tensor_tensor(out=ub.ap, in0=hb.ap, scalar=ssq.ap[:, 2:3], in1=nm.ap, op0=ALU.mult, op1=ALU.mult), [hb, ssq, nm], [ub])
        return ub

    def emit_u2(ub, tb, W):
        tp = W["tp"].next()
        tpb = tp.ap.bitcast(BF16)
        with P.group(pe):
            for c in range(8):
                P.op(pe, lambda e, c=c: e.transpose(out=tpb[:, c * 128:(c + 1) * 128], in_=ub.ap[:, c * 128:(c + 1) * 128], identity=ident_bf), [ub, t_cbf], [tp])
        ut = uT_t[tb // 4]
        P.op(act, lambda e: e.activation(out=uT_sb[:, :, tb * 128:(tb + 1) * 128], in_=tpb.rearrange("p (c t) -> p c t", c=8), func=AF.Copy), [tp], [ut])

    def emit_u(hb, tb, nm, W, on_dve=False):
        emit_u2(emit_u1(hb, nm, W, on_dve), tb, W)

    def load_w(W, src_ap, rows_c=8):
        ws = W["wst"].next()
        wb = W["wbf"].next()
        P.op(sp, lambda e: e.dma_start(out=ws.ap[:, 0:rows_c, :], in_=src_ap.rearrange("(c p) n -> p c n", p=128)), [], [ws])
        P.op(pool, lambda e: e.tensor_copy(out=wb.ap[:, 0:rows_c, :], in_=ws.ap[:, 0:rows_c, :]), [ws], [wb])
        return wb

    def proj_fm(W, wb, j, bk):
        for c in range(8):
            P.op(pe, lambda e, c=c: e.matmul(bk.ap, lhsT=wb.ap[:, c, :], rhs=uT_sb[:, c, j * 512:(j + 1) * 512], start=(c == 0), stop=(c == 7)), [wb, uT_t[j]], [bk])

    def proj_tok(W, wb, vt, vsb, j):
        bk = W["pbank"].next()
        for q in range(4):
            tb = 4 * j + q
            for c in range(8):
                P.op(pe, lambda e, c=c, q=q, tb=tb: e.matmul(bk.ap[:, q * 128:(q + 1) * 128], lhsT=uT_sb[:, c, tb * 128:(tb + 1) * 128], rhs=wb.ap[:, c, :], start=(c == 0), stop=(c == 7)), [wb, uT_t[j]], [bk])
        P.op(dve, lambda e: e.tensor_copy(out=vsb[:, 4 * j:4 * j + 4, :], in_=bk.ap.rearrange("p (q d) -> p q d", q=4)), [bk], [vt])

    def silu_from_bank(W, bk, outt, lnscale_col=None, sign=-1.0):
        en = outt if lnscale_col is not None else W["f32"].next()
        P.op(act, lambda e: e.activation(out=en.ap, in_=bk.ap, func=AF.Exp, scale=sign), [bk], [en])
        P.op(act, lambda e: e.activation(out=en.ap, in_=en.ap, func=AF.Ln, bias=1.0), [en], [en])
        if lnscale_col is None:
            P.op(act, lambda e: e.activation(out=en.ap, in_=en.ap, func=AF.Exp, scale=-1.0), [en], [en])
            P.op(dve, lambda e: e.tensor_tensor(out=outt.ap, in0=bk.ap, in1=en.ap, op=ALU.mult), [bk, en], [outt])
        else:
            P.op(act, lambda e: e.activation(out=en.ap, in_=en.ap, func=AF.Exp, scale=-1.0, bias=lnscale_col), [en, t_lbw], [en])

    def head_norm_store(W, obank, blk_bf, gcol, sg, row0, j):
        o_sb = W["f32"].next()
        P.op(dve, lambda e: e.tensor_copy(out=o_sb.ap, in_=obank.ap), [obank], [o_sb])
        osq = W["bf"].next()
        P.op(pool, lambda e: e.tensor_tensor(out=osq.ap, in0=o_sb.ap, in1=o_sb.ap, op=ALU.mult), [o_sb], [osq])
        mb = W["mbank"].next()
        P.op(pe, lambda e: e.matmul(mb.ap, lhsT=blk_bf, rhs=osq.ap, start=True, stop=True), [osq, t_cbf], [mb])
        rs = W["f32"].next()
        P.op(act, lambda e: e.activation(out=rs.ap, in_=mb.ap, func=AF.Ln, bias=W["eps"].ap), [mb, W["eps"]], [rs])
        P.op(act, lambda e: e.activation(out=rs.ap, in_=rs.ap, func=AF.Exp, scale=-0.5), [rs], [rs])
        P.op(dve, lambda e: e.scalar_tensor_tensor(out=o_sb.ap, in0=o_sb.ap, scalar=gcol, in1=rs.ap, op0=ALU.mult, op1=ALU.mult), [o_sb, rs, t_colv], [o_sb])
        yb = W["bf"].next()
        P.op(dve, lambda e: e.tensor_tensor(out=yb.ap, in0=o_sb.ap, in1=sg.ap, op=ALU.mult), [o_sb, sg], [yb])
        P.op(sp, lambda e: e.dma_start(out=yT[row0:row0 + 128, j * 512:(j + 1) * 512], in_=yb.ap), [yb], [W["yT_t"]])

    def phase_A0():
        ov_reset()
        W = {}
        W["junk"] = ov_pool(1, [128, D], F32, "junk")
        W["small"] = ov_pool(4, [128, 4], F32, "small")
        W["ub"] = ov_pool(2, [128, D], BF16, "ub")
        W["tp"] = RPool([bank[6], bank[7]])
        W["eps"] = ov_alloc([128, 1], F32, "eps")
        P.op(pool, lambda e: e.memset(W["eps"].ap, EPS), [], [W["eps"]])
        nm = ov_alloc([128, D], F32, "nm")
        P.op(sp, lambda e: e.dma_start(out=nm.ap, in_=vecs[0]), [], [nm])
        hbp = ov_pool(3, [128, D], F32, "hb")
        prev = None
        for tb in range(NB):
            hb = hbp.next()
            P.op(sp, lambda e, hb=hb, tb=tb: e.dma_start(out=hb.ap, in_=x[tb * 128:(tb + 1) * 128, :]), [], [hb])
            ub = emit_u1(hb, nm, W, on_dve=(tb % 2 == 1))
            if prev is not None:
                emit_u2(*prev, W)
            prev = (ub, tb)
        emit_u2(*prev, W)

    def phase_BC(l, stop=None):
        ov_reset()
        yT_tile = Tile(None, "yT")

        W = {}
        W["eps"] = ov_alloc([128, 1], F32, "eps")
        P.op(pool, lambda e: e.memset(W["eps"].ap, EPS), [], [W["eps"]])
        W["wst"] = ov_pool(2, [128, 8, 128], F32, "wst")
        W["wbf"] = ov_pool(8, [128, 8, 128], BF16, "wbf")
        W["f32"] = ov_pool(4, [128, 512], F32, "f32")
        W["bf"] = ov_pool(3, [128, 512], BF16, "bf")
        W["yT_t"] = yT_tile
        W["pbank"] = RPool([bank[4]])
        W["mbank"] = RPool([bank[4]])
        vsb_t = ov_alloc([128, NB, 128], BF16, "vsb")
        vsb = vsb_t.ap
        vts = [Tile(vsb[:, 4 * j:4 * j + 4, :], f"v{j}") for j in range(NT)]
        sgp = ov_pool(2, [128, 512], F32, "sg")

        kT_one = ov_alloc([128, S], BF16, "kT")
        kT_ts = [kT_one, kT_one]
        kts_one = [Tile(kT_one.ap[:, j * 512:(j + 1) * 512]) for j in range(NT)]
        kts_p = [kts_one, kts_one]
        vsb_p = [vsb, vsb]
        vts_p = [vts, vts]
        qp = ov_pool(2, [128, 512], BF16, "q")
        ep = ov_pool(3, [128, 2, 512], F32, "e")
        spp = ov_pool(2, [128, 2, 512], BF16, "sp")
        wxp = ov_pool(2, [128, 2, 512], BF16, "wx")
        wp = ov_pool(2, [128, 2, 512], BF16, "w")
        Ap = ov_pool(4, [128, 2, 512], BF16, "A")
        zps, rps = psbig[0], psbig[1]
        z3 = zps[:, :].rearrange("p (h c) -> p h c", h=2)
        r3 = rps[:, :].rearrange("p (h c) -> p h c", h=2)
        zb = [bank[0], bank[1]]
        rb = [bank[2], bank[3]]
        ob = bank[6]
        mlt3 = mlt_bf.unsqueeze(1).broadcast_to([128, 2, 128])

        items = []
        wts = {}

        def loadw_b(hp):
            wts[hp] = tuple(load_w(W, w_in[l, :, base + hp * 128:base + hp * 128 + 128]) for base in (2048, 2560, 3072, 3584))

        tiles_l = [(hp, j) for hp in range(4) for j in range(NT)]
        for idx, (hp, j) in enumerate(tiles_l):
            if idx == 0:
                items.append(("loadw", 0))
                items.append(("prep", hp, j))
            n_it = 4 * j + 4
            for m, kb in enumerate(range(4 * j + 3, -1, -1)):
                items.append(("att", hp, j, kb))
                if j == 4 and m == 0 and hp < 3:
                    items.append(("loadw", hp + 1))
                if idx + 1 < len(tiles_l):
                    nxt = tiles_l[idx + 1]
                    if (nxt[1] != 0 and m == n_it // 2 - 1) or (nxt[1] == 0 and m == n_it - 1):
                        items.append(("prep",) + nxt)

        state = {}

        def prep(hp, j):
            wq, wk, wv, wg = wts[hp]
            kts = kts_p[hp % 2]
            vt, vs = vts_p[hp % 2][j], vsb_p[hp % 2]
            bk = W["pbank"].next()
            for q in range(4):
                tb = 4 * j + q
                with P.group(pe):
                    for c in range(8):
                        P.op(pe, lambda e, c=c, q=q, tb=tb, bk=bk: e.matmul(bk.ap[:, q * 128:(q + 1) * 128], lhsT=uT_sb[:, c, tb * 128:(tb + 1) * 128], rhs=wv.ap[:, c, :], start=(c == 0), stop=(c == 7)), [wv, uT_t[j]], [bk])
                yield
            P.op(dve, lambda e, bk=bk: e.tensor_copy(out=vs[:, 4 * j:4 * j + 4, :], in_=bk.ap.rearrange("p (q d) -> p q d", q=4)), [bk], [vt])
            bk1 = W["pbank"].next()
            with P.group(pe):
                proj_fm(W, wq, j, bk1)
            qt = qp.next()
            P.op(dve, lambda e: e.tensor_copy(out=qt.ap, in_=bk1.ap), [bk1], [qt])
            yield
            bk2 = W["pbank"].next()
            with P.group(pe):
                proj_fm(W, wk, j, bk2)
            P.op(dve, lambda e: e.tensor_copy(out=kts[j].ap, in_=bk2.ap), [bk2], [kts[j]])
            yield
            bk3 = W["pbank"].next()
            with P.group(pe):
                proj_fm(W, wg, j, bk3)
            sg = sgp.next()
            silu_from_bank(W, bk3, sg)
            state[(hp, j)] = dict(q=qt, sg=sg)

        def s1(it):
            _, hp, j, kb = it
            qt = state[(hp, j)]["q"]
            if kb == 4 * j + 3:
                a0, a1 = Ap.next(), Ap.next()
                P.op(pool, lambda e: e.memset(a0.ap, 0.0), [], [a0])
                P.op(pool, lambda e: e.memset(a1.ap, 0.0), [], [a1])
                state["A"] = [a0, a1]
            c0 = max(0, kb - 4 * j) * 128
            kj = kb // 4
            kT = kT_ts[hp % 2].ap
            kts = kts_p[hp % 2]
            with P.group(pe):
                for hh in range(2):
                    r = slice(hh * 64, hh * 64 + 64)
                    P.op(pe, lambda e, hh=hh, r=r: e.matmul(zps[:, hh * 512 + c0:hh * 512 + 512], lhsT=kT[r, kb * 128:(kb + 1) * 128], rhs=qt.ap[r, c0:512], start=True, stop=True), [kts[kj], qt], [zb[hh]])
            et = ep.next()
            P.op(act, lambda e: e.activation(out=et.ap[:, :, c0:512], in_=z3[:, :, c0:512], func=AF.Exp, scale=0.125), zb, [et])
            spt = spp.next()
            P.op(act, lambda e: e.activation(out=spt.ap[:, :, c0:512], in_=et.ap[:, :, c0:512], func=AF.Ln, bias=1.0), [et], [spt])
            if kb >= 4 * j:
                P.op(pool, lambda e: e.tensor_tensor(out=spt.ap[:, :, c0:c0 + 128], in0=spt.ap[:, :, c0:c0 + 128], in1=mlt3, op=ALU.mult), [spt, t_cbf], [spt])
            return dict(c0=c0, sp=spt, e=et, qt=qt, A=state["A"])

        def s2(it, ctx):
            _, hp, j, kb = it
            c0, spt, et = ctx["c0"], ctx["sp"], ctx["e"]
            n = (4 * j + 3) - kb
            acur, anxt = ctx["A"][n % 2], ctx["A"][(n + 1) % 2]
            with P.group(pe):
                for hh in range(2):
                    P.op(pe, lambda e, hh=hh: e.matmul(rps[:, hh * 512 + c0:hh * 512 + 512], lhsT=tri_bf, rhs=spt.ap[:, hh, c0:512], start=True, stop=(n == 0)), [spt, t_cbf], [rb[hh]])
                    if n > 0:
                        P.op(pe, lambda e, hh=hh: e.matmul(rps[:, hh * 512 + c0:hh * 512 + 512], lhsT=ones_bf, rhs=acur.ap[:, hh, c0:512], start=False, stop=True), [acur, t_cbf], [rb[hh]])
            if kb > 0:
                P.op(dve, lambda e: e.tensor_tensor(out=anxt.ap[:, :, c0:512], in0=acur.ap[:, :, c0:512], in1=spt.ap[:, :, c0:512], op=ALU.add), [acur, spt], [anxt])
            wx = wxp.next()
            P.op(act, lambda e: e.activation(out=wx.ap[:, :, c0:512], in_=r3[:, :, c0:512], func=AF.Exp, scale=-1.0), rb, [wx])
            wt = wp.next()
            P.op(dve, lambda e: e.tensor_tensor(out=wt.ap[:, :, c0:512], in0=et.ap[:, :, c0:512], in1=wx.ap[:, :, c0:512], op=ALU.mult), [et, wx], [wt])
            if kb >= 4 * j:
                P.op(pool, lambda e: e.tensor_tensor(out=wt.ap[:, :, c0:c0 + 128], in0=wt.ap[:, :, c0:c0 + 128], in1=mlt3, op=ALU.mult), [wt, t_cbf], [wt])
            ctx["w"] = wt

        def s3(it, ctx):
            _, hp, j, kb = it
            c0, wt = ctx["c0"], ctx["w"]
            with P.group(pe):
                for hh in range(2):
                    P.op(pe, lambda e, hh=hh: e.matmul(ob.ap[hh * 64:hh * 64 + 64, c0:512], lhsT=vsb_p[hp % 2][:, kb, hh * 64:hh * 64 + 64], rhs=wt.ap[:, hh, c0:512], start=(kb == 4 * j + 3), stop=(kb == 0), skip_group_check=True), [vts_p[hp % 2][kb // 4], wt], [ob])
            if kb == 0:
                head_norm_store(W, ob, blkB_bf, colv_sb[:, 8 + l * 4 + hp:8 + l * 4 + hp + 1], state[(hp, j)]["sg"], 512 + hp * 128, j)

        pend1 = None
        pend2 = None

        def flush():
            nonlocal pend1, pend2
            if pend2 is not None:
                s3(*pend2)
                pend2 = None
            if pend1 is not None:
                s2(*pend1)
                s3(*pend1)
                pend1 = None

        XB, YB = bank[5], bank[7]

        def a_stream():
            Ws = dict(W)
            Ws["pbank"] = RPool([XB])
            Ws["mbank"] = RPool([XB])
            Ws["f32"] = ov_pool(3, [128, 512], F32, "f32a")
            Ws["bf"] = ov_pool(3, [128, 512], BF16, "bfa")
            Ws["wbf"] = ov_pool(4, [128, 8, 128], BF16, "wbfa")
            oab = YB
            vbufs = [ov_alloc([128, 4, 128], BF16, f"va{i}") for i in range(2)]
            sgpa = ov_pool(2, [128, 512], F32, "sga")
            f32b = ov_pool(7, [128, 512], F32, "fa")
            bfb = ov_pool(4, [128, 512], BF16, "ba")
            khp = ov_pool(2, [128, 4, 128], BF16, "kht")
            atp = ov_pool(2, [128, 4, 64], BF16, "at")
            gdp = ov_pool(2, [128, 8], F32, "gd")
            s32p = ov_pool(2, [128, 128], F32, "s32")
            sbfp = ov_pool(2, [128, 128], BF16, "sbf")
            atv = XB.ap[:, 0:256]
            tpv = XB.ap[:, 256:512].bitcast(BF16)
            ui = 0
            for ha in range(4):
                wq = load_w(Ws, w_in[l, :, ha * 128:ha * 128 + 128])
                wf = load_w(Ws, w_in[l, :, 512 + ha * 128:512 + ha * 128 + 128])
                wi = load_w(Ws, w_in[l, :, 1024 + ha * 128:1024 + ha * 128 + 128])
                wg = load_w(Ws, w_in[l, :, 1536 + ha * 128:1536 + ha * 128 + 128])
                omlc = lbw_sb[:, 24 + l * 4 + ha:24 + l * 4 + ha + 1]
                s32 = s32p.next()
                sbf = sbfp.next()
                P.op(pool, lambda e, s32=s32: e.memset(s32.ap, 0.0), [], [s32])
                P.op(pool, lambda e, sbf=sbf: e.memset(sbf.ap, 0.0), [], [sbf])
                yield
                for j in range(NT):
                    vt = vbufs[ui % 2]
                    ui += 1
                    with P.group(pe):
                        bk = Ws["pbank"].next()
                        proj_fm(Ws, wq, j, bk)
                    qa = f32b.next()
                    silu_from_bank(Ws, bk, qa)
                    yield
                    with P.group(pe):
                        bk = Ws["pbank"].next()
                        proj_fm(Ws, wf, j, bk)
                    ka = f32b.next()
                    silu_from_bank(Ws, bk, ka, lnscale_col=omlc, sign=1.0)
                    yield
                    lf = f32b.next()
                    P.op(act, lambda e, lf=lf, ka=ka: e.activation(out=lf.ap, in_=ka.ap, func=AF.Ln, scale=-1.0, bias=1.0), [ka], [lf])
                    yield
                    with P.group(pe):
                        bk = Ws["pbank"].next()
                        proj_fm(Ws, wg, j, bk)
                    sg = sgpa.next()
                    silu_from_bank(Ws, bk, sg)
                    yield
                    bk = Ws["pbank"].next()
                    for q in range(4):
                        tb = 4 * j + q
                        with P.group(pe):
                            for c in range(8):
                                P.op(pe, lambda e, c=c, q=q, tb=tb, bk=bk, wi=wi: e.matmul(bk.ap[:, q * 128:(q + 1) * 128], lhsT=uT_sb[:, c, tb * 128:(tb + 1) * 128], rhs=wi.ap[:, c, :], start=(c == 0), stop=(c == 7)), [wi, uT_t[j]], [bk])
                        yield
                    P.op(dve, lambda e, bk=bk, vt=vt: e.tensor_copy(out=vt.ap, in_=bk.ap.rearrange("p (q d) -> p q d", q=4)), [bk], [vt])
                    yield
                    bt = f32b.next()
                    P.op(dve, lambda e, bt=bt, lf=lf: e.tensor_tensor_scan(out=bt.ap, data0=mreset_f, data1=lf.ap, initial=0.0, op0=ALU.mult, op1=ALU.add), [lf, t_cst], [bt])
                    yield
                    b3 = bt.ap.rearrange("p (c s) -> p c s", s=64)
                    d1 = f32b.next()
                    d13 = d1.ap.rearrange("p (c s) -> p c s", s=64)
                    P.op(dve, lambda e, d13=d13, b3=b3: e.tensor_tensor(out=d13, in0=b3, in1=b3[:, :, 31:32].broadcast_to([128, 8, 64]), op=ALU.subtract), [bt], [d1])
                    yield
                    d2 = f32b.next()
                    d23 = d2.ap.rearrange("p (c s) -> p c s", s=64)
                    P.op(dve, lambda e, d23=d23, b3=b3: e.tensor_tensor(out=d23, in0=b3[:, :, 63:64].broadcast_to([128, 8, 64]), in1=b3, op=ALU.subtract), [bt], [d2])
                    yield
                    ex = f32b.next()
                    P.op(act, lambda e, ex=ex, d1=d1: e.activation(out=ex.ap, in_=d1.ap, func=AF.Exp), [d1], [ex])
                    yield
                    P.op(act, lambda e, d1=d1: e.activation(out=d1.ap, in_=d1.ap, func=AF.Exp, scale=-1.0), [d1], [d1])
                    yield
                    P.op(act, lambda e, lf=lf, bt=bt: e.activation(out=lf.ap, in_=bt.ap, func=AF.Exp), [bt], [lf])
                    yield
                    P.op(act, lambda e, d2=d2: e.activation(out=d2.ap, in_=d2.ap, func=AF.Exp), [d2], [d2])
                    yield
                    gd = gdp.next()
                    P.op(act, lambda e, gd=gd, bt=bt: e.activation(out=gd.ap, in_=bt.ap[:, 63:512:64], func=AF.Exp), [bt], [gd])
                    yield
                    qin, kin, qdec, khat = bfb.next(), bfb.next(), bfb.next(), bfb.next()
                    P.op(dve, lambda e, qin=qin, qa=qa, ex=ex: e.tensor_tensor(out=qin.ap, in0=qa.ap, in1=ex.ap, op=ALU.mult), [qa, ex], [qin])
                    yield
                    P.op(dve, lambda e, kin=kin, ka=ka, d1=d1: e.tensor_tensor(out=kin.ap, in0=ka.ap, in1=d1.ap, op=ALU.mult), [ka, d1], [kin])
                    yield
                    P.op(dve, lambda e, khat=khat, ka=ka, d2=d2: e.tensor_tensor(out=khat.ap, in0=ka.ap, in1=d2.ap, op=ALU.mult), [ka, d2], [khat])
                    yield
                    P.op(dve, lambda e, qdec=qdec, qa=qa, lf=lf: e.tensor_tensor(out=qdec.ap, in0=qa.ap, in1=lf.ap, op=ALU.mult), [qa, lf], [qdec])
                    yield
                    with P.group(pe):
                        for q in range(4):
                            P.op(pe, lambda e, q=q, khat=khat: e.transpose(out=tpv[:, q * 128:(q + 1) * 128], in_=khat.ap[:, q * 128:(q + 1) * 128], identity=ident_bf), [khat, t_cbf], [XB])
                    kht = khp.next()
                    P.op(dve, lambda e, kht=kht: e.tensor_copy(out=kht.ap, in_=tpv.rearrange("p (q k) -> p q k", q=4)), [XB], [kht])
                    yield
                    with P.group(pe):
                        for c in range(8):
                            hf = (c % 2) * 64
                            P.op(pe, lambda e, c=c, hf=hf, kin=kin, qin=qin: e.matmul(XB.ap[hf:hf + 64, (c // 2) * 64:(c // 2) * 64 + 64], lhsT=kin.ap[:, c * 64:(c + 1) * 64], rhs=qin.ap[:, c * 64:(c + 1) * 64], start=True, stop=True), [kin, qin], [XB])
                    at = atp.next()
                    P.op(dve, lambda e, at=at: e.tensor_tensor(out=at.ap, in0=atv.rearrange("p (q t) -> p q t", q=4), in1=mle_f.unsqueeze(1).broadcast_to([128, 4, 64]), op=ALU.mult), [XB, t_cst], [at])
                    yield
                    for c in range(8):
                        rows = slice((c % 2) * 64, (c % 2) * 64 + 64)
                        sv = XB.ap[:, 256 + (c % 2) * 128:256 + (c % 2) * 128 + 128]
                        with P.group(pe):
                            P.op(pe, lambda e, c=c, sbf=sbf, qdec=qdec: e.matmul(oab.ap[:, c * 64:(c + 1) * 64], lhsT=sbf.ap, rhs=qdec.ap[:, c * 64:(c + 1) * 64], start=True, stop=False), [sbf, qdec], [oab])
                            P.op(pe, lambda e, c=c, rows=rows, at=at, vt=vt: e.matmul(oab.ap[:, c * 64:(c + 1) * 64], lhsT=vt.ap[rows, c // 2, :], rhs=at.ap[rows, c // 2, :], start=False, stop=True), [vt, at], [oab])
                            P.op(pe, lambda e, c=c, rows=rows, kht=kht, vt=vt, sv=sv: e.matmul(sv, lhsT=kht.ap[rows, c // 2, :], rhs=vt.ap[rows, c // 2, :], start=True, stop=True), [kht, vt], [XB])
                        s32n = s32p.next()
                        sbfn = sbfp.next()
                        P.op(dve, lambda e, sbfn=sbfn, s32=s32, gd=gd, c=c, sv=sv: e.scalar_tensor_tensor(out=sbfn.ap, in0=s32.ap, scalar=gd.ap[:, c:c + 1], in1=sv, op0=ALU.mult, op1=ALU.add), [s32, gd, XB], [sbfn])
                        P.op(dve, lambda e, s32n=s32n, s32=s32, gd=gd, c=c, sv=sv: e.scalar_tensor_tensor(out=s32n.ap, in0=s32.ap, scalar=gd.ap[:, c:c + 1], in1=sv, op0=ALU.mult, op1=ALU.add), [s32, gd, XB], [s32n])
                        s32, sbf = s32n, sbfn
                        yield
                    head_norm_store(Ws, oab, blkA_bf, colv_sb[:, l * 4 + ha:l * 4 + ha + 1], sg, ha * 128, j)
                    yield

        agen = [None if stop == "B" else a_stream()]
        A_RATE = 2

        def adv_a(k):
            for _ in range(k):
                if agen[0] is None:
                    return
                try:
                    next(agen[0])
                except StopIteration:
                    agen[0] = None

        gen = [None, 0]

        def advance(k):
            for _ in range(k):
                if gen[0] is None:
                    return
                try:
                    next(gen[0])
                except StopIteration:
                    gen[0] = None

        for ii, it in enumerate(items):
            if it[0] == "loadw":
                loadw_b(it[1])
            elif it[0] == "prep":
                advance(100)
                if it[2] == 0 and it[1] > 0:
                    flush()
                gen[0] = prep(*it[1:])
                n_left = 0
                for it2 in items[ii + 1:]:
                    if it2[0] == "att":
                        if (it2[1], it2[2]) == (it[1], it[2]):
                            break
                        n_left += 1
                gen[1] = 100 if n_left == 0 else -(-7 // n_left)
                if n_left == 0:
                    advance(100)
            else:
                if (it[1], it[2]) not in state:
                    advance(100)
                ctx = s1(it)
                if pend1 is not None:
                    s2(*pend1)
                if pend2 is not None:
                    s3(*pend2)
                pend2 = pend1
                pend1 = (it, ctx)
                advance(gen[1])
                adv_a(A_RATE)
        flush()
        adv_a(10 ** 9)

        return W["yT_t"]

    def phase_D(l, yT_tile, last):
        ov_reset()
        W = {}
        W["eps"] = ov_alloc([128, 1], F32, "eps")
        P.op(pool, lambda e: e.memset(W["eps"].ap, EPS), [], [W["eps"]])
        W["wst"] = ov_pool(2, [128, 8, 128], F32, "wst")
        W["junk"] = ov_pool(1, [128, D], BF16, "junk")
        W["small"] = ov_pool(12, [128, 4], F32, "small")
        W["ub"] = ov_pool(2, [128, D], BF16, "ub")
        W["tp"] = RPool([bank[6], bank[7]])
        wo = ov_alloc([128, 8, D], BF16, "wo")
        wg = ov_alloc([128, 8, D], BF16, "wg")
        wpj = ov_alloc([128, 2, D], BF16, "wpj")
        for g in range(8):
            for (dst, src, rc) in ((wo, w_out, 8), (wg, w_pg, 8), (wpj, w_pp, 2)):
                ws = W["wst"].next()
                P.op(sp, lambda e, ws=ws, src=src, g=g, rc=rc: e.dma_start(out=ws.ap[:, 0:rc, :], in_=src[l, :, g * 128:(g + 1) * 128].rearrange("(c p) n -> p c n", p=128)), [], [ws])
                P.op(pool, lambda e, ws=ws, dst=dst, g=g, rc=rc: e.tensor_copy(out=dst.ap[:, :, g * 128:(g + 1) * 128], in_=ws.ap[:, 0:rc, :]), [ws], [dst])
        gn = ov_alloc([128, D], F32, "gn")
        pn = ov_alloc([128, D], F32, "pn")
        nmn = ov_alloc([128, D], F32, "nmn")
        P.op(sp, lambda e: e.dma_start(out=gn.ap, in_=vecs[2 + l]), [], [gn])
        P.op(sp, lambda e: e.dma_start(out=pn.ap, in_=vecs[4 + l]), [], [pn])
        P.op(sp, lambda e: e.dma_start(out=nmn.ap, in_=vecs[6] if last else vecs[l + 1]), [], [nmn])
        hbp = ov_pool(2, [128, D], F32, "hb")
        hmp = ov_pool(3, [128, D], F32, "hm")
        gtp = ov_pool(1, [128, D], F32, "gt")
        pep = ov_pool(3, [128, D], F32, "pe")
        otp = ov_pool(1 if last else 0, [128, D], F32, "ot")
        hnp = ov_pool(2, [128, D], F32, "hn")
        ybp = ov_pool(2, [128, 8, 512], BF16, "yb")
        pbp = ov_pool(2, [128, 256], F32, "pb")
        pbfp = ov_pool(2, [128, 256], BF16, "pbf")
        ptp = ov_pool(2, [128, 2, 128], BF16, "pT")
        ugp = ov_pool(2, [128, 8, 128], BF16, "ugT")
        h_src = x if l == 0 else hres
        hres_t = Tile(None, "hres")
        mixP = (psbig[0], bank[0], bank[1])
        gP = (psbig[1], bank[2], bank[3])
        pP = (psbig[2], bank[4], bank[5])
        ybs = {}

        def rstd_chain(src_ap, src_tiles, on_dve=False):
            junk = W["junk"].next()
            sm = W["small"].next()
            if on_dve:
                P.op(dve, lambda e: e.scalar_tensor_tensor(out=junk.ap, in0=src_ap, scalar=1.0, in1=src_ap, op0=ALU.mult, op1=ALU.mult, accum_out=sm.ap[:, 0:1]), src_tiles, [junk, sm])
            else:
                P.op(act, lambda e: e.activation(out=junk.ap, in_=src_ap, func=AF.Square, accum_out=sm.ap[:, 0:1]), src_tiles, [junk, sm])
            P.op(act, lambda e: e.activation(out=sm.ap[:, 1:2], in_=sm.ap[:, 0:1], func=AF.Ln, scale=1.0 / D, bias=W["eps"].ap), [sm, W["eps"]], [sm])
            P.op(act, lambda e: e.activation(out=sm.ap[:, 2:3], in_=sm.ap[:, 1:2], func=AF.Exp, scale=-0.5), [sm], [sm])
            return sm

        ctxs = {}

        def st_a(tb):
            j, q = tb // 4, tb % 4
            if q == 0:
                yb = ybp.next()
                P.op(sp, lambda e: e.dma_start(out=yb.ap, in_=yT[:, j * 512:(j + 1) * 512].rearrange("(c p) t -> p c t", p=128)), [yT_tile], [yb])
                ybs[j] = yb
            yb = ybs[j]
            hb = hbp.next()
            P.op(sp, lambda e: e.dma_start(out=hb.ap, in_=h_src[tb * 128:(tb + 1) * 128, :]), [hres_t] if l > 0 else [], [hb])
            pb = pbp.next()
            P.op(sp, lambda e: e.dma_start(out=pb.ap, in_=p_in[l, tb * 128:(tb + 1) * 128, :]), [], [pb])
            with P.group(pe):
                for half in range(2):
                    bk = mixP[1 + half]
                    for c in range(8):
                        P.op(pe, lambda e, c=c, half=half, bk=bk: e.matmul(bk.ap, lhsT=yb.ap[:, c, q * 128:(q + 1) * 128], rhs=wo.ap[:, c, half * 512:(half + 1) * 512], start=(c == 0), stop=(c == 7)), [yb, wo], [bk])
            hm = hmp.next()
            P.op(dve, lambda e: e.tensor_tensor(out=hm.ap, in0=mixP[0][:, :], in1=hb.ap, op=ALU.add), [mixP[1], mixP[2], hb], [hm])
            pbf = pbfp.next()
            P.op(pool, lambda e: e.tensor_copy(out=pbf.ap, in_=pb.ap), [pb], [pbf])
            tp2 = W["tp"].next()
            tpv2 = tp2.ap.bitcast(BF16)
            with P.group(pe):
                for c in range(2):
                    P.op(pe, lambda e, c=c: e.transpose(out=tpv2[:, c * 128:(c + 1) * 128], in_=pbf.ap[:, c * 128:(c + 1) * 128], identity=ident_bf), [pbf, t_cbf], [tp2])
            pT = ptp.next()
            P.op(act, lambda e: e.activation(out=pT.ap, in_=tpv2[:, 0:256].rearrange("p (c t) -> p c t", c=2), func=AF.Copy), [tp2], [pT])
            ctxs[tb] = dict(hm=hm, pT=pT)

        def st_b1(tb):
            cx = ctxs[tb]
            hm = cx["hm"]
            sm = rstd_chain(hm.ap, [hm], on_dve=True)
            ug = W["ub"].next()
            P.op(dve, lambda e: e.scalar_tensor_tensor(out=ug.ap, in0=hm.ap, scalar=sm.ap[:, 2:3], in1=gn.ap, op0=ALU.mult, op1=ALU.mult), [hm, sm, gn], [ug])
            cx["ug"] = ug

        def st_b2(tb):
            cx = ctxs[tb]
            pT, ug = cx["pT"], cx["ug"]
            with P.group(pe):
                for half in range(2):
                    bk = pP[1 + half]
                    for c in range(2):
                        P.op(pe, lambda e, c=c, half=half, bk=bk: e.matmul(bk.ap, lhsT=pT.ap[:, c, :], rhs=wpj.ap[:, c, half * 512:(half + 1) * 512], start=(c == 0), stop=(c == 1)), [pT, wpj], [bk])
            sm2 = rstd_chain(pP[0][:, :], [pP[1], pP[2]])
            pet = pep.next()
            P.op(dve, lambda e: e.scalar_tensor_tensor(out=pet.ap, in0=pP[0][:, :], scalar=sm2.ap[:, 2:3], in1=pn.ap, op0=ALU.mult, op1=ALU.mult), [pP[1], pP[2], sm2, pn], [pet])
            tp = W["tp"].next()
            tpv = tp.ap.bitcast(BF16)
            with P.group(pe):
                for c in range(8):
                    P.op(pe, lambda e, c=c: e.transpose(out=tpv[:, c * 128:(c + 1) * 128], in_=ug.ap[:, c * 128:(c + 1) * 128], identity=ident_bf), [ug, t_cbf], [tp])
            ugT = ugp.next()
            P.op(act, lambda e: e.activation(out=ugT.ap, in_=tpv.rearrange("p (c t) -> p c t", c=8), func=AF.Copy), [tp], [ugT])
            cx["pet"] = pet
            cx["ugT"] = ugT

        def st_c(tb):
            cx = ctxs[tb]
            hm, pet, ugT = cx["hm"], cx["pet"], cx["ugT"]
            with P.group(pe):
                for half in range(2):
                    bk = gP[1 + half]
                    for c in range(8):
                        P.op(pe, lambda e, c=c, half=half, bk=bk: e.matmul(bk.ap, lhsT=ugT.ap[:, c, :], rhs=wg.ap[:, c, half * 512:(half + 1) * 512], start=(c == 0), stop=(c == 7)), [ugT, wg], [bk])
            gt = gtp.next()
            P.op(act, lambda e: e.activation(out=gt.ap, in_=gP[0][:, :], func=AF.Exp, scale=-1.0), [gP[1], gP[2]], [gt])
            P.op(act, lambda e: e.activation(out=gt.ap, in_=gt.ap, func=AF.Ln, bias=1.0), [gt], [gt])
            P.op(act, lambda e: e.activation(out=gt.ap, in_=gt.ap, func=AF.Exp, scale=-1.0), [gt], [gt])
            P.op(pool, lambda e: e.tensor_tensor(out=gt.ap, in0=gt.ap, in1=pet.ap, op=ALU.mult), [pet, gt], [gt])
            hn = hnp.next()
            P.op(pool, lambda e: e.tensor_tensor(out=hn.ap, in0=gt.ap, in1=hm.ap, op=ALU.add), [gt, hm], [hn])
            if not last:
                P.op(sp, lambda e: e.dma_start(out=hres[tb * 128:(tb + 1) * 128, :], in_=hn.ap), [hn], [hres_t])
                if debug and l == 0:
                    P.op(sp, lambda e: e.dma_start(out=dbg["h1"][tb * 128:(tb + 1) * 128, :], in_=hn.ap), [hn], [])
            cx["hn"] = hn

        def st_d1(tb):
            cx = ctxs[tb]
            hn = cx["hn"]
            if not last:
                cx["ub"] = emit_u1(hn, nmn, W, on_dve=True)
            else:
                sm3 = rstd_chain(hn.ap, [hn], on_dve=True)
                ot = otp.next()
                P.op(dve, lambda e: e.scalar_tensor_tensor(out=ot.ap, in0=hn.ap, scalar=sm3.ap[:, 2:3], in1=nmn.ap, op0=ALU.mult, op1=ALU.mult), [hn, sm3, nmn], [ot])
                P.op(sp, lambda e: e.dma_start(out=out[tb * 128:(tb + 1) * 128, :], in_=ot.ap), [ot], [])

        def st_d2(tb):
            if not last:
                emit_u2(ctxs[tb]["ub"], tb, W)
            del ctxs[tb]

        ok = lambda t: 0 <= t < NB
        for i in range(NB + 3):
            if ok(i - 1):
                st_b1(i - 1)
            if ok(i - 3):
                st_d1(i - 3)
            if ok(i):
                st_a(i)
            if ok(i - 2):
                st_c(i - 2)
            if ok(i - 1):
                st_b2(i - 1)
            if ok(i - 3):
                st_d2(i - 3)

    phase_A0()
    for l in range(n_layers):
        P.barrier()
        if debug and l == 0:
            for c in range(8):
                P.op(sp, lambda e, c=c: e.dma_start(out=dbg["uT0"][:, c, :], in_=uT_sb[:, c, :]), uT_t, [])
        if stop == "A0":
            break
        yt = phase_BC(l, stop)
        P.barrier()
        if debug and l == 0:
            for c in range(8):
                P.op(sp, lambda e, c=c: e.dma_start(out=dbg["yT0"][c * 128:(c + 1) * 128, :], in_=yT[c * 128:(c + 1) * 128, :]), [yt], [])
        if stop in ("B", "BC"):
            break
        phase_D(l, yt, last=(l == n_layers - 1))
    P.emit(nc, st)
    st.close()
    return nc


def make_consts():
    c = np.zeros((128, C_END), np.float32)
    i = np.arange(128)
    c[:, C_ID:C_ID + 128] = np.eye(128)
    c[:, C_TRI:C_TRI + 128] = (i[:, None] >= i[None, :])
    c[:, C_ONE:C_ONE + 128] = 1.0
    c[:, C_BA:C_BA + 128] = 1.0 / 128
    c[:, C_BB:C_BB + 128] = ((i[:, None] // 64) == (i[None, :] // 64)) / 64.0
    c[:, C_LT:C_LT + 128] = (i[:, None] < i[None, :])
    c[:, C_LE:C_LE + 64] = ((i[:, None] % 64) <= np.arange(64)[None, :])
    m = np.ones((128, 512), np.float32)
    m[:, ::64] = 0.0
    c[:, C_MR:C_MR + 512] = m
    return c


_NC_CACHE = {}


def kernel(x, p, norm_mix, w_in, a_out_norm, b_out_norm, w_out, lb_logits,
           ple_gate_norm, w_ple_gate, w_ple_proj, ple_post_norm, final_norm):
    f = lambda a: np.ascontiguousarray(np.asarray(a, dtype=np.float32))
    x, p = f(x), f(p)
    B = x.shape[0]
    rep = lambda v: np.broadcast_to(f(v)[None, :], (128, D))
    vecs = np.ascontiguousarray(np.stack([rep(norm_mix[0]), rep(norm_mix[1]), rep(ple_gate_norm[0]), rep(ple_gate_norm[1]),
                                          rep(ple_post_norm[0]), rep(ple_post_norm[1]), rep(final_norm)], axis=0))
    colv = np.zeros((128, 24), np.float32)
    aon, bon, lbl = f(a_out_norm), f(b_out_norm), f(lb_logits)
    for l in range(DEPTH):
        for h in range(4):
            colv[:, l * 4 + h] = aon[l, h * 128:(h + 1) * 128]
            colv[:, 8 + l * 4 + h] = bon[l, h * 128:(h + 1) * 128]
            colv[:, 16 + l * 4 + h] = lbl[l, h * 128:(h + 1) * 128]
    cst = make_consts()
    if "nc" not in _NC_CACHE:
        _NC_CACHE["nc"] = build_program()
    nc = _NC_CACHE["nc"]
    shared = dict(w_in=f(w_in), w_out=f(w_out), w_pg=f(w_ple_gate), w_pp=f(w_ple_proj), vecs=vecs, colv=colv, cst=cst)
    in_maps = []
    for b in range(B):
        d = dict(shared)
        d["x"] = np.ascontiguousarray(x[b])
        d["p"] = np.ascontiguousarray(p[:, b])
        in_maps.append(d)
    res = run_bass_kernel_spmd(nc, in_maps, core_ids=list(range(B)))
    return np.stack([np.asarray(r["out"], dtype=np.float32) for r in res.results], axis=0)
```

```python
from contextlib import ExitStack
import numpy as np
import concourse.bass as bass
import concourse.mybir as mybir
from concourse.bass_utils import run_bass_kernel_spmd

F32 = mybir.dt.float32
BF16 = mybir.dt.bfloat16
AF = mybir.ActivationFunctionType
ALU = mybir.AluOpType

S = 4096
D = 1024
DEPTH = 2
NT = 8
NB = 32
EPS = 1e-6
DMA_RING = 8

C_ID, C_TRI, C_ONE, C_BA, C_BB, C_LT, C_LE, C_MR, C_END = 0, 128, 256, 384, 512, 640, 768, 832, 1344


class Tile:
    __slots__ = ("ap", "lw", "rd", "name")

    def __init__(self, ap, name=""):
        self.ap = ap
        self.lw = None
        self.rd = {}
        self.name = name


class Eng:
    def __init__(self, name, is_dma=False):
        self.name = name
        self.is_dma = is_dma
        self.instrs = []
        self.pending = set()
        self.gfirst = None


class Instr:
    __slots__ = ("fn", "deps", "signal", "val")

    def __init__(self, fn, deps):
        self.fn = fn
        self.deps = deps
        self.signal = False
        self.val = 0


class Plan:
    def __init__(self):
        self.pe = Eng("pe")
        self.act = Eng("act")
        self.dve = Eng("dve")
        self.pool = Eng("pool")
        self.sp = Eng("sp", is_dma=True)
        self.engs = [self.sp, self.pe, self.act, self.dve, self.pool]

    def op(self, eng, fn, reads=(), writes=()):
        deps = set()
        for t in reads:
            if t.lw is not None:
                deps.add(t.lw)
        for t in writes:
            if t.lw is not None:
                deps.add(t.lw)
            for e, s in t.rd.items():
                if e.is_dma:
                    for ss in s:
                        deps.add((e, ss))
                else:
                    deps.add((e, s))
        if eng is self.pe:
            deps = {d for d in deps if d[0] is not self.pe}
        if eng.pending:
            deps |= eng.pending
            eng.pending = set()
        seq = len(eng.instrs)
        if eng.gfirst is not None:
            if eng.gfirst < 0:
                eng.gfirst = seq
            else:
                eng.instrs[eng.gfirst].deps |= deps
                deps = set()
        eng.instrs.append(Instr(fn, deps))
        for t in reads:
            if eng.is_dma:
                t.rd.setdefault(eng, []).append(seq)
            else:
                t.rd[eng] = seq
        for t in writes:
            t.lw = (eng, seq)
            t.rd = {}
        return seq

    def group(self, eng):
        plan = self

        class _G:
            def __enter__(self_g):
                eng.gfirst = -1

            def __exit__(self_g, *a):
                eng.gfirst = None
        return _G()

    def barrier(self):
        deps = set()
        for e in self.engs:
            n = len(e.instrs)
            if n == 0:
                continue
            if e.is_dma:
                for s in range(max(0, n - DMA_RING), n):
                    deps.add((e, s))
            else:
                deps.add((e, n - 1))
        for e in self.engs:
            e.pending |= {d for d in deps if d[0] is not e or e.is_dma}

    def emit(self, nc, stack):
        for e in self.engs:
            for ins in e.instrs:
                for (de, ds) in ins.deps:
                    if not de.is_dma:
                        de.instrs[ds].signal = True
        for e in self.engs:
            if e.is_dma:
                continue
            c = 0
            for ins in e.instrs:
                if ins.signal:
                    c += 1
                    ins.val = c
        sems = {}
        for e in self.engs:
            if e.is_dma:
                sems[e] = [stack.enter_context(nc.semaphore(f"s_{e.name}{i}")) for i in range(DMA_RING)]
            else:
                sems[e] = stack.enter_context(nc.semaphore(f"s_{e.name}"))
        block = stack.enter_context(nc.Block())

        def resolve(dep):
            de, ds = dep
            if de.is_dma:
                return (sems[de][ds % DMA_RING], 16 * (ds // DMA_RING + 1))
            return (sems[de], de.instrs[ds].val)

        def run(e, beng):
            known = {}
            n = len(e.instrs)
            for seq, ins in enumerate(e.instrs):
                waits = {}
                if e.is_dma and seq >= DMA_RING:
                    s, v = resolve((e, seq - DMA_RING))
                    waits[s] = max(waits.get(s, 0), v)
                for dep in ins.deps:
                    s, v = resolve(dep)
                    if v > waits.get(s, 0):
                        waits[s] = v
                for s, v in waits.items():
                    if known.get(s, 0) >= v:
                        continue
                    beng.wait_ge(s, v)
                    known[s] = v
                bi = ins.fn(beng)
                if e.is_dma:
                    bi.then_inc(sems[e][seq % DMA_RING], 16)
                elif ins.signal:
                    bi.then_inc(sems[e], 1)
            if e.is_dma:
                for seq in range(max(0, n - DMA_RING), n):
                    s, v = resolve((e, seq))
                    if known.get(s, 0) < v:
                        beng.wait_ge(s, v)
                        known[s] = v

        @block.sync
        def _(sync):
            run(self.sp, sync)

        @block.tensor
        def _(tensor):
            run(self.pe, tensor)

        @block.scalar
        def _(scalar):
            run(self.act, scalar)

        @block.vector
        def _(vector):
            run(self.dve, vector)

        @block.gpsimd
        def _(gpsimd):
            run(self.pool, gpsimd)


class RPool:
    def __init__(self, tiles):
        self.tiles = tiles
        self.i = 0

    def next(self):
        t = self.tiles[self.i % len(self.tiles)]
        self.i += 1
        return t


def build_program(debug=False, n_layers=DEPTH, stop=None):
    nc = bass.Bass("TRN2", target_bir_lowering=False)
    dram = lambda name, shape, dt=F32, kind="ExternalInput": nc.dram_tensor(name, shape, dt, kind=kind).ap()
    x = dram("x", [S, D])
    p_in = dram("p", [DEPTH, S, 256])
    w_in = dram("w_in", [DEPTH, D, 4096])
    w_out = dram("w_out", [DEPTH, D, D])
    w_pg = dram("w_pg", [DEPTH, D, D])
    w_pp = dram("w_pp", [DEPTH, 256, D])
    vecs = dram("vecs", [7, 128, D])
    colv = dram("colv", [128, 24])
    cst = dram("cst", [128, C_END])
    out = dram("out", [S, D], F32, "ExternalOutput")
    hres = dram("hres", [S, D], F32, "Internal")
    yT = dram("yT", [D, S], BF16, "Internal")

    P = Plan()
    pe, act, dve, pool, sp = P.pe, P.act, P.dve, P.pool, P.sp
    st = ExitStack()
    sbt = lambda name, shape, dt=F32: st.enter_context(nc.sbuf_tensor(name, shape, dt))

    uT_sb = sbt("uT", [128, 8, S], BF16)
    uT_t = [Tile(uT_sb[:, :, j * 512:(j + 1) * 512], f"uT{j}") for j in range(NT)]
    cst_sb = sbt("cst_sb", [128, C_END])
    cbf_sb = sbt("cbf_sb", [128, C_LE], BF16)
    colv_sb = sbt("colv_sb", [128, 24])
    lbw_sb = sbt("lbw_sb", [128, 32])
    t_cst = Tile(cst_sb[:])
    t_cbf = Tile(cbf_sb[:])
    t_colv = Tile(colv_sb[:])
    t_lbw = Tile(lbw_sb[:])
    OVW = 34800
    ov = sbt("ov", [128, OVW])
    ovp = [0]

    ovmax = [0]

    def ov_reset():
        ovmax[0] = max(ovmax[0], ovp[0])
        ovp[0] = 0

    def ov_alloc(shape, dt=F32, name=""):
        n = int(np.prod(shape[1:]))
        words = n if dt == F32 else (n + 1) // 2
        a = ov[:, ovp[0]:ovp[0] + words]
        ovp[0] += words
        assert ovp[0] <= OVW, ("overlay overflow", ovp[0])
        if dt != F32:
            a = a.bitcast(dt)
        if len(shape) == 3:
            a = a.rearrange("p (a b) -> p a b", a=shape[1])
        return Tile(a, name)

    def ov_pool(n, shape, dt=F32, name=""):
        return RPool([ov_alloc(shape, dt, f"{name}{i}") for i in range(n)])

    psbig = [st.enter_context(nc.psum_tensor(f"ps{i}", [128, 1024], F32)) for i in range(4)]
    bank = [Tile(psbig[i // 2][:, (i % 2) * 512:(i % 2) * 512 + 512], f"bank{i}") for i in range(8)]

    ident_bf = cbf_sb[:, C_ID:C_ID + 128]
    tri_bf = cbf_sb[:, C_TRI:C_TRI + 128]
    ones_bf = cbf_sb[:, C_ONE:C_ONE + 128]
    blkA_bf = cbf_sb[:, C_BA:C_BA + 128]
    blkB_bf = cbf_sb[:, C_BB:C_BB + 128]
    mlt_bf = cbf_sb[:, C_LT:C_LT + 128]
    mle_f = cst_sb[:, C_LE:C_LE + 64]
    mreset_f = cst_sb[:, C_MR:C_MR + 512]

    P.op(sp, lambda e: e.dma_start(out=cst_sb[:], in_=cst), [], [t_cst])
    P.op(sp, lambda e: e.dma_start(out=colv_sb[:], in_=colv), [], [t_colv])
    P.op(dve, lambda e: e.tensor_copy(out=cbf_sb[:], in_=cst_sb[:, 0:C_LE]), [t_cst], [t_cbf])
    L0 = colv_sb[:, 16:20]
    L1 = colv_sb[:, 20:24]
    w = lambda a, b: lbw_sb[:, a:b]
    P.op(dve, lambda e: e.tensor_tensor(out=w(0, 4), in0=L0, in1=L1, op=ALU.max), [t_colv], [t_lbw])
    P.op(dve, lambda e: e.tensor_tensor(out=w(4, 8), in0=L0, in1=w(0, 4), op=ALU.subtract), [t_colv, t_lbw], [t_lbw])
    P.op(dve, lambda e: e.tensor_tensor(out=w(8, 12), in0=L1, in1=w(0, 4), op=ALU.subtract), [t_colv, t_lbw], [t_lbw])
    P.op(act, lambda e: e.activation(out=w(4, 12), in_=w(4, 12), func=AF.Exp), [t_lbw], [t_lbw])
    P.op(dve, lambda e: e.tensor_tensor(out=w(0, 4), in0=w(4, 8), in1=w(8, 12), op=ALU.add), [t_lbw], [t_lbw])
    P.op(dve, lambda e: e.reciprocal(out=w(0, 4), in_=w(0, 4)), [t_lbw], [t_lbw])
    P.op(dve, lambda e: e.tensor_tensor(out=w(4, 8), in0=w(4, 8), in1=w(0, 4), op=ALU.mult), [t_lbw], [t_lbw])
    P.op(dve, lambda e: e.tensor_tensor(out=w(8, 12), in0=w(8, 12), in1=w(0, 4), op=ALU.mult), [t_lbw], [t_lbw])
    P.op(dve, lambda e: e.tensor_tensor(out=w(12, 16), in0=w(4, 8), in1=w(8, 12), op=ALU.add), [t_lbw], [t_lbw])
    P.op(dve, lambda e: e.tensor_tensor(out=w(0, 4), in0=w(4, 8), in1=w(4, 8), op=ALU.subtract), [t_lbw], [t_lbw])
    P.op(dve, lambda e: e.tensor_tensor(out=w(12, 16), in0=w(12, 16), in1=w(4, 8), op=ALU.subtract), [t_lbw], [t_lbw])
    P.op(dve, lambda e: e.tensor_scalar(out=w(16, 20), in0=w(0, 4), scalar1=-1.0, scalar2=1.0, op0=ALU.mult, op1=ALU.add), [t_lbw], [t_lbw])
    P.op(dve, lambda e: e.tensor_scalar(out=w(20, 24), in0=w(12, 16), scalar1=-1.0, scalar2=1.0, op0=ALU.mult, op1=ALU.add), [t_lbw], [t_lbw])

    P.op(act, lambda e: e.activation(out=w(24, 32), in_=w(16, 24), func=AF.Ln), [t_lbw], [t_lbw])

    dbg = {}
    if debug:
        dbg["uT0"] = dram("d_uT0", [128, 8, S], BF16, "ExternalOutput")
        dbg["yT0"] = dram("d_yT0", [D, S], BF16, "ExternalOutput")
        dbg["h1"] = dram("d_h1", [S, D], F32, "ExternalOutput")

    def emit_u1(hb, nm, W, on_dve=False):
        junk = W["junk"].next()
        ssq = W["small"].next()
        if on_dve:
            P.op(dve, lambda e: e.scalar_tensor_tensor(out=junk.ap, in0=hb.ap, scalar=1.0, in1=hb.ap, op0=ALU.mult, op1=ALU.mult, accum_out=ssq.ap[:, 0:1]), [hb], [junk, ssq])
        else:
            P.op(act, lambda e: e.activation(out=junk.ap, in_=hb.ap, func=AF.Square, accum_out=ssq.ap[:, 0:1]), [hb], [junk, ssq])
        P.op(act, lambda e: e.activation(out=ssq.ap[:, 1:2], in_=ssq.ap[:, 0:1], func=AF.Ln, scale=1.0 / D, bias=W["eps"].ap), [ssq, W["eps"]], [ssq])
        P.op(act, lambda e: e.activation(out=ssq.ap[:, 2:3], in_=ssq.ap[:, 1:2], func=AF.Exp, scale=-0.5), [ssq], [ssq])
        ub = W["ub"].next()
        P.op(dve, lambda e: e.scalar_tensor_tensor(out=ub.ap, in0=hb.ap, scalar=ssq.ap[:, 2:3], in1=nm.ap, op0=ALU.mult, op1=ALU.mult), [hb, ssq, nm], [ub])
        return ub

    def emit_u2(ub, tb, W):
        tp = W["tp"].next()
        tpb = tp.ap.bitcast(BF16)
        with P.group(pe):
            for c in range(8):
                P.op(pe, lambda e, c=c: e.transpose(out=tpb[:, c * 128:(c + 1) * 128], in_=ub.ap[:, c * 128:(c + 1) * 128], identity=ident_bf), [ub, t_cbf], [tp])
        ut = uT_t[tb // 4]
        P.op(act, lambda e: e.activation(out=uT_sb[:, :, tb * 128:(tb + 1) * 128], in_=tpb.rearrange("p (c t) -> p c t", c=8), func=AF.Copy), [tp], [ut])

    def emit_u(hb, tb, nm, W, on_dve=False):
        emit_u2(emit_u1(hb, nm, W, on_dve), tb, W)

    def load_w(W, src_ap, rows_c=8):
        ws = W["wst"].next()
        wb = W["wbf"].next()
        P.op(sp, lambda e: e.dma_start(out=ws.ap[:, 0:rows_c, :], in_=src_ap.rearrange("(c p) n -> p c n", p=128)), [], [ws])
        P.op(pool, lambda e: e.tensor_copy(out=wb.ap[:, 0:rows_c, :], in_=ws.ap[:, 0:rows_c, :]), [ws], [wb])
        return wb

    def proj_fm(W, wb, j, bk):
        for c in range(8):
            P.op(pe, lambda e, c=c: e.matmul(bk.ap, lhsT=wb.ap[:, c, :], rhs=uT_sb[:, c, j * 512:(j + 1) * 512], start=(c == 0), stop=(c == 7)), [wb, uT_t[j]], [bk])

    def proj_tok(W, wb, vt, vsb, j):
        bk = W["pbank"].next()
        for q in range(4):
            tb = 4 * j + q
            for c in range(8):
                P.op(pe, lambda e, c=c, q=q, tb=tb: e.matmul(bk.ap[:, q * 128:(q + 1) * 128], lhsT=uT_sb[:, c, tb * 128:(tb + 1) * 128], rhs=wb.ap[:, c, :], start=(c == 0), stop=(c == 7)), [wb, uT_t[j]], [bk])
        P.op(dve, lambda e: e.tensor_copy(out=vsb[:, 4 * j:4 * j + 4, :], in_=bk.ap.rearrange("p (q d) -> p q d", q=4)), [bk], [vt])

    def silu_from_bank(W, bk, outt, lnscale_col=None, sign=-1.0):
        en = outt if lnscale_col is not None else W["f32"].next()
        P.op(act, lambda e: e.activation(out=en.ap, in_=bk.ap, func=AF.Exp, scale=sign), [bk], [en])
        P.op(act, lambda e: e.activation(out=en.ap, in_=en.ap, func=AF.Ln, bias=1.0), [en], [en])
        if lnscale_col is None:
            P.op(act, lambda e: e.activation(out=en.ap, in_=en.ap, func=AF.Exp, scale=-1.0), [en], [en])
            P.op(dve, lambda e: e.tensor_tensor(out=outt.ap, in0=bk.ap, in1=en.ap, op=ALU.mult), [bk, en], [outt])
        else:
            P.op(act, lambda e: e.activation(out=en.ap, in_=en.ap, func=AF.Exp, scale=-1.0, bias=lnscale_col), [en, t_lbw], [en])

    def head_norm_store(W, obank, blk_bf, gcol, sg, row0, j):
        o_sb = W["f32"].next()
        P.op(dve, lambda e: e.tensor_copy(out=o_sb.ap, in_=obank.ap), [obank], [o_sb])
        osq = W["bf"].next()
        P.op(pool, lambda e: e.tensor_tensor(out=osq.ap, in0=o_sb.ap, in1=o_sb.ap, op=ALU.mult), [o_sb], [osq])
        mb = W["mbank"].next()
        P.op(pe, lambda e: e.matmul(mb.ap, lhsT=blk_bf, rhs=osq.ap, start=True, stop=True), [osq, t_cbf], [mb])
        rs = W["f32"].next()
        P.op(act, lambda e: e.activation(out=rs.ap, in_=mb.ap, func=AF.Ln, bias=W["eps"].ap), [mb, W["eps"]], [rs])
        P.op(act, lambda e: e.activation(out=rs.ap, in_=rs.ap, func=AF.Exp, scale=-0.5), [rs], [rs])
        P.op(dve, lambda e: e.scalar_tensor_tensor(out=o_sb.ap, in0=o_sb.ap, scalar=gcol, in1=rs.ap, op0=ALU.mult, op1=ALU.mult), [o_sb, rs, t_colv], [o_sb])
        yb = W["bf"].next()
        P.op(dve, lambda e: e.tensor_tensor(out=yb.ap, in0=o_sb.ap, in1=sg.ap, op=ALU.mult), [o_sb, sg], [yb])
        P.op(sp, lambda e: e.dma_start(out=yT[row0:row0 + 128, j * 512:(j + 1) * 512], in_=yb.ap), [yb], [W["yT_t"]])

    def phase_A0():
        ov_reset()
        W = {}
        W["junk"] = ov_pool(1, [128, D], F32, "junk")
        W["small"] = ov_pool(4, [128, 4], F32, "small")
        W["ub"] = ov_pool(2, [128, D], BF16, "ub")
        W["tp"] = RPool([bank[6], bank[7]])
        W["eps"] = ov_alloc([128, 1], F32, "eps")
        P.op(pool, lambda e: e.memset(W["eps"].ap, EPS), [], [W["eps"]])
        nm = ov_alloc([128, D], F32, "nm")
        P.op(sp, lambda e: e.dma_start(out=nm.ap, in_=vecs[0]), [], [nm])
        hbp = ov_pool(3, [128, D], F32, "hb")
        prev = None
        for tb in range(NB):
            hb = hbp.next()
            P.op(sp, lambda e, hb=hb, tb=tb: e.dma_start(out=hb.ap, in_=x[tb * 128:(tb + 1) * 128, :]), [], [hb])
            ub = emit_u1(hb, nm, W, on_dve=(tb % 2 == 1))
            if prev is not None:
                emit_u2(*prev, W)
            prev = (ub, tb)
        emit_u2(*prev, W)

    def phase_BC(l, stop=None):
        ov_reset()
        yT_tile = Tile(None, "yT")

        def common_alloc():
            ov_reset()
            W = {}
            W["eps"] = ov_alloc([128, 1], F32, "eps")
            P.op(pool, lambda e: e.memset(W["eps"].ap, EPS), [], [W["eps"]])
            W["wst"] = ov_pool(3, [128, 8, 128], F32, "wst")
            W["wbf"] = ov_pool(8, [128, 8, 128], BF16, "wbf")
            W["f32"] = ov_pool(6, [128, 512], F32, "f32")
            W["bf"] = ov_pool(4, [128, 512], BF16, "bf")
            W["yT_t"] = yT_tile
            W["pbank"] = RPool([bank[4], bank[5]])
            W["mbank"] = W["pbank"]
            vsb_t = ov_alloc([128, NB, 128], BF16, "vsb")
            vsb = vsb_t.ap
            vts = [Tile(vsb[:, 4 * j:4 * j + 4, :], f"v{j}") for j in range(NT)]
            sgp = ov_pool(2, [128, 512], F32, "sg")
            return W, vsb, vts, sgp

        W, vsb, vts, sgp = common_alloc()

        kT_ts = [ov_alloc([128, S], BF16, "kT0"), ov_alloc([128, S], BF16, "kT1")]
        kts_p = [[Tile(kT_ts[par].ap[:, j * 512:(j + 1) * 512]) for j in range(NT)] for par in range(2)]
        vsb2_t = ov_alloc([128, NB, 128], BF16, "vsb2")
        vsb_p = [vsb, vsb2_t.ap]
        vts_p = [vts, [Tile(vsb2_t.ap[:, 4 * j:4 * j + 4, :], f"v2_{j}") for j in range(NT)]]
        qp = ov_pool(3, [128, 512], BF16, "q")
        sgp = ov_pool(3, [128, 512], F32, "sgb")
        ep = ov_pool(3, [128, 2, 512], F32, "e")
        spp = ov_pool(3, [128, 2, 512], BF16, "sp")
        wxp = ov_pool(2, [128, 2, 512], F32, "wx")
        wp = ov_pool(3, [128, 2, 512], BF16, "w")
        Ap = ov_pool(4, [128, 2, 512], BF16, "A")
        zps, rps = psbig[0], psbig[1]
        z3 = zps[:, :].rearrange("p (h c) -> p h c", h=2)
        r3 = rps[:, :].rearrange("p (h c) -> p h c", h=2)
        zb = [bank[0], bank[1]]
        rb = [bank[2], bank[3]]
        obs = [bank[6], bank[7]]
        mlt3 = mlt_bf.unsqueeze(1).broadcast_to([128, 2, 128])

        items = []
        wts = {}

        def loadw_b(hp):
            wts[hp] = tuple(load_w(W, w_in[l, :, base + hp * 128:base + hp * 128 + 128]) for base in (2048, 2560, 3072, 3584))

        tiles_l = [(hp, j) for hp in range(4) for j in range(NT)]
        per_tile = []
        for idx, (hp, j) in enumerate(tiles_l):
            n_it = 4 * j + 4
            lst = []
            for m, kb in enumerate(range(4 * j + 3, -1, -1)):
                ent = [("att", hp, j, kb)]
                if j == 4 and m == 0 and hp < 3:
                    ent.append(("loadw", hp + 1))
                if m == n_it // 2 - 1 and idx + 1 < len(tiles_l):
                    ent.append(("prep",) + tiles_l[idx + 1])
                lst.append(ent)
            per_tile.append(lst)
        items.append(("loadw", 0))
        items.append(("prep",) + tiles_l[0])
        for ent in per_tile[0][:4]:
            items.extend(ent)
        for idx in range(len(tiles_l)):
            lst = per_tile[idx]
            n_it = len(lst)
            tail = lst[-4:] if n_it >= 8 else []
            body = lst[4:n_it - len(tail)]
            for ent in body:
                items.extend(ent)
            nxt_head = per_tile[idx + 1][:4] if idx + 1 < len(tiles_l) else []
            for q in range(4):
                if q < len(tail):
                    items.extend(tail[q])
                if q < len(nxt_head):
                    items.extend(nxt_head[q])

        state = {}

        def prep(hp, j):
            wq, wk, wv, wg = wts[hp]
            kts = kts_p[hp % 2]
            vt, vs = vts_p[hp % 2][j], vsb_p[hp % 2]
            bk = W["pbank"].next()
            for q in range(4):
                tb = 4 * j + q
                with P.group(pe):
                    for c in range(8):
                        P.op(pe, lambda e, c=c, q=q, tb=tb, bk=bk: e.matmul(bk.ap[:, q * 128:(q + 1) * 128], lhsT=uT_sb[:, c, tb * 128:(tb + 1) * 128], rhs=wv.ap[:, c, :], start=(c == 0), stop=(c == 7)), [wv, uT_t[j]], [bk])
                yield
            P.op(dve, lambda e, bk=bk: e.tensor_copy(out=vs[:, 4 * j:4 * j + 4, :], in_=bk.ap.rearrange("p (q d) -> p q d", q=4)), [bk], [vt])
            bk1 = W["pbank"].next()
            with P.group(pe):
                proj_fm(W, wq, j, bk1)
            qt = qp.next()
            P.op(dve, lambda e: e.tensor_copy(out=qt.ap, in_=bk1.ap), [bk1], [qt])
            yield
            bk2 = W["pbank"].next()
            with P.group(pe):
                proj_fm(W, wk, j, bk2)
            P.op(dve, lambda e: e.tensor_copy(out=kts[j].ap, in_=bk2.ap), [bk2], [kts[j]])
            yield
            bk3 = W["pbank"].next()
            with P.group(pe):
                proj_fm(W, wg, j, bk3)
            sg = sgp.next()
            silu_from_bank(W, bk3, sg)
            state[(hp, j)] = dict(q=qt, sg=sg)

        def s1(it):
            _, hp, j, kb = it
            qt = state[(hp, j)]["q"]
            if kb == 4 * j + 3:
                a0, a1 = Ap.next(), Ap.next()
                P.op(pool, lambda e: e.memset(a0.ap, 0.0), [], [a0])
                P.op(pool, lambda e: e.memset(a1.ap, 0.0), [], [a1])
                state[(hp, j)]["A"] = [a0, a1]
            c0 = max(0, kb - 4 * j) * 128
            kj = kb // 4
            kT = kT_ts[hp % 2].ap
            kts = kts_p[hp % 2]
            with P.group(pe):
                for hh in range(2):
                    r = slice(hh * 64, hh * 64 + 64)
                    P.op(pe, lambda e, hh=hh, r=r: e.matmul(zps[:, hh * 512 + c0:hh * 512 + 512], lhsT=kT[r, kb * 128:(kb + 1) * 128], rhs=qt.ap[r, c0:512], start=True, stop=True), [kts[kj], qt], [zb[hh]])
            et = ep.next()
            P.op(act, lambda e: e.activation(out=et.ap[:, :, c0:512], in_=z3[:, :, c0:512], func=AF.Exp, scale=0.125), zb, [et])
            spt = spp.next()
            P.op(act, lambda e: e.activation(out=spt.ap[:, :, c0:512], in_=et.ap[:, :, c0:512], func=AF.Ln, bias=1.0), [et], [spt])
            if kb >= 4 * j:
                P.op(pool, lambda e: e.tensor_tensor(out=spt.ap[:, :, c0:c0 + 128], in0=spt.ap[:, :, c0:c0 + 128], in1=mlt3, op=ALU.mult), [spt, t_cbf], [spt])
            return dict(c0=c0, sp=spt, e=et, qt=qt, A=state[(hp, j)]["A"])

        def s2(it, ctx):
            _, hp, j, kb = it
            c0, spt, et = ctx["c0"], ctx["sp"], ctx["e"]
            n = (4 * j + 3) - kb
            acur, anxt = ctx["A"][n % 2], ctx["A"][(n + 1) % 2]
            with P.group(pe):
                for hh in range(2):
                    P.op(pe, lambda e, hh=hh: e.matmul(rps[:, hh * 512 + c0:hh * 512 + 512], lhsT=tri_bf, rhs=spt.ap[:, hh, c0:512], start=True, stop=(n == 0)), [spt, t_cbf], [rb[hh]])
                    if n > 0:
                        P.op(pe, lambda e, hh=hh: e.matmul(rps[:, hh * 512 + c0:hh * 512 + 512], lhsT=ones_bf, rhs=acur.ap[:, hh, c0:512], start=False, stop=True), [acur, t_cbf], [rb[hh]])
            if kb > 0:
                P.op(dve, lambda e: e.tensor_tensor(out=anxt.ap[:, :, c0:512], in0=acur.ap[:, :, c0:512], in1=spt.ap[:, :, c0:512], op=ALU.add), [acur, spt], [anxt])
            wx = wxp.next()
            P.op(act, lambda e: e.activation(out=wx.ap[:, :, c0:512], in_=r3[:, :, c0:512], func=AF.Exp, scale=-1.0), rb, [wx])
            wt = wp.next()
            P.op(dve, lambda e: e.tensor_tensor(out=wt.ap[:, :, c0:512], in0=et.ap[:, :, c0:512], in1=wx.ap[:, :, c0:512], op=ALU.mult), [et, wx], [wt])
            if kb >= 4 * j:
                P.op(pool, lambda e: e.tensor_tensor(out=wt.ap[:, :, c0:c0 + 128], in0=wt.ap[:, :, c0:c0 + 128], in1=mlt3, op=ALU.mult), [wt, t_cbf], [wt])
            ctx["w"] = wt

        def s3(it, ctx):
            _, hp, j, kb = it
            c0, wt = ctx["c0"], ctx["w"]
            ob = obs[(hp * NT + j) % 2]
            with P.group(pe):
                for hh in range(2):
                    P.op(pe, lambda e, hh=hh: e.matmul(ob.ap[hh * 64:hh * 64 + 64, c0:512], lhsT=vsb_p[hp % 2][:, kb, hh * 64:hh * 64 + 64], rhs=wt.ap[:, hh, c0:512], start=(kb == 4 * j + 3), stop=(kb == 0), skip_group_check=True), [vts_p[hp % 2][kb // 4], wt], [ob])
            if kb == 0:
                head_norm_store(W, ob, blkB_bf, colv_sb[:, 8 + l * 4 + hp:8 + l * 4 + hp + 1], state[(hp, j)]["sg"], 512 + hp * 128, j)

        pend1 = None
        pend2 = None

        def flush():
            nonlocal pend1, pend2
            if pend2 is not None:
                s3(*pend2)
                pend2 = None
            if pend1 is not None:
                s2(*pend1)
                s3(*pend1)
                pend1 = None

        gen = [None, 0]

        def advance(k):
            for _ in range(k):
                if gen[0] is None:
                    return
                try:
                    next(gen[0])
                except StopIteration:
                    gen[0] = None

        for ii, it in enumerate(items):
            if it[0] == "loadw":
                loadw_b(it[1])
            elif it[0] == "prep":
                advance(100)
                gen[0] = prep(*it[1:])
                n_left = 0
                for it2 in items[ii + 1:]:
                    if it2[0] == "att":
                        if (it2[1], it2[2]) == (it[1], it[2]):
                            break
                        n_left += 1
                gen[1] = 100 if n_left == 0 else -(-7 // n_left)
                if n_left == 0:
                    advance(100)
            else:
                if (it[1], it[2]) not in state:
                    advance(100)
                ctx = s1(it)
                if pend1 is not None:
                    s2(*pend1)
                if pend2 is not None:
                    s3(*pend2)
                pend2 = pend1
                pend1 = (it, ctx)
                advance(gen[1])
        flush()

        if stop == "B":
            return W["yT_t"]
        P.barrier()
        ov_reset()
        Wc = {}
        Wc["eps"] = ov_alloc([128, 1], F32, "eps")
        P.op(pool, lambda e: e.memset(Wc["eps"].ap, EPS), [], [Wc["eps"]])
        Wc["wst"] = ov_pool(3, [128, 8, 128], F32, "wst")
        Wc["wbf"] = ov_pool(8, [128, 8, 128], BF16, "wbf")
        Wc["yT_t"] = yT_tile

        def a_stream(si, heads):
            Ws = dict(Wc)
            bs = bank[4 * si:4 * si + 4]
            Ws["pbank"] = RPool([bs[0]])
            Ws["mbank"] = RPool([bs[0]])
            Ws["f32"] = ov_pool(4, [128, 512], F32, f"f32_{si}")
            Ws["bf"] = ov_pool(3, [128, 512], BF16, f"bf_{si}")
            mixb, oab, stbk = bs[1], bs[2], bs[3]
            vsb_t = ov_alloc([128, NB, 128], BF16, f"vsb_{si}")
            vsb = vsb_t.ap
            vts = [Tile(vsb[:, 4 * j:4 * j + 4, :], f"v{si}_{j}") for j in range(NT)]
            sgp = ov_pool(2, [128, 512], F32, f"sg_{si}")
            f32b = ov_pool(9, [128, 512], F32, f"fa_{si}")
            bfb = ov_pool(8, [128, 512], BF16, f"ba_{si}")
            khp = ov_pool(2, [128, 4, 128], BF16, f"kht_{si}")
            atp = ov_pool(2, [128, 4, 64], BF16, f"at_{si}")
            gdp = ov_pool(2, [128, 8], F32, f"gd_{si}")
            s32p = ov_pool(2, [128, 128], F32, f"s32_{si}")
            sbfp = ov_pool(2, [128, 128], BF16, f"sbf_{si}")
            atv = mixb.ap[:, 0:256]
            tpv = stbk.ap[:, 256:512].bitcast(BF16)
            for ha in heads:
                wq = load_w(Ws, w_in[l, :, ha * 128:ha * 128 + 128])
                wf = load_w(Ws, w_in[l, :, 512 + ha * 128:512 + ha * 128 + 128])
                wi = load_w(Ws, w_in[l, :, 1024 + ha * 128:1024 + ha * 128 + 128])
                wg = load_w(Ws, w_in[l, :, 1536 + ha * 128:1536 + ha * 128 + 128])
                omlc = lbw_sb[:, 24 + l * 4 + ha:24 + l * 4 + ha + 1]
                s32 = s32p.next()
                sbf = sbfp.next()
                P.op(pool, lambda e, s32=s32: e.memset(s32.ap, 0.0), [], [s32])
                P.op(pool, lambda e, sbf=sbf: e.memset(sbf.ap, 0.0), [], [sbf])
                yield
                for j in range(NT):
                    with P.group(pe):
                        bk = Ws["pbank"].next()
                        proj_fm(Ws, wq, j, bk)
                    qa = f32b.next()
                    silu_from_bank(Ws, bk, qa)
                    yield
                    with P.group(pe):
                        bk = Ws["pbank"].next()
                        proj_fm(Ws, wf, j, bk)
                    ka = f32b.next()
                    silu_from_bank(Ws, bk, ka, lnscale_col=omlc, sign=1.0)
                    lf = f32b.next()
                    P.op(act, lambda e, lf=lf, ka=ka: e.activation(out=lf.ap, in_=ka.ap, func=AF.Ln, scale=-1.0, bias=1.0), [ka], [lf])
                    yield
                    yield
                    with P.group(pe):
                        bk = Ws["pbank"].next()
                        proj_fm(Ws, wg, j, bk)
                    sg = sgp.next()
                    silu_from_bank(Ws, bk, sg)
                    yield
                    bk = Ws["pbank"].next()
                    for q in range(4):
                        tb = 4 * j + q
                        with P.group(pe):
                            for c in range(8):
                                P.op(pe, lambda e, c=c, q=q, tb=tb, bk=bk, wi=wi: e.matmul(bk.ap[:, q * 128:(q + 1) * 128], lhsT=uT_sb[:, c, tb * 128:(tb + 1) * 128], rhs=wi.ap[:, c, :], start=(c == 0), stop=(c == 7)), [wi, uT_t[j]], [bk])
                    P.op(dve, lambda e, bk=bk, j=j: e.tensor_copy(out=vsb[:, 4 * j:4 * j + 4, :], in_=bk.ap.rearrange("p (q d) -> p q d", q=4)), [bk], [vts[j]])
                    yield
                    bt = f32b.next()
                    P.op(dve, lambda e, bt=bt, lf=lf: e.tensor_tensor_scan(out=bt.ap, data0=mreset_f, data1=lf.ap, initial=0.0, op0=ALU.mult, op1=ALU.add), [lf, t_cst], [bt])
                    yield
                    yield
                    b3 = bt.ap.rearrange("p (c s) -> p c s", s=64)
                    d1 = f32b.next()
                    d13 = d1.ap.rearrange("p (c s) -> p c s", s=64)
                    P.op(dve, lambda e, d13=d13, b3=b3: e.tensor_tensor(out=d13, in0=b3, in1=b3[:, :, 31:32].broadcast_to([128, 8, 64]), op=ALU.subtract), [bt], [d1])
                    yield
                    d2 = f32b.next()
                    d23 = d2.ap.rearrange("p (c s) -> p c s", s=64)
                    P.op(dve, lambda e, d23=d23, b3=b3: e.tensor_tensor(out=d23, in0=b3[:, :, 63:64].broadcast_to([128, 8, 64]), in1=b3, op=ALU.subtract), [bt], [d2])
                    yield
                    yield
                    ex = f32b.next()
                    ex2 = f32b.next()
                    ex3 = f32b.next()
                    P.op(act, lambda e, ex=ex, d1=d1: e.activation(out=ex.ap, in_=d1.ap, func=AF.Exp), [d1], [ex])
                    yield
                    P.op(act, lambda e, ex2=ex2, d1=d1: e.activation(out=ex2.ap, in_=d1.ap, func=AF.Exp, scale=-1.0), [d1], [ex2])
                    yield
                    P.op(act, lambda e, ex3=ex3, bt=bt: e.activation(out=ex3.ap, in_=bt.ap, func=AF.Exp), [bt], [ex3])
                    yield
                    P.op(act, lambda e, d2=d2: e.activation(out=d2.ap, in_=d2.ap, func=AF.Exp), [d2], [d2])
                    yield
                    gd = gdp.next()
                    P.op(act, lambda e, gd=gd, bt=bt: e.activation(out=gd.ap, in_=bt.ap[:, 63:512:64], func=AF.Exp), [bt], [gd])
                    yield
                    yield
                    qin, kin, qdec, khat = bfb.next(), bfb.next(), bfb.next(), bfb.next()
                    P.op(dve, lambda e, qin=qin, qa=qa, ex=ex: e.tensor_tensor(out=qin.ap, in0=qa.ap, in1=ex.ap, op=ALU.mult), [qa, ex], [qin])
                    yield
                    P.op(dve, lambda e, kin=kin, ka=ka, ex2=ex2: e.tensor_tensor(out=kin.ap, in0=ka.ap, in1=ex2.ap, op=ALU.mult), [ka, ex2], [kin])
                    yield
                    P.op(dve, lambda e, khat=khat, ka=ka, d2=d2: e.tensor_tensor(out=khat.ap, in0=ka.ap, in1=d2.ap, op=ALU.mult), [ka, d2], [khat])
                    yield
                    P.op(dve, lambda e, qdec=qdec, qa=qa, ex3=ex3: e.tensor_tensor(out=qdec.ap, in0=qa.ap, in1=ex3.ap, op=ALU.mult), [qa, ex3], [qdec])
                    yield
                    yield
                    with P.group(pe):
                        for q in range(4):
                            P.op(pe, lambda e, q=q, khat=khat: e.transpose(out=tpv[:, q * 128:(q + 1) * 128], in_=khat.ap[:, q * 128:(q + 1) * 128], identity=ident_bf), [khat, t_cbf], [stbk])
                    kht = khp.next()
                    P.op(dve, lambda e, kht=kht: e.tensor_copy(out=kht.ap, in_=tpv.rearrange("p (q k) -> p q k", q=4)), [stbk], [kht])
                    yield
                    with P.group(pe):
                        for c in range(8):
                            hf = (c % 2) * 64
                            P.op(pe, lambda e, c=c, hf=hf, kin=kin, qin=qin: e.matmul(mixb.ap[hf:hf + 64, (c // 2) * 64:(c // 2) * 64 + 64], lhsT=kin.ap[:, c * 64:(c + 1) * 64], rhs=qin.ap[:, c * 64:(c + 1) * 64], start=True, stop=True), [kin, qin], [mixb])
                    at = atp.next()
                    P.op(dve, lambda e, at=at: e.tensor_tensor(out=at.ap, in0=atv.rearrange("p (q t) -> p q t", q=4), in1=mle_f.unsqueeze(1).broadcast_to([128, 4, 64]), op=ALU.mult), [mixb, t_cst], [at])
                    yield
                    yield
                    mode[si] = "chunk"
                    for c in range(8):
                        tb = 4 * j + c // 2
                        rows = slice((c % 2) * 64, (c % 2) * 64 + 64)
                        with P.group(pe):
                            P.op(pe, lambda e, c=c, sbf=sbf, qdec=qdec: e.matmul(oab.ap[:, c * 64:(c + 1) * 64], lhsT=sbf.ap, rhs=qdec.ap[:, c * 64:(c + 1) * 64], start=True, stop=False), [sbf, qdec], [oab])
                            P.op(pe, lambda e, c=c, rows=rows, tb=tb, at=at: e.matmul(oab.ap[:, c * 64:(c + 1) * 64], lhsT=vsb[rows, tb, :], rhs=at.ap[rows, c // 2, :], start=False, stop=True), [vts[j], at], [oab])
                            P.op(pe, lambda e, c=c, rows=rows, tb=tb, kht=kht: e.matmul(stbk.ap[:, (c % 2) * 128:(c % 2) * 128 + 128], lhsT=kht.ap[rows, c // 2, :], rhs=vsb[rows, tb, :], start=True, stop=True), [kht, vts[j]], [stbk])
                        s32n = s32p.next()
                        sbfn = sbfp.next()
                        sv = stbk.ap[:, (c % 2) * 128:(c % 2) * 128 + 128]
                        P.op(dve, lambda e, sbfn=sbfn, s32=s32, gd=gd, c=c, sv=sv: e.scalar_tensor_tensor(out=sbfn.ap, in0=s32.ap, scalar=gd.ap[:, c:c + 1], in1=sv, op0=ALU.mult, op1=ALU.add), [s32, gd, stbk], [sbfn])
                        P.op(dve, lambda e, s32n=s32n, s32=s32, gd=gd, c=c, sv=sv: e.scalar_tensor_tensor(out=s32n.ap, in0=s32.ap, scalar=gd.ap[:, c:c + 1], in1=sv, op0=ALU.mult, op1=ALU.add), [s32, gd, stbk], [s32n])
                        s32, sbf = s32n, sbfn
                        yield
                    mode[si] = "op"
                    head_norm_store(Ws, oab, blkA_bf, colv_sb[:, l * 4 + ha:l * 4 + ha + 1], sg, ha * 128, j)
                    yield

        mode = {0: "op", 1: "op"}
        gens = {0: a_stream(0, [0, 1]), 1: a_stream(1, [2, 3])}

        def adv(si, k):
            for _ in range(k):
                if si not in gens:
                    return
                try:
                    next(gens[si])
                except StopIteration:
                    del gens[si]

        adv(0, 14)
        while gens:
            for si in (0, 1):
                if si not in gens:
                    continue
                other = 1 - si
                if mode[si] == "chunk":
                    adv(si, 1)
                elif other in gens and mode[other] == "chunk":
                    adv(si, 3)
                else:
                    adv(si, 1)
        W = Wc
        return W["yT_t"]

    def phase_D(l, yT_tile, last):
        ov_reset()
        W = {}
        W["eps"] = ov_alloc([128, 1], F32, "eps")
        P.op(pool, lambda e: e.memset(W["eps"].ap, EPS), [], [W["eps"]])
        W["wst"] = ov_pool(2, [128, 8, 128], F32, "wst")
        W["junk"] = ov_pool(1, [128, D], BF16, "junk")
        W["small"] = ov_pool(12, [128, 4], F32, "small")
        W["ub"] = ov_pool(2, [128, D], BF16, "ub")
        W["tp"] = RPool([bank[6], bank[7]])
        wo = ov_alloc([128, 8, D], BF16, "wo")
        wg = ov_alloc([128, 8, D], BF16, "wg")
        wpj = ov_alloc([128, 2, D], BF16, "wpj")
        for g in range(8):
            for (dst, src, rc) in ((wo, w_out, 8), (wg, w_pg, 8), (wpj, w_pp, 2)):
                ws = W["wst"].next()
                P.op(sp, lambda e, ws=ws, src=src, g=g, rc=rc: e.dma_start(out=ws.ap[:, 0:rc, :], in_=src[l, :, g * 128:(g + 1) * 128].rearrange("(c p) n -> p c n", p=128)), [], [ws])
                P.op(pool, lambda e, ws=ws, dst=dst, g=g, rc=rc: e.tensor_copy(out=dst.ap[:, :, g * 128:(g + 1) * 128], in_=ws.ap[:, 0:rc, :]), [ws], [dst])
        gn = ov_alloc([128, D], F32, "gn")
        pn = ov_alloc([128, D], F32, "pn")
        nmn = ov_alloc([128, D], F32, "nmn")
        P.op(sp, lambda e: e.dma_start(out=gn.ap, in_=vecs[2 + l]), [], [gn])
        P.op(sp, lambda e: e.dma_start(out=pn.ap, in_=vecs[4 + l]), [], [pn])
        P.op(sp, lambda e: e.dma_start(out=nmn.ap, in_=vecs[6] if last else vecs[l + 1]), [], [nmn])
        hbp = ov_pool(2, [128, D], F32, "hb")
        hmp = ov_pool(3, [128, D], F32, "hm")
        gtp = ov_pool(1, [128, D], F32, "gt")
        pep = ov_pool(3, [128, D], F32, "pe")
        otp = ov_pool(1 if last else 0, [128, D], F32, "ot")
        hnp = ov_pool(2, [128, D], F32, "hn")
        ybp = ov_pool(2, [128, 8, 512], BF16, "yb")
        pbp = ov_pool(2, [128, 256], F32, "pb")
        pbfp = ov_pool(2, [128, 256], BF16, "pbf")
        ptp = ov_pool(2, [128, 2, 128], BF16, "pT")
        ugp = ov_pool(2, [128, 8, 128], BF16, "ugT")
        h_src = x if l == 0 else hres
        hres_t = Tile(None, "hres")
        mixP = (psbig[0], bank[0], bank[1])
        gP = (psbig[1], bank[2], bank[3])
        pP = (psbig[2], bank[4], bank[5])
        ybs = {}

        def rstd_chain(src_ap, src_tiles, on_dve=False):
            junk = W["junk"].next()
            sm = W["small"].next()
            if on_dve:
                P.op(dve, lambda e: e.scalar_tensor_tensor(out=junk.ap, in0=src_ap, scalar=1.0, in1=src_ap, op0=ALU.mult, op1=ALU.mult, accum_out=sm.ap[:, 0:1]), src_tiles, [junk, sm])
            else:
                P.op(act, lambda e: e.activation(out=junk.ap, in_=src_ap, func=AF.Square, accum_out=sm.ap[:, 0:1]), src_tiles, [junk, sm])
            P.op(act, lambda e: e.activation(out=sm.ap[:, 1:2], in_=sm.ap[:, 0:1], func=AF.Ln, scale=1.0 / D, bias=W["eps"].ap), [sm, W["eps"]], [sm])
            P.op(act, lambda e: e.activation(out=sm.ap[:, 2:3], in_=sm.ap[:, 1:2], func=AF.Exp, scale=-0.5), [sm], [sm])
            return sm

        ctxs = {}

        def st_a(tb):
            j, q = tb // 4, tb % 4
            if q == 0:
                yb = ybp.next()
                P.op(sp, lambda e: e.dma_start(out=yb.ap, in_=yT[:, j * 512:(j + 1) * 512].rearrange("(c p) t -> p c t", p=128)), [yT_tile], [yb])
                ybs[j] = yb
            yb = ybs[j]
            hb = hbp.next()
            P.op(sp, lambda e: e.dma_start(out=hb.ap, in_=h_src[tb * 128:(tb + 1) * 128, :]), [hres_t] if l > 0 else [], [hb])
            pb = pbp.next()
            P.op(sp, lambda e: e.dma_start(out=pb.ap, in_=p_in[l, tb * 128:(tb + 1) * 128, :]), [], [pb])
            with P.group(pe):
                for half in range(2):
                    bk = mixP[1 + half]
                    for c in range(8):
                        P.op(pe, lambda e, c=c, half=half, bk=bk: e.matmul(bk.ap, lhsT=yb.ap[:, c, q * 128:(q + 1) * 128], rhs=wo.ap[:, c, half * 512:(half + 1) * 512], start=(c == 0), stop=(c == 7)), [yb, wo], [bk])
            hm = hmp.next()
            P.op(dve, lambda e: e.tensor_tensor(out=hm.ap, in0=mixP[0][:, :], in1=hb.ap, op=ALU.add), [mixP[1], mixP[2], hb], [hm])
            pbf = pbfp.next()
            P.op(pool, lambda e: e.tensor_copy(out=pbf.ap, in_=pb.ap), [pb], [pbf])
            tp2 = W["tp"].next()
            tpv2 = tp2.ap.bitcast(BF16)
            with P.group(pe):
                for c in range(2):
                    P.op(pe, lambda e, c=c: e.transpose(out=tpv2[:, c * 128:(c + 1) * 128], in_=pbf.ap[:, c * 128:(c + 1) * 128], identity=ident_bf), [pbf, t_cbf], [tp2])
            pT = ptp.next()
            P.op(act, lambda e: e.activation(out=pT.ap, in_=tpv2[:, 0:256].rearrange("p (c t) -> p c t", c=2), func=AF.Copy), [tp2], [pT])
            ctxs[tb] = dict(hm=hm, pT=pT)

        def st_b1(tb):
            cx = ctxs[tb]
            hm = cx["hm"]
            sm = rstd_chain(hm.ap, [hm], on_dve=True)
            ug = W["ub"].next()
            P.op(dve, lambda e: e.scalar_tensor_tensor(out=ug.ap, in0=hm.ap, scalar=sm.ap[:, 2:3], in1=gn.ap, op0=ALU.mult, op1=ALU.mult), [hm, sm, gn], [ug])
            cx["ug"] = ug

        def st_b2(tb):
            cx = ctxs[tb]
            pT, ug = cx["pT"], cx["ug"]
            with P.group(pe):
                for half in range(2):
                    bk = pP[1 + half]
                    for c in range(2):
                        P.op(pe, lambda e, c=c, half=half, bk=bk: e.matmul(bk.ap, lhsT=pT.ap[:, c, :], rhs=wpj.ap[:, c, half * 512:(half + 1) * 512], start=(c == 0), stop=(c == 1)), [pT, wpj], [bk])
            sm2 = rstd_chain(pP[0][:, :], [pP[1], pP[2]])
            pet = pep.next()
            P.op(dve, lambda e: e.scalar_tensor_tensor(out=pet.ap, in0=pP[0][:, :], scalar=sm2.ap[:, 2:3], in1=pn.ap, op0=ALU.mult, op1=ALU.mult), [pP[1], pP[2], sm2, pn], [pet])
            tp = W["tp"].next()
            tpv = tp.ap.bitcast(BF16)
            with P.group(pe):
                for c in range(8):
                    P.op(pe, lambda e, c=c: e.transpose(out=tpv[:, c * 128:(c + 1) * 128], in_=ug.ap[:, c * 128:(c + 1) * 128], identity=ident_bf), [ug, t_cbf], [tp])
            ugT = ugp.next()
            P.op(act, lambda e: e.activation(out=ugT.ap, in_=tpv.rearrange("p (c t) -> p c t", c=8), func=AF.Copy), [tp], [ugT])
            cx["pet"] = pet
            cx["ugT"] = ugT

        def st_c(tb):
            cx = ctxs[tb]
            hm, pet, ugT = cx["hm"], cx["pet"], cx["ugT"]
            with P.group(pe):
                for half in range(2):
                    bk = gP[1 + half]
                    for c in range(8):
                        P.op(pe, lambda e, c=c, half=half, bk=bk: e.matmul(bk.ap, lhsT=ugT.ap[:, c, :], rhs=wg.ap[:, c, half * 512:(half + 1) * 512], start=(c == 0), stop=(c == 7)), [ugT, wg], [bk])
            gt = gtp.next()
            P.op(act, lambda e: e.activation(out=gt.ap, in_=gP[0][:, :], func=AF.Exp, scale=-1.0), [gP[1], gP[2]], [gt])
            P.op(act, lambda e: e.activation(out=gt.ap, in_=gt.ap, func=AF.Ln, bias=1.0), [gt], [gt])
            P.op(act, lambda e: e.activation(out=gt.ap, in_=gt.ap, func=AF.Exp, scale=-1.0), [gt], [gt])
            P.op(pool, lambda e: e.tensor_tensor(out=gt.ap, in0=gt.ap, in1=pet.ap, op=ALU.mult), [pet, gt], [gt])
            hn = hnp.next()
            P.op(pool, lambda e: e.tensor_tensor(out=hn.ap, in0=gt.ap, in1=hm.ap, op=ALU.add), [gt, hm], [hn])
            if not last:
                P.op(sp, lambda e: e.dma_start(out=hres[tb * 128:(tb + 1) * 128, :], in_=hn.ap), [hn], [hres_t])
                if debug and l == 0:
                    P.op(sp, lambda e: e.dma_start(out=dbg["h1"][tb * 128:(tb + 1) * 128, :], in_=hn.ap), [hn], [])
            cx["hn"] = hn

        def st_d1(tb):
            cx = ctxs[tb]
            hn = cx["hn"]
            if not last:
                cx["ub"] = emit_u1(hn, nmn, W, on_dve=True)
            else:
                sm3 = rstd_chain(hn.ap, [hn], on_dve=True)
                ot = otp.next()
                P.op(dve, lambda e: e.scalar_tensor_tensor(out=ot.ap, in0=hn.ap, scalar=sm3.ap[:, 2:3], in1=nmn.ap, op0=ALU.mult, op1=ALU.mult), [hn, sm3, nmn], [ot])
                P.op(sp, lambda e: e.dma_start(out=out[tb * 128:(tb + 1) * 128, :], in_=ot.ap), [ot], [])

        def st_d2(tb):
            if not last:
                emit_u2(ctxs[tb]["ub"], tb, W)
            del ctxs[tb]

        ok = lambda t: 0 <= t < NB
        for i in range(NB + 3):
            if ok(i - 1):
                st_b1(i - 1)
            if ok(i - 3):
                st_d1(i - 3)
            if ok(i):
                st_a(i)
            if ok(i - 2):
                st_c(i - 2)
            if ok(i - 1):
                st_b2(i - 1)
            if ok(i - 3):
                st_d2(i - 3)

    phase_A0()
    for l in range(n_layers):
        P.barrier()
        if debug and l == 0:
            for c in range(8):
                P.op(sp, lambda e, c=c: e.dma_start(out=dbg["uT0"][:, c, :], in_=uT_sb[:, c, :]), uT_t, [])
        if stop == "A0":
            break
        yt = phase_BC(l, stop)
        P.barrier()
        if debug and l == 0:
            for c in range(8):
                P.op(sp, lambda e, c=c: e.dma_start(out=dbg["yT0"][c * 128:(c + 1) * 128, :], in_=yT[c * 128:(c + 1) * 128, :]), [yt], [])
        if stop in ("B", "BC"):
            break
        phase_D(l, yt, last=(l == n_layers - 1))
    P.emit(nc, st)
    st.close()
    return nc


def make_consts():
    c = np.zeros((128, C_END), np.float32)
    i = np.arange(128)
    c[:, C_ID:C_ID + 128] = np.eye(128)
    c[:, C_TRI:C_TRI + 128] = (i[:, None] >= i[None, :])
    c[:, C_ONE:C_ONE + 128] = 1.0
    c[:, C_BA:C_BA + 128] = 1.0 / 128
    c[:, C_BB:C_BB + 128] = ((i[:, None] // 64) == (i[None, :] // 64)) / 64.0
    c[:, C_LT:C_LT + 128] = (i[:, None] < i[None, :])
    c[:, C_LE:C_LE + 64] = ((i[:, None] % 64) <= np.arange(64)[None, :])
    m = np.ones((128, 512), np.float32)
    m[:, ::64] = 0.0
    c[:, C_MR:C_MR + 512] = m
    return c


_NC_CACHE = {}


def kernel(x, p, norm_mix, w_in, a_out_norm, b_out_norm, w_out, lb_logits,
           ple_gate_norm, w_ple_gate, w_ple_proj, ple_post_norm, final_norm):
    f = lambda a: np.ascontiguousarray(np.asarray(a, dtype=np.float32))
    x, p = f(x), f(p)
    B = x.shape[0]
    rep = lambda v: np.broadcast_to(f(v)[None, :], (128, D))
    vecs = np.ascontiguousarray(np.stack([rep(norm_mix[0]), rep(norm_mix[1]), rep(ple_gate_norm[0]), rep(ple_gate_norm[1]),
                                          rep(ple_post_norm[0]), rep(ple_post_norm[1]), rep(final_norm)], axis=0))
    colv = np.zeros((128, 24), np.float32)
    aon, bon, lbl = f(a_out_norm), f(b_out_norm), f(lb_logits)
    for l in range(DEPTH):
        for h in range(4):
            colv[:, l * 4 + h] = aon[l, h * 128:(h + 1) * 128]
            colv[:, 8 + l * 4 + h] = bon[l, h * 128:(h + 1) * 128]
            colv[:, 16 + l * 4 + h] = lbl[l, h * 128:(h + 1) * 128]
    cst = make_consts()
    if "nc" not in _NC_CACHE:
        _NC_CACHE["nc"] = build_program()
    nc = _NC_CACHE["nc"]
    shared = dict(w_in=f(w_in), w_out=f(w_out), w_pg=f(w_ple_gate), w_pp=f(w_ple_proj), vecs=vecs, colv=colv, cst=cst)
    in_maps = []
    for b in range(B):
        d = dict(shared)
        d["x"] = np.ascontiguousarray(x[b])
        d["p"] = np.ascontiguousarray(p[:, b])
        in_maps.append(d)
    res = run_bass_kernel_spmd(nc, in_maps, core_ids=list(range(B)))
    return np.stack([np.asarray(r["out"], dtype=np.float32) for r in res.results], axis=0)
```

```python
from contextlib import ExitStack
import numpy as np
import concourse.bass as bass
import concourse.mybir as mybir
from concourse.bass_utils import run_bass_kernel_spmd

F32 = mybir.dt.float32
BF16 = mybir.dt.bfloat16
AF = mybir.ActivationFunctionType
ALU = mybir.AluOpType

S = 4096
D = 1024
DEPTH = 2
NT = 8
NB = 32
EPS = 1e-6
DMA_RING = 8

C_ID, C_TRI, C_ONE, C_BA, C_BB, C_LT, C_LE, C_MR, C_END = 0, 128, 256, 384, 512, 640, 768, 832, 1344


class Tile:
    __slots__ = ("ap", "lw", "rd", "name")

    def __init__(self, ap, name=""):
        self.ap = ap
        self.lw = None
        self.rd = {}
        self.name = name


class Eng:
    def __init__(self, name, is_dma=False):
        self.name = name
        self.is_dma = is_dma
        self.instrs = []
        self.pending = set()
        self.gfirst = None


class Instr:
    __slots__ = ("fn", "deps", "signal", "val")

    def __init__(self, fn, deps):
        self.fn = fn
        self.deps = deps
        self.signal = False
        self.val = 0


class Plan:
    def __init__(self):
        self.pe = Eng("pe")
        self.act = Eng("act")
        self.dve = Eng("dve")
        self.pool = Eng("pool")
        self.sp = Eng("sp", is_dma=True)
        self.engs = [self.sp, self.pe, self.act, self.dve, self.pool]

    def op(self, eng, fn, reads=(), writes=()):
        deps = set()
        for t in reads:
            if t.lw is not None:
                deps.add(t.lw)
        for t in writes:
            if t.lw is not None:
                deps.add(t.lw)
            for e, s in t.rd.items():
                if e.is_dma:
                    for ss in s:
                        deps.add((e, ss))
                else:
                    deps.add((e, s))
        if eng is self.pe:
            deps = {d for d in deps if d[0] is not self.pe}
        if eng.pending:
            deps |= eng.pending
            eng.pending = set()
        seq = len(eng.instrs)
        if eng.gfirst is not None:
            if eng.gfirst < 0:
                eng.gfirst = seq
            else:
                eng.instrs[eng.gfirst].deps |= deps
                deps = set()
        eng.instrs.append(Instr(fn, deps))
        for t in reads:
            if eng.is_dma:
                t.rd.setdefault(eng, []).append(seq)
            else:
                t.rd[eng] = seq
        for t in writes:
            t.lw = (eng, seq)
            t.rd = {}
        return seq

    def group(self, eng):
        plan = self

        class _G:
            def __enter__(self_g):
                eng.gfirst = -1

            def __exit__(self_g, *a):
                eng.gfirst = None
        return _G()

    def barrier(self):
        deps = set()
        for e in self.engs:
            n = len(e.instrs)
            if n == 0:
                continue
            if e.is_dma:
                for s in range(max(0, n - DMA_RING), n):
                    deps.add((e, s))
            else:
                deps.add((e, n - 1))
        for e in self.engs:
            e.pending |= {d for d in deps if d[0] is not e or e.is_dma}

    def emit(self, nc, stack):
        for e in self.engs:
            for ins in e.instrs:
                for (de, ds) in ins.deps:
                    if not de.is_dma:
                        de.instrs[ds].signal = True
        for e in self.engs:
            if e.is_dma:
                continue
            c = 0
            for ins in e.instrs:
                if ins.signal:
                    c += 1
                    ins.val = c
        sems = {}
        for e in self.engs:
            if e.is_dma:
                sems[e] = [stack.enter_context(nc.semaphore(f"s_{e.name}{i}")) for i in range(DMA_RING)]
            else:
                sems[e] = stack.enter_context(nc.semaphore(f"s_{e.name}"))
        block = stack.enter_context(nc.Block())

        def resolve(dep):
            de, ds = dep
            if de.is_dma:
                return (sems[de][ds % DMA_RING], 16 * (ds // DMA_RING + 1))
            return (sems[de], de.instrs[ds].val)

        def run(e, beng):
            known = {}
            n = len(e.instrs)
            for seq, ins in enumerate(e.instrs):
                waits = {}
                if e.is_dma and seq >= DMA_RING:
                    s, v = resolve((e, seq - DMA_RING))
                    waits[s] = max(waits.get(s, 0), v)
                for dep in ins.deps:
                    s, v = resolve(dep)
                    if v > waits.get(s, 0):
                        waits[s] = v
                for s, v in waits.items():
                    if known.get(s, 0) >= v:
                        continue
                    beng.wait_ge(s, v)
                    known[s] = v
                bi = ins.fn(beng)
                if e.is_dma:
                    bi.then_inc(sems[e][seq % DMA_RING], 16)
                elif ins.signal:
                    bi.then_inc(sems[e], 1)
            if e.is_dma:
                for seq in range(max(0, n - DMA_RING), n):
                    s, v = resolve((e, seq))
                    if known.get(s, 0) < v:
                        beng.wait_ge(s, v)
                        known[s] = v

        @block.sync
        def _(sync):
            run(self.sp, sync)

        @block.tensor
        def _(tensor):
            run(self.pe, tensor)

        @block.scalar
        def _(scalar):
            run(self.act, scalar)

        @block.vector
        def _(vector):
            run(self.dve, vector)

        @block.gpsimd
        def _(gpsimd):
            run(self.pool, gpsimd)


class RPool:
    def __init__(self, tiles):
        self.tiles = tiles
        self.i = 0

    def next(self):
        t = self.tiles[self.i % len(self.tiles)]
        self.i += 1
        return t


def build_program(debug=False, n_layers=DEPTH, stop=None):
    nc = bass.Bass("TRN2", target_bir_lowering=False)
    dram = lambda name, shape, dt=F32, kind="ExternalInput": nc.dram_tensor(name, shape, dt, kind=kind).ap()
    x = dram("x", [S, D])
    p_in = dram("p", [DEPTH, S, 256])
    w_in = dram("w_in", [DEPTH, D, 4096])
    w_out = dram("w_out", [DEPTH, D, D])
    w_pg = dram("w_pg", [DEPTH, D, D])
    w_pp = dram("w_pp", [DEPTH, 256, D])
    vecs = dram("vecs", [7, 128, D])
    colv = dram("colv", [128, 24])
    cst = dram("cst", [128, C_END])
    out = dram("out", [S, D], F32, "ExternalOutput")
    hres = dram("hres", [S, D], F32, "Internal")
    yT = dram("yT", [D, S], BF16, "Internal")

    P = Plan()
    pe, act, dve, pool, sp = P.pe, P.act, P.dve, P.pool, P.sp
    st = ExitStack()
    sbt = lambda name, shape, dt=F32: st.enter_context(nc.sbuf_tensor(name, shape, dt))

    uT_sb = sbt("uT", [128, 8, S], BF16)
    uT_t = [Tile(uT_sb[:, :, j * 512:(j + 1) * 512], f"uT{j}") for j in range(NT)]
    cst_sb = sbt("cst_sb", [128, C_END])
    cbf_sb = sbt("cbf_sb", [128, C_LE], BF16)
    colv_sb = sbt("colv_sb", [128, 24])
    lbw_sb = sbt("lbw_sb", [128, 32])
    t_cst = Tile(cst_sb[:])
    t_cbf = Tile(cbf_sb[:])
    t_colv = Tile(colv_sb[:])
    t_lbw = Tile(lbw_sb[:])
    OVW = 34800
    ov = sbt("ov", [128, OVW])
    ovp = [0]

    ovmax = [0]

    def ov_reset():
        ovmax[0] = max(ovmax[0], ovp[0])
        ovp[0] = 0

    def ov_alloc(shape, dt=F32, name=""):
        n = int(np.prod(shape[1:]))
        words = n if dt == F32 else (n + 1) // 2
        a = ov[:, ovp[0]:ovp[0] + words]
        ovp[0] += words
        assert ovp[0] <= OVW, ("overlay overflow", ovp[0])
        if dt != F32:
            a = a.bitcast(dt)
        if len(shape) == 3:
            a = a.rearrange("p (a b) -> p a b", a=shape[1])
        return Tile(a, name)

    def ov_pool(n, shape, dt=F32, name=""):
        return RPool([ov_alloc(shape, dt, f"{name}{i}") for i in range(n)])

    psbig = [st.enter_context(nc.psum_tensor(f"ps{i}", [128, 1024], F32)) for i in range(4)]
    bank = [Tile(psbig[i // 2][:, (i % 2) * 512:(i % 2) * 512 + 512], f"bank{i}") for i in range(8)]

    ident_bf = cbf_sb[:, C_ID:C_ID + 128]
    tri_bf = cbf_sb[:, C_TRI:C_TRI + 128]
    ones_bf = cbf_sb[:, C_ONE:C_ONE + 128]
    blkA_bf = cbf_sb[:, C_BA:C_BA + 128]
    blkB_bf = cbf_sb[:, C_BB:C_BB + 128]
    mlt_bf = cbf_sb[:, C_LT:C_LT + 128]
    mle_f = cst_sb[:, C_LE:C_LE + 64]
    mreset_f = cst_sb[:, C_MR:C_MR + 512]

    P.op(sp, lambda e: e.dma_start(out=cst_sb[:], in_=cst), [], [t_cst])
    P.op(sp, lambda e: e.dma_start(out=colv_sb[:], in_=colv), [], [t_colv])
    P.op(dve, lambda e: e.tensor_copy(out=cbf_sb[:], in_=cst_sb[:, 0:C_LE]), [t_cst], [t_cbf])
    L0 = colv_sb[:, 16:20]
    L1 = colv_sb[:, 20:24]
    w = lambda a, b: lbw_sb[:, a:b]
    P.op(dve, lambda e: e.tensor_tensor(out=w(0, 4), in0=L0, in1=L1, op=ALU.max), [t_colv], [t_lbw])
    P.op(dve, lambda e: e.tensor_tensor(out=w(4, 8), in0=L0, in1=w(0, 4), op=ALU.subtract), [t_colv, t_lbw], [t_lbw])
    P.op(dve, lambda e: e.tensor_tensor(out=w(8, 12), in0=L1, in1=w(0, 4), op=ALU.subtract), [t_colv, t_lbw], [t_lbw])
    P.op(act, lambda e: e.activation(out=w(4, 12), in_=w(4, 12), func=AF.Exp), [t_lbw], [t_lbw])
    P.op(dve, lambda e: e.tensor_tensor(out=w(0, 4), in0=w(4, 8), in1=w(8, 12), op=ALU.add), [t_lbw], [t_lbw])
    P.op(dve, lambda e: e.reciprocal(out=w(0, 4), in_=w(0, 4)), [t_lbw], [t_lbw])
    P.op(dve, lambda e: e.tensor_tensor(out=w(4, 8), in0=w(4, 8), in1=w(0, 4), op=ALU.mult), [t_lbw], [t_lbw])
    P.op(dve, lambda e: e.tensor_tensor(out=w(8, 12), in0=w(8, 12), in1=w(0, 4), op=ALU.mult), [t_lbw], [t_lbw])
    P.op(dve, lambda e: e.tensor_tensor(out=w(12, 16), in0=w(4, 8), in1=w(8, 12), op=ALU.add), [t_lbw], [t_lbw])
    P.op(dve, lambda e: e.tensor_tensor(out=w(0, 4), in0=w(4, 8), in1=w(4, 8), op=ALU.subtract), [t_lbw], [t_lbw])
    P.op(dve, lambda e: e.tensor_tensor(out=w(12, 16), in0=w(12, 16), in1=w(4, 8), op=ALU.subtract), [t_lbw], [t_lbw])
    P.op(dve, lambda e: e.tensor_scalar(out=w(16, 20), in0=w(0, 4), scalar1=-1.0, scalar2=1.0, op0=ALU.mult, op1=ALU.add), [t_lbw], [t_lbw])
    P.op(dve, lambda e: e.tensor_scalar(out=w(20, 24), in0=w(12, 16), scalar1=-1.0, scalar2=1.0, op0=ALU.mult, op1=ALU.add), [t_lbw], [t_lbw])

    P.op(act, lambda e: e.activation(out=w(24, 32), in_=w(16, 24), func=AF.Ln), [t_lbw], [t_lbw])

    dbg = {}
    if debug:
        dbg["uT0"] = dram("d_uT0", [128, 8, S], BF16, "ExternalOutput")
        dbg["yT0"] = dram("d_yT0", [D, S], BF16, "ExternalOutput")
        dbg["h1"] = dram("d_h1", [S, D], F32, "ExternalOutput")

    def emit_u1(hb, nm, W, on_dve=False):
        junk = W["junk"].next()
        ssq = W["small"].next()
        if on_dve:
            P.op(dve, lambda e: e.scalar_tensor_tensor(out=junk.ap, in0=hb.ap, scalar=1.0, in1=hb.ap, op0=ALU.mult, op1=ALU.mult, accum_out=ssq.ap[:, 0:1]), [hb], [junk, ssq])
        else:
            P.op(act, lambda e: e.activation(out=junk.ap, in_=hb.ap, func=AF.Square, accum_out=ssq.ap[:, 0:1]), [hb], [junk, ssq])
        P.op(act, lambda e: e.activation(out=ssq.ap[:, 1:2], in_=ssq.ap[:, 0:1], func=AF.Ln, scale=1.0 / D, bias=W["eps"].ap), [ssq, W["eps"]], [ssq])
        P.op(act, lambda e: e.activation(out=ssq.ap[:, 2:3], in_=ssq.ap[:, 1:2], func=AF.Exp, scale=-0.5), [ssq], [ssq])
        ub = W["ub"].next()
        P.op(dve, lambda e: e.scalar_tensor_tensor(out=ub.ap, in0=hb.ap, scalar=ssq.ap[:, 2:3], in1=nm.ap, op0=ALU.mult, op1=ALU.mult), [hb, ssq, nm], [ub])
        return ub

    def emit_u2(ub, tb, W):
        tp = W["tp"].next()
        tpb = tp.ap.bitcast(BF16)
        with P.group(pe):
            for c in range(8):
                P.op(pe, lambda e, c=c: e.transpose(out=tpb[:, c * 128:(c + 1) * 128], in_=ub.ap[:, c * 128:(c + 1) * 128], identity=ident_bf), [ub, t_cbf], [tp])
        ut = uT_t[tb // 4]
        P.op(act, lambda e: e.activation(out=uT_sb[:, :, tb * 128:(tb + 1) * 128], in_=tpb.rearrange("p (c t) -> p c t", c=8), func=AF.Copy), [tp], [ut])

    def emit_u(hb, tb, nm, W, on_dve=False):
        emit_u2(emit_u1(hb, nm, W, on_dve), tb, W)

    def load_w(W, src_ap, rows_c=8):
        ws = W["wst"].next()
        wb = W["wbf"].next()
        P.op(sp, lambda e: e.dma_start(out=ws.ap[:, 0:rows_c, :], in_=src_ap.rearrange("(c p) n -> p c n", p=128)), [], [ws])
        P.op(pool, lambda e: e.tensor_copy(out=wb.ap[:, 0:rows_c, :], in_=ws.ap[:, 0:rows_c, :]), [ws], [wb])
        return wb

    def proj_fm(W, wb, j, bk):
        for c in range(8):
            P.op(pe, lambda e, c=c: e.matmul(bk.ap, lhsT=wb.ap[:, c, :], rhs=uT_sb[:, c, j * 512:(j + 1) * 512], start=(c == 0), stop=(c == 7)), [wb, uT_t[j]], [bk])

    def proj_tok(W, wb, vt, vsb, j):
        bk = W["pbank"].next()
        for q in range(4):
            tb = 4 * j + q
            for c in range(8):
                P.op(pe, lambda e, c=c, q=q, tb=tb: e.matmul(bk.ap[:, q * 128:(q + 1) * 128], lhsT=uT_sb[:, c, tb * 128:(tb + 1) * 128], rhs=wb.ap[:, c, :], start=(c == 0), stop=(c == 7)), [wb, uT_t[j]], [bk])
        P.op(dve, lambda e: e.tensor_copy(out=vsb[:, 4 * j:4 * j + 4, :], in_=bk.ap.rearrange("p (q d) -> p q d", q=4)), [bk], [vt])

    def silu_from_bank(W, bk, outt, lnscale_col=None, sign=-1.0):
        en = outt if lnscale_col is not None else W["f32"].next()
        P.op(act, lambda e: e.activation(out=en.ap, in_=bk.ap, func=AF.Exp, scale=sign), [bk], [en])
        P.op(act, lambda e: e.activation(out=en.ap, in_=en.ap, func=AF.Ln, bias=1.0), [en], [en])
        if lnscale_col is None:
            P.op(act, lambda e: e.activation(out=en.ap, in_=en.ap, func=AF.Exp, scale=-1.0), [en], [en])
            P.op(dve, lambda e: e.tensor_tensor(out=outt.ap, in0=bk.ap, in1=en.ap, op=ALU.mult), [bk, en], [outt])
        else:
            P.op(act, lambda e: e.activation(out=en.ap, in_=en.ap, func=AF.Exp, scale=-1.0, bias=lnscale_col), [en, t_lbw], [en])

    def head_norm_1(W, obank, blk_bf):
        o_sb = W["f32"].next()
        P.op(dve, lambda e: e.tensor_copy(out=o_sb.ap, in_=obank.ap), [obank], [o_sb])
        osq = W["bf"].next()
        P.op(pool, lambda e: e.tensor_tensor(out=osq.ap, in0=o_sb.ap, in1=o_sb.ap, op=ALU.mult), [o_sb], [osq])
        mb = W["mbank"].next()
        P.op(pe, lambda e: e.matmul(mb.ap, lhsT=blk_bf, rhs=osq.ap, start=True, stop=True), [osq, t_cbf], [mb])
        return o_sb, mb

    def head_norm_2(W, o_sb, mb, gcol, sg, row0, j):
        rs = W["f32"].next()
        P.op(act, lambda e: e.activation(out=rs.ap, in_=mb.ap, func=AF.Ln, bias=W["eps"].ap), [mb, W["eps"]], [rs])
        P.op(act, lambda e: e.activation(out=rs.ap, in_=rs.ap, func=AF.Exp, scale=-0.5), [rs], [rs])
        P.op(dve, lambda e: e.scalar_tensor_tensor(out=o_sb.ap, in0=o_sb.ap, scalar=gcol, in1=rs.ap, op0=ALU.mult, op1=ALU.mult), [o_sb, rs, t_colv], [o_sb])
        yb = W["bf"].next()
        P.op(dve, lambda e: e.tensor_tensor(out=yb.ap, in0=o_sb.ap, in1=sg.ap, op=ALU.mult), [o_sb, sg], [yb])
        P.op(sp, lambda e: e.dma_start(out=yT[row0:row0 + 128, j * 512:(j + 1) * 512], in_=yb.ap), [yb], [W["yT_t"]])

    def head_norm_store(W, obank, blk_bf, gcol, sg, row0, j):
        o_sb, mb = head_norm_1(W, obank, blk_bf)
        head_norm_2(W, o_sb, mb, gcol, sg, row0, j)

    def phase_A0():
        ov_reset()
        W = {}
        W["junk"] = ov_pool(1, [128, D], F32, "junk")
        W["small"] = ov_pool(4, [128, 4], F32, "small")
        W["ub"] = ov_pool(2, [128, D], BF16, "ub")
        W["tp"] = RPool([bank[6], bank[7]])
        W["eps"] = ov_alloc([128, 1], F32, "eps")
        P.op(pool, lambda e: e.memset(W["eps"].ap, EPS), [], [W["eps"]])
        nm = ov_alloc([128, D], F32, "nm")
        P.op(sp, lambda e: e.dma_start(out=nm.ap, in_=vecs[0]), [], [nm])
        hbp = ov_pool(3, [128, D], F32, "hb")
        prev = None
        for tb in range(NB):
            hb = hbp.next()
            P.op(sp, lambda e, hb=hb, tb=tb: e.dma_start(out=hb.ap, in_=x[tb * 128:(tb + 1) * 128, :]), [], [hb])
            ub = emit_u1(hb, nm, W, on_dve=(tb % 2 == 1))
            if prev is not None:
                emit_u2(*prev, W)
            prev = (ub, tb)
        emit_u2(*prev, W)

    def phase_BC(l, stop=None):
        ov_reset()
        yT_tile = Tile(None, "yT")

        def common_alloc():
            ov_reset()
            W = {}
            W["eps"] = ov_alloc([128, 1], F32, "eps")
            P.op(pool, lambda e: e.memset(W["eps"].ap, EPS), [], [W["eps"]])
            W["wst"] = ov_pool(3, [128, 8, 128], F32, "wst")
            W["wbf"] = ov_pool(8, [128, 8, 128], BF16, "wbf")
            W["f32"] = ov_pool(6, [128, 512], F32, "f32")
            W["bf"] = ov_pool(4, [128, 512], BF16, "bf")
            W["yT_t"] = yT_tile
            W["pbank"] = RPool([bank[4], bank[5]])
            W["mbank"] = RPool([bank[7]])
            vsb_t = ov_alloc([128, NB, 128], BF16, "vsb")
            vsb = vsb_t.ap
            vts = [Tile(vsb[:, 4 * j:4 * j + 4, :], f"v{j}") for j in range(NT)]
            sgp = ov_pool(2, [128, 512], F32, "sg")
            return W, vsb, vts, sgp

        W, vsb, vts, sgp = common_alloc()

        kT_ts = [ov_alloc([128, S], BF16, "kT0"), ov_alloc([128, S], BF16, "kT1")]
        kts_p = [[Tile(kT_ts[par].ap[:, j * 512:(j + 1) * 512]) for j in range(NT)] for par in range(2)]
        vsb2_t = ov_alloc([128, NB, 128], BF16, "vsb2")
        vsb_p = [vsb, vsb2_t.ap]
        vts_p = [vts, [Tile(vsb2_t.ap[:, 4 * j:4 * j + 4, :], f"v2_{j}") for j in range(NT)]]
        qp = ov_pool(2, [128, 512], BF16, "q")
        ep = ov_pool(3, [128, 2, 512], F32, "e")
        spp = ov_pool(3, [128, 2, 512], BF16, "sp")
        wxp = ov_pool(2, [128, 2, 512], F32, "wx")
        wp = ov_pool(3, [128, 2, 512], BF16, "w")
        Ap = ov_pool(4, [128, 2, 512], BF16, "A")
        zps, rps = psbig[0], psbig[1]
        z3 = zps[:, :].rearrange("p (h c) -> p h c", h=2)
        r3 = rps[:, :].rearrange("p (h c) -> p h c", h=2)
        zb = [bank[0], bank[1]]
        rb = [bank[2], bank[3]]
        ob = bank[6]
        mlt3 = mlt_bf.unsqueeze(1).broadcast_to([128, 2, 128])

        items = []
        wts = {}

        def loadw_b(hp):
            wts[hp] = tuple(load_w(W, w_in[l, :, base + hp * 128:base + hp * 128 + 128]) for base in (2048, 2560, 3072, 3584))

        tiles_l = [(hp, j) for hp in range(4) for j in range(NT)]
        for idx, (hp, j) in enumerate(tiles_l):
            if idx == 0:
                items.append(("loadw", 0))
                items.append(("prep", hp, j))
            n_it = 4 * j + 4
            for m, kb in enumerate(range(4 * j + 3, -1, -1)):
                items.append(("att", hp, j, kb))
                if j == 4 and m == 0 and hp < 3:
                    items.append(("loadw", hp + 1))
                if m == n_it // 2 - 1 and idx + 1 < len(tiles_l):
                    items.append(("prep",) + tiles_l[idx + 1])

        state = {}
        deferred = []

        def prep(hp, j):
            wq, wk, wv, wg = wts[hp]
            kts = kts_p[hp % 2]
            vt, vs = vts_p[hp % 2][j], vsb_p[hp % 2]
            bk = W["pbank"].next()
            for q in range(4):
                tb = 4 * j + q
                with P.group(pe):
                    for c in range(8):
                        P.op(pe, lambda e, c=c, q=q, tb=tb, bk=bk: e.matmul(bk.ap[:, q * 128:(q + 1) * 128], lhsT=uT_sb[:, c, tb * 128:(tb + 1) * 128], rhs=wv.ap[:, c, :], start=(c == 0), stop=(c == 7)), [wv, uT_t[j]], [bk])
                yield
            bk1 = W["pbank"].next()
            with P.group(pe):
                proj_fm(W, wq, j, bk1)
            P.op(dve, lambda e, bk=bk: e.tensor_copy(out=vs[:, 4 * j:4 * j + 4, :], in_=bk.ap.rearrange("p (q d) -> p q d", q=4)), [bk], [vt])
            yield
            bk2 = W["pbank"].next()
            with P.group(pe):
                proj_fm(W, wk, j, bk2)
            qt = qp.next()
            P.op(dve, lambda e: e.tensor_copy(out=qt.ap, in_=bk1.ap), [bk1], [qt])
            yield
            bk3 = W["pbank"].next()
            with P.group(pe):
                proj_fm(W, wg, j, bk3)
            P.op(dve, lambda e: e.tensor_copy(out=kts[j].ap, in_=bk2.ap), [bk2], [kts[j]])
            yield
            sg = sgp.next()
            silu_from_bank(W, bk3, sg)
            state[(hp, j)] = dict(q=qt, sg=sg)

        def s1(it):
            _, hp, j, kb = it
            qt = state[(hp, j)]["q"]
            if kb == 4 * j + 3:
                a0, a1 = Ap.next(), Ap.next()
                P.op(pool, lambda e: e.memset(a0.ap, 0.0), [], [a0])
                P.op(pool, lambda e: e.memset(a1.ap, 0.0), [], [a1])
                state["A"] = [a0, a1]
            c0 = max(0, kb - 4 * j) * 128
            kj = kb // 4
            kT = kT_ts[hp % 2].ap
            kts = kts_p[hp % 2]
            with P.group(pe):
                for hh in range(2):
                    r = slice(hh * 64, hh * 64 + 64)
                    P.op(pe, lambda e, hh=hh, r=r: e.matmul(zps[:, hh * 512 + c0:hh * 512 + 512], lhsT=kT[r, kb * 128:(kb + 1) * 128], rhs=qt.ap[r, c0:512], start=True, stop=True), [kts[kj], qt], [zb[hh]])
            et = ep.next()
            P.op(act, lambda e: e.activation(out=et.ap[:, :, c0:512], in_=z3[:, :, c0:512], func=AF.Exp, scale=0.125), zb, [et])
            spt = spp.next()
            P.op(act, lambda e: e.activation(out=spt.ap[:, :, c0:512], in_=et.ap[:, :, c0:512], func=AF.Ln, bias=1.0), [et], [spt])
            if kb >= 4 * j:
                P.op(pool, lambda e: e.tensor_tensor(out=spt.ap[:, :, c0:c0 + 128], in0=spt.ap[:, :, c0:c0 + 128], in1=mlt3, op=ALU.mult), [spt, t_cbf], [spt])
            return dict(c0=c0, sp=spt, e=et, qt=qt, A=state["A"])

        def s2(it, ctx):
            _, hp, j, kb = it
            c0, spt, et = ctx["c0"], ctx["sp"], ctx["e"]
            n = (4 * j + 3) - kb
            acur, anxt = ctx["A"][n % 2], ctx["A"][(n + 1) % 2]
            with P.group(pe):
                for hh in range(2):
                    P.op(pe, lambda e, hh=hh: e.matmul(rps[:, hh * 512 + c0:hh * 512 + 512], lhsT=tri_bf, rhs=spt.ap[:, hh, c0:512], start=True, stop=(n == 0)), [spt, t_cbf], [rb[hh]])
                    if n > 0:
                        P.op(pe, lambda e, hh=hh: e.matmul(rps[:, hh * 512 + c0:hh * 512 + 512], lhsT=ones_bf, rhs=acur.ap[:, hh, c0:512], start=False, stop=True), [acur, t_cbf], [rb[hh]])
            if kb > 0:
                P.op(dve, lambda e: e.tensor_tensor(out=anxt.ap[:, :, c0:512], in0=acur.ap[:, :, c0:512], in1=spt.ap[:, :, c0:512], op=ALU.add), [acur, spt], [anxt])
            wx = wxp.next()
            P.op(act, lambda e: e.activation(out=wx.ap[:, :, c0:512], in_=r3[:, :, c0:512], func=AF.Exp, scale=-1.0), rb, [wx])
            wt = wp.next()
            P.op(dve, lambda e: e.tensor_tensor(out=wt.ap[:, :, c0:512], in0=et.ap[:, :, c0:512], in1=wx.ap[:, :, c0:512], op=ALU.mult), [et, wx], [wt])
            if kb >= 4 * j:
                P.op(pool, lambda e: e.tensor_tensor(out=wt.ap[:, :, c0:c0 + 128], in0=wt.ap[:, :, c0:c0 + 128], in1=mlt3, op=ALU.mult), [wt, t_cbf], [wt])
            ctx["w"] = wt

        def s3(it, ctx):
            _, hp, j, kb = it
            c0, wt = ctx["c0"], ctx["w"]
            with P.group(pe):
                for hh in range(2):
                    P.op(pe, lambda e, hh=hh: e.matmul(ob.ap[hh * 64:hh * 64 + 64, c0:512], lhsT=vsb_p[hp % 2][:, kb, hh * 64:hh * 64 + 64], rhs=wt.ap[:, hh, c0:512], start=(kb == 4 * j + 3), stop=(kb == 0), skip_group_check=True), [vts_p[hp % 2][kb // 4], wt], [ob])
            if kb == 0:
                o_sb, mb = head_norm_1(W, ob, blkB_bf)
                gcol = colv_sb[:, 8 + l * 4 + hp:8 + l * 4 + hp + 1]
                sgt = state[(hp, j)]["sg"]
                deferred.append([2, lambda: head_norm_2(W, o_sb, mb, gcol, sgt, 512 + hp * 128, j)])

        pend1 = None
        pend2 = None

        def flush():
            nonlocal pend1, pend2
            if pend2 is not None:
                s3(*pend2)
                pend2 = None
            if pend1 is not None:
                s2(*pend1)
                s3(*pend1)
                pend1 = None

        gen = [None, 0]

        def run_deferred(force=False):
            for d in list(deferred):
                d[0] -= 1
                if d[0] <= 0 or force:
                    deferred.remove(d)
                    d[1]()

        def advance(k):
            for _ in range(k):
                if gen[0] is None:
                    return
                try:
                    next(gen[0])
                except StopIteration:
                    gen[0] = None

        for ii, it in enumerate(items):
            if it[0] == "loadw":
                loadw_b(it[1])
            elif it[0] == "prep":
                advance(100)
                gen[0] = prep(*it[1:])
                n_left = 0
                for it2 in items[ii + 1:]:
                    if it2[0] == "att":
                        if (it2[1], it2[2]) == (it[1], it[2]):
                            break
                        n_left += 1
                gen[1] = 100 if n_left == 0 else -(-8 // n_left)
                if n_left == 0:
                    advance(100)
            else:
                if (it[1], it[2]) not in state:
                    advance(100)
                ctx = s1(it)
                if pend1 is not None:
                    s2(*pend1)
                if pend2 is not None:
                    s3(*pend2)
                pend2 = pend1
                pend1 = (it, ctx)
                advance(gen[1])
                run_deferred()
        flush()
        run_deferred(force=True)

        if stop == "B":
            return W["yT_t"]
        P.barrier()
        ov_reset()
        Wc = {}
        Wc["eps"] = ov_alloc([128, 1], F32, "eps")
        P.op(pool, lambda e: e.memset(Wc["eps"].ap, EPS), [], [Wc["eps"]])
        Wc["wst"] = ov_pool(3, [128, 8, 128], F32, "wst")
        Wc["wbf"] = ov_pool(8, [128, 8, 128], BF16, "wbf")
        Wc["yT_t"] = yT_tile

        def a_stream(si, heads):
            Ws = dict(Wc)
            bs = bank[4 * si:4 * si + 4]
            Ws["pbank"] = RPool([bs[0]])
            Ws["mbank"] = RPool([bs[0]])
            Ws["f32"] = ov_pool(4, [128, 512], F32, f"f32_{si}")
            Ws["bf"] = ov_pool(3, [128, 512], BF16, f"bf_{si}")
            mixb, oab, stbk = bs[1], bs[2], bs[3]
            vsb_t = ov_alloc([128, NB, 128], BF16, f"vsb_{si}")
            vsb = vsb_t.ap
            vts = [Tile(vsb[:, 4 * j:4 * j + 4, :], f"v{si}_{j}") for j in range(NT)]
            sgp = ov_pool(2, [128, 512], F32, f"sg_{si}")
            f32b = ov_pool(9, [128, 512], F32, f"fa_{si}")
            bfb = ov_pool(8, [128, 512], BF16, f"ba_{si}")
            khp = ov_pool(2, [128, 4, 128], BF16, f"kht_{si}")
            atp = ov_pool(2, [128, 4, 64], BF16, f"at_{si}")
            gdp = ov_pool(2, [128, 8], F32, f"gd_{si}")
            s32p = ov_pool(2, [128, 128], F32, f"s32_{si}")
            sbfp = ov_pool(2, [128, 128], BF16, f"sbf_{si}")
            atv = mixb.ap[:, 0:256]
            tpv = stbk.ap[:, 256:512].bitcast(BF16)
            for ha in heads:
                wq = load_w(Ws, w_in[l, :, ha * 128:ha * 128 + 128])
                wf = load_w(Ws, w_in[l, :, 512 + ha * 128:512 + ha * 128 + 128])
                wi = load_w(Ws, w_in[l, :, 1024 + ha * 128:1024 + ha * 128 + 128])
                wg = load_w(Ws, w_in[l, :, 1536 + ha * 128:1536 + ha * 128 + 128])
                omlc = lbw_sb[:, 24 + l * 4 + ha:24 + l * 4 + ha + 1]
                s32 = s32p.next()
                sbf = sbfp.next()
                P.op(pool, lambda e, s32=s32: e.memset(s32.ap, 0.0), [], [s32])
                P.op(pool, lambda e, sbf=sbf: e.memset(sbf.ap, 0.0), [], [sbf])
                yield
                for j in range(NT):
                    with P.group(pe):
                        bk = Ws["pbank"].next()
                        proj_fm(Ws, wq, j, bk)
                    qa = f32b.next()
                    silu_from_bank(Ws, bk, qa)
                    yield
                    with P.group(pe):
                        bk = Ws["pbank"].next()
                        proj_fm(Ws, wf, j, bk)
                    ka = f32b.next()
                    silu_from_bank(Ws, bk, ka, lnscale_col=omlc, sign=1.0)
                    lf = f32b.next()
                    P.op(act, lambda e, lf=lf, ka=ka: e.activation(out=lf.ap, in_=ka.ap, func=AF.Ln, scale=-1.0, bias=1.0), [ka], [lf])
                    yield
                    yield
                    with P.group(pe):
                        bk = Ws["pbank"].next()
                        proj_fm(Ws, wg, j, bk)
                    sg = sgp.next()
                    silu_from_bank(Ws, bk, sg)
                    yield
                    bk = Ws["pbank"].next()
                    for q in range(4):
                        tb = 4 * j + q
                        with P.group(pe):
                            for c in range(8):
                                P.op(pe, lambda e, c=c, q=q, tb=tb, bk=bk, wi=wi: e.matmul(bk.ap[:, q * 128:(q + 1) * 128], lhsT=uT_sb[:, c, tb * 128:(tb + 1) * 128], rhs=wi.ap[:, c, :], start=(c == 0), stop=(c == 7)), [wi, uT_t[j]], [bk])
                    P.op(dve, lambda e, bk=bk, j=j: e.tensor_copy(out=vsb[:, 4 * j:4 * j + 4, :], in_=bk.ap.rearrange("p (q d) -> p q d", q=4)), [bk], [vts[j]])
                    yield
                    bt = f32b.next()
                    P.op(dve, lambda e, bt=bt, lf=lf: e.tensor_tensor_scan(out=bt.ap, data0=mreset_f, data1=lf.ap, initial=0.0, op0=ALU.mult, op1=ALU.add), [lf, t_cst], [bt])
                    yield
                    yield
                    b3 = bt.ap.rearrange("p (c s) -> p c s", s=64)
                    d1 = f32b.next()
                    d13 = d1.ap.rearrange("p (c s) -> p c s", s=64)
                    P.op(dve, lambda e, d13=d13, b3=b3: e.tensor_tensor(out=d13, in0=b3, in1=b3[:, :, 31:32].broadcast_to([128, 8, 64]), op=ALU.subtract), [bt], [d1])
                    yield
                    d2 = f32b.next()
                    d23 = d2.ap.rearrange("p (c s) -> p c s", s=64)
                    P.op(dve, lambda e, d23=d23, b3=b3: e.tensor_tensor(out=d23, in0=b3[:, :, 63:64].broadcast_to([128, 8, 64]), in1=b3, op=ALU.subtract), [bt], [d2])
                    yield
                    yield
                    ex = f32b.next()
                    ex2 = f32b.next()
                    ex3 = f32b.next()
                    P.op(act, lambda e, ex=ex, d1=d1: e.activation(out=ex.ap, in_=d1.ap, func=AF.Exp), [d1], [ex])
                    yield
                    P.op(act, lambda e, ex2=ex2, d1=d1: e.activation(out=ex2.ap, in_=d1.ap, func=AF.Exp, scale=-1.0), [d1], [ex2])
                    yield
                    P.op(act, lambda e, ex3=ex3, bt=bt: e.activation(out=ex3.ap, in_=bt.ap, func=AF.Exp), [bt], [ex3])
                    yield
                    P.op(act, lambda e, d2=d2: e.activation(out=d2.ap, in_=d2.ap, func=AF.Exp), [d2], [d2])
                    yield
                    gd = gdp.next()
                    P.op(act, lambda e, gd=gd, bt=bt: e.activation(out=gd.ap, in_=bt.ap[:, 63:512:64], func=AF.Exp), [bt], [gd])
                    yield
                    yield
                    qin, kin, qdec, khat = bfb.next(), bfb.next(), bfb.next(), bfb.next()
                    P.op(dve, lambda e, qin=qin, qa=qa, ex=ex: e.tensor_tensor(out=qin.ap, in0=qa.ap, in1=ex.ap, op=ALU.mult), [qa, ex], [qin])
                    yield
                    P.op(dve, lambda e, kin=kin, ka=ka, ex2=ex2: e.tensor_tensor(out=kin.ap, in0=ka.ap, in1=ex2.ap, op=ALU.mult), [ka, ex2], [kin])
                    yield
                    P.op(dve, lambda e, khat=khat, ka=ka, d2=d2: e.tensor_tensor(out=khat.ap, in0=ka.ap, in1=d2.ap, op=ALU.mult), [ka, d2], [khat])
                    yield
                    P.op(dve, lambda e, qdec=qdec, qa=qa, ex3=ex3: e.tensor_tensor(out=qdec.ap, in0=qa.ap, in1=ex3.ap, op=ALU.mult), [qa, ex3], [qdec])
                    yield
                    yield
                    with P.group(pe):
                        for q in range(4):
                            P.op(pe, lambda e, q=q, khat=khat: e.transpose(out=tpv[:, q * 128:(q + 1) * 128], in_=khat.ap[:, q * 128:(q + 1) * 128], identity=ident_bf), [khat, t_cbf], [stbk])
                    kht = khp.next()
                    P.op(dve, lambda e, kht=kht: e.tensor_copy(out=kht.ap, in_=tpv.rearrange("p (q k) -> p q k", q=4)), [stbk], [kht])
                    yield
                    with P.group(pe):
                        for c in range(8):
                            hf = (c % 2) * 64
                            P.op(pe, lambda e, c=c, hf=hf, kin=kin, qin=qin: e.matmul(mixb.ap[hf:hf + 64, (c // 2) * 64:(c // 2) * 64 + 64], lhsT=kin.ap[:, c * 64:(c + 1) * 64], rhs=qin.ap[:, c * 64:(c + 1) * 64], start=True, stop=True), [kin, qin], [mixb])
                    at = atp.next()
                    P.op(dve, lambda e, at=at: e.tensor_tensor(out=at.ap, in0=atv.rearrange("p (q t) -> p q t", q=4), in1=mle_f.unsqueeze(1).broadcast_to([128, 4, 64]), op=ALU.mult), [mixb, t_cst], [at])
                    yield
                    yield
                    mode[si] = "chunk"
                    for c in range(8):
                        tb = 4 * j + c // 2
                        rows = slice((c % 2) * 64, (c % 2) * 64 + 64)
                        with P.group(pe):
                            P.op(pe, lambda e, c=c, sbf=sbf, qdec=qdec: e.matmul(oab.ap[:, c * 64:(c + 1) * 64], lhsT=sbf.ap, rhs=qdec.ap[:, c * 64:(c + 1) * 64], start=True, stop=False), [sbf, qdec], [oab])
                            P.op(pe, lambda e, c=c, rows=rows, tb=tb, at=at: e.matmul(oab.ap[:, c * 64:(c + 1) * 64], lhsT=vsb[rows, tb, :], rhs=at.ap[rows, c // 2, :], start=False, stop=True), [vts[j], at], [oab])
                            P.op(pe, lambda e, c=c, rows=rows, tb=tb, kht=kht: e.matmul(stbk.ap[:, (c % 2) * 128:(c % 2) * 128 + 128], lhsT=kht.ap[rows, c // 2, :], rhs=vsb[rows, tb, :], start=True, stop=True), [kht, vts[j]], [stbk])
                        s32n = s32p.next()
                        sbfn = sbfp.next()
                        sv = stbk.ap[:, (c % 2) * 128:(c % 2) * 128 + 128]
                        P.op(dve, lambda e, sbfn=sbfn, s32=s32, gd=gd, c=c, sv=sv: e.scalar_tensor_tensor(out=sbfn.ap, in0=s32.ap, scalar=gd.ap[:, c:c + 1], in1=sv, op0=ALU.mult, op1=ALU.add), [s32, gd, stbk], [sbfn])
                        P.op(dve, lambda e, s32n=s32n, s32=s32, gd=gd, c=c, sv=sv: e.scalar_tensor_tensor(out=s32n.ap, in0=s32.ap, scalar=gd.ap[:, c:c + 1], in1=sv, op0=ALU.mult, op1=ALU.add), [s32, gd, stbk], [s32n])
                        s32, sbf = s32n, sbfn
                        yield
                    mode[si] = "op"
                    head_norm_store(Ws, oab, blkA_bf, colv_sb[:, l * 4 + ha:l * 4 + ha + 1], sg, ha * 128, j)
                    yield

        mode = {0: "op", 1: "op"}
        gens = {0: a_stream(0, [0, 1]), 1: a_stream(1, [2, 3])}

        def adv(si, k):
            for _ in range(k):
                if si not in gens:
                    return
                try:
                    next(gens[si])
                except StopIteration:
                    del gens[si]

        adv(0, 14)
        while gens:
            for si in (0, 1):
                if si not in gens:
                    continue
                other = 1 - si
                if mode[si] == "chunk":
                    adv(si, 1)
                elif other in gens and mode[other] == "chunk":
                    adv(si, 3)
                else:
                    adv(si, 1)
        W = Wc
        return W["yT_t"]

    def phase_D(l, yT_tile, last):
        ov_reset()
        W = {}
        W["eps"] = ov_alloc([128, 1], F32, "eps")
        P.op(pool, lambda e: e.memset(W["eps"].ap, EPS), [], [W["eps"]])
        W["wst"] = ov_pool(2, [128, 8, 128], F32, "wst")
        W["junk"] = ov_pool(1, [128, D], BF16, "junk")
        W["small"] = ov_pool(12, [128, 4], F32, "small")
        W["ub"] = ov_pool(2, [128, D], BF16, "ub")
        W["tp"] = RPool([bank[6], bank[7]])
        wo = ov_alloc([128, 8, D], BF16, "wo")
        wg = ov_alloc([128, 8, D], BF16, "wg")
        wpj = ov_alloc([128, 2, D], BF16, "wpj")
        for g in range(8):
            for (dst, src, rc) in ((wo, w_out, 8), (wg, w_pg, 8), (wpj, w_pp, 2)):
                ws = W["wst"].next()
                P.op(sp, lambda e, ws=ws, src=src, g=g, rc=rc: e.dma_start(out=ws.ap[:, 0:rc, :], in_=src[l, :, g * 128:(g + 1) * 128].rearrange("(c p) n -> p c n", p=128)), [], [ws])
                P.op(pool, lambda e, ws=ws, dst=dst, g=g, rc=rc: e.tensor_copy(out=dst.ap[:, :, g * 128:(g + 1) * 128], in_=ws.ap[:, 0:rc, :]), [ws], [dst])
        gn = ov_alloc([128, D], F32, "gn")
        pn = ov_alloc([128, D], F32, "pn")
        nmn = ov_alloc([128, D], F32, "nmn")
        P.op(sp, lambda e: e.dma_start(out=gn.ap, in_=vecs[2 + l]), [], [gn])
        P.op(sp, lambda e: e.dma_start(out=pn.ap, in_=vecs[4 + l]), [], [pn])
        P.op(sp, lambda e: e.dma_start(out=nmn.ap, in_=vecs[6] if last else vecs[l + 1]), [], [nmn])
        hbp = ov_pool(2, [128, D], F32, "hb")
        hmp = ov_pool(3, [128, D], F32, "hm")
        gtp = ov_pool(1, [128, D], F32, "gt")
        pep = ov_pool(3, [128, D], F32, "pe")
        otp = ov_pool(1 if last else 0, [128, D], F32, "ot")
        hnp = ov_pool(2, [128, D], F32, "hn")
        ybp = ov_pool(2, [128, 8, 512], BF16, "yb")
        pbp = ov_pool(2, [128, 256], F32, "pb")
        pbfp = ov_pool(2, [128, 256], BF16, "pbf")
        ptp = ov_pool(2, [128, 2, 128], BF16, "pT")
        ugp = ov_pool(2, [128, 8, 128], BF16, "ugT")
        h_src = x if l == 0 else hres
        hres_t = Tile(None, "hres")
        mixP = (psbig[0], bank[0], bank[1])
        gP = (psbig[1], bank[2], bank[3])
        pP = (psbig[2], bank[4], bank[5])
        ybs = {}

        def rstd_chain(src_ap, src_tiles, on_dve=False):
            junk = W["junk"].next()
            sm = W["small"].next()
            if on_dve:
                P.op(dve, lambda e: e.scalar_tensor_tensor(out=junk.ap, in0=src_ap, scalar=1.0, in1=src_ap, op0=ALU.mult, op1=ALU.mult, accum_out=sm.ap[:, 0:1]), src_tiles, [junk, sm])
            else:
                P.op(act, lambda e: e.activation(out=junk.ap, in_=src_ap, func=AF.Square, accum_out=sm.ap[:, 0:1]), src_tiles, [junk, sm])
            P.op(act, lambda e: e.activation(out=sm.ap[:, 1:2], in_=sm.ap[:, 0:1], func=AF.Ln, scale=1.0 / D, bias=W["eps"].ap), [sm, W["eps"]], [sm])
            P.op(act, lambda e: e.activation(out=sm.ap[:, 2:3], in_=sm.ap[:, 1:2], func=AF.Exp, scale=-0.5), [sm], [sm])
            return sm

        ctxs = {}

        def st_a(tb):
            j, q = tb // 4, tb % 4
            if q == 0:
                yb = ybp.next()
                P.op(sp, lambda e: e.dma_start(out=yb.ap, in_=yT[:, j * 512:(j + 1) * 512].rearrange("(c p) t -> p c t", p=128)), [yT_tile], [yb])
                ybs[j] = yb
            yb = ybs[j]
            hb = hbp.next()
            P.op(sp, lambda e: e.dma_start(out=hb.ap, in_=h_src[tb * 128:(tb + 1) * 128, :]), [hres_t] if l > 0 else [], [hb])
            pb = pbp.next()
            P.op(sp, lambda e: e.dma_start(out=pb.ap, in_=p_in[l, tb * 128:(tb + 1) * 128, :]), [], [pb])
            with P.group(pe):
                for half in range(2):
                    bk = mixP[1 + half]
                    for c in range(8):
                        P.op(pe, lambda e, c=c, half=half, bk=bk: e.matmul(bk.ap, lhsT=yb.ap[:, c, q * 128:(q + 1) * 128], rhs=wo.ap[:, c, half * 512:(half + 1) * 512], start=(c == 0), stop=(c == 7)), [yb, wo], [bk])
            hm = hmp.next()
            P.op(dve, lambda e: e.tensor_tensor(out=hm.ap, in0=mixP[0][:, :], in1=hb.ap, op=ALU.add), [mixP[1], mixP[2], hb], [hm])
            pbf = pbfp.next()
            P.op(pool, lambda e: e.tensor_copy(out=pbf.ap, in_=pb.ap), [pb], [pbf])
            tp2 = W["tp"].next()
            tpv2 = tp2.ap.bitcast(BF16)
            with P.group(pe):
                for c in range(2):
                    P.op(pe, lambda e, c=c: e.transpose(out=tpv2[:, c * 128:(c + 1) * 128], in_=pbf.ap[:, c * 128:(c + 1) * 128], identity=ident_bf), [pbf, t_cbf], [tp2])
            pT = ptp.next()
            P.op(act, lambda e: e.activation(out=pT.ap, in_=tpv2[:, 0:256].rearrange("p (c t) -> p c t", c=2), func=AF.Copy), [tp2], [pT])
            ctxs[tb] = dict(hm=hm, pT=pT)

        def st_b1(tb):
            cx = ctxs[tb]
            hm = cx["hm"]
            sm = rstd_chain(hm.ap, [hm], on_dve=True)
            ug = W["ub"].next()
            P.op(dve, lambda e: e.scalar_tensor_tensor(out=ug.ap, in0=hm.ap, scalar=sm.ap[:, 2:3], in1=gn.ap, op0=ALU.mult, op1=ALU.mult), [hm, sm, gn], [ug])
            cx["ug"] = ug

        def st_b2(tb):
            cx = ctxs[tb]
            pT, ug = cx["pT"], cx["ug"]
            with P.group(pe):
                for half in range(2):
                    bk = pP[1 + half]
                    for c in range(2):
                        P.op(pe, lambda e, c=c, half=half, bk=bk: e.matmul(bk.ap, lhsT=pT.ap[:, c, :], rhs=wpj.ap[:, c, half * 512:(half + 1) * 512], start=(c == 0), stop=(c == 1)), [pT, wpj], [bk])
            sm2 = rstd_chain(pP[0][:, :], [pP[1], pP[2]])
            pet = pep.next()
            P.op(dve, lambda e: e.scalar_tensor_tensor(out=pet.ap, in0=pP[0][:, :], scalar=sm2.ap[:, 2:3], in1=pn.ap, op0=ALU.mult, op1=ALU.mult), [pP[1], pP[2], sm2, pn], [pet])
            tp = W["tp"].next()
            tpv = tp.ap.bitcast(BF16)
            with P.group(pe):
                for c in range(8):
                    P.op(pe, lambda e, c=c: e.transpose(out=tpv[:, c * 128:(c + 1) * 128], in_=ug.ap[:, c * 128:(c + 1) * 128], identity=ident_bf), [ug, t_cbf], [tp])
            ugT = ugp.next()
            P.op(act, lambda e: e.activation(out=ugT.ap, in_=tpv.rearrange("p (c t) -> p c t", c=8), func=AF.Copy), [tp], [ugT])
            cx["pet"] = pet
            cx["ugT"] = ugT

        def st_c(tb):
            cx = ctxs[tb]
            hm, pet, ugT = cx["hm"], cx["pet"], cx["ugT"]
            with P.group(pe):
                for half in range(2):
                    bk = gP[1 + half]
                    for c in range(8):
                        P.op(pe, lambda e, c=c, half=half, bk=bk: e.matmul(bk.ap, lhsT=ugT.ap[:, c, :], rhs=wg.ap[:, c, half * 512:(half + 1) * 512], start=(c == 0), stop=(c == 7)), [ugT, wg], [bk])
            gt = gtp.next()
            P.op(act, lambda e: e.activation(out=gt.ap, in_=gP[0][:, :], func=AF.Exp, scale=-1.0), [gP[1], gP[2]], [gt])
            P.op(act, lambda e: e.activation(out=gt.ap, in_=gt.ap, func=AF.Ln, bias=1.0), [gt], [gt])
            P.op(act, lambda e: e.activation(out=gt.ap, in_=gt.ap, func=AF.Exp, scale=-1.0), [gt], [gt])
            P.op(pool, lambda e: e.tensor_tensor(out=gt.ap, in0=gt.ap, in1=pet.ap, op=ALU.mult), [pet, gt], [gt])
            hn = hnp.next()
            P.op(pool, lambda e: e.tensor_tensor(out=hn.ap, in0=gt.ap, in1=hm.ap, op=ALU.add), [gt, hm], [hn])
            if not last:
                P.op(sp, lambda e: e.dma_start(out=hres[tb * 128:(tb + 1) * 128, :], in_=hn.ap), [hn], [hres_t])
                if debug and l == 0:
                    P.op(sp, lambda e: e.dma_start(out=dbg["h1"][tb * 128:(tb + 1) * 128, :], in_=hn.ap), [hn], [])
            cx["hn"] = hn

        def st_d1(tb):
            cx = ctxs[tb]
            hn = cx["hn"]
            if not last:
                cx["ub"] = emit_u1(hn, nmn, W, on_dve=True)
            else:
                sm3 = rstd_chain(hn.ap, [hn], on_dve=True)
                ot = otp.next()
                P.op(dve, lambda e: e.scalar_tensor_tensor(out=ot.ap, in0=hn.ap, scalar=sm3.ap[:, 2:3], in1=nmn.ap, op0=ALU.mult, op1=ALU.mult), [hn, sm3, nmn], [ot])
                P.op(sp, lambda e: e.dma_start(out=out[tb * 128:(tb + 1) * 128, :], in_=ot.ap), [ot], [])

        def st_d2(tb):
            if not last:
                emit_u2(ctxs[tb]["ub"], tb, W)
            del ctxs[tb]

        ok = lambda t: 0 <= t < NB
        for i in range(NB + 3):
            if ok(i - 1):
                st_b1(i - 1)
            if ok(i - 3):
                st_d1(i - 3)
            if ok(i):
                st_a(i)
            if ok(i - 2):
                st_c(i - 2)
            if ok(i - 1):
                st_b2(i - 1)
            if ok(i - 3):
                st_d2(i - 3)

    phase_A0()
    for l in range(n_layers):
        P.barrier()
        if debug and l == 0:
            for c in range(8):
                P.op(sp, lambda e, c=c: e.dma_start(out=dbg["uT0"][:, c, :], in_=uT_sb[:, c, :]), uT_t, [])
        if stop == "A0":
            break
        yt = phase_BC(l, stop)
        P.barrier()
        if debug and l == 0:
            for c in range(8):
                P.op(sp, lambda e, c=c: e.dma_start(out=dbg["yT0"][c * 128:(c + 1) * 128, :], in_=yT[c * 128:(c + 1) * 128, :]), [yt], [])
        if stop in ("B", "BC"):
            break
        phase_D(l, yt, last=(l == n_layers - 1))
    P.emit(nc, st)
    st.close()
    return nc


def make_consts():
    c = np.zeros((128, C_END), np.float32)
    i = np.arange(128)
    c[:, C_ID:C_ID + 128] = np.eye(128)
    c[:, C_TRI:C_TRI + 128] = (i[:, None] >= i[None, :])
    c[:, C_ONE:C_ONE + 128] = 1.0
    c[:, C_BA:C_BA + 128] = 1.0 / 128
    c[:, C_BB:C_BB + 128] = ((i[:, None] // 64) == (i[None, :] // 64)) / 64.0
    c[:, C_LT:C_LT + 128] = (i[:, None] < i[None, :])
    c[:, C_LE:C_LE + 64] = ((i[:, None] % 64) <= np.arange(64)[None, :])
    m = np.ones((128, 512), np.float32)
    m[:, ::64] = 0.0
    c[:, C_MR:C_MR + 512] = m
    return c


_NC_CACHE = {}


def kernel(x, p, norm_mix, w_in, a_out_norm, b_out_norm, w_out, lb_logits,
           ple_gate_norm, w_ple_gate, w_ple_proj, ple_post_norm, final_norm):
    f = lambda a: np.ascontiguousarray(np.asarray(a, dtype=np.float32))
    x, p = f(x), f(p)
    B = x.shape[0]
    rep = lambda v: np.broadcast_to(f(v)[None, :], (128, D))
    vecs = np.ascontiguousarray(np.stack([rep(norm_mix[0]), rep(norm_mix[1]), rep(ple_gate_norm[0]), rep(ple_gate_norm[1]),
                                          rep(ple_post_norm[0]), rep(ple_post_norm[1]), rep(final_norm)], axis=0))
    colv = np.zeros((128, 24), np.float32)
    aon, bon, lbl = f(a_out_norm), f(b_out_norm), f(lb_logits)
    for l in range(DEPTH):
        for h in range(4):
            colv[:, l * 4 + h] = aon[l, h * 128:(h + 1) * 128]
            colv[:, 8 + l * 4 + h] = bon[l, h * 128:(h + 1) * 128]
            colv[:, 16 + l * 4 + h] = lbl[l, h * 128:(h + 1) * 128]
    cst = make_consts()
    if "nc" not in _NC_CACHE:
        _NC_CACHE["nc"] = build_program()
    nc = _NC_CACHE["nc"]
    shared = dict(w_in=f(w_in), w_out=f(w_out), w_pg=f(w_ple_gate), w_pp=f(w_ple_proj), vecs=vecs, colv=colv, cst=cst)
    in_maps = []
    for b in range(B):
        d = dict(shared)
        d["x"] = np.ascontiguousarray(x[b])
        d["p"] = np.ascontiguousarray(p[:, b])
        in_maps.append(d)
    res = run_bass_kernel_spmd(nc, in_maps, core_ids=list(range(B)))
    return np.stack([np.asarray(r["out"], dtype=np.float32) for r in res.results], axis=0)
```

```python
from contextlib import ExitStack
import numpy as np
import concourse.bass as bass
import concourse.mybir as mybir
from concourse.bass_utils import run_bass_kernel_spmd

F32 = mybir.dt.float32
BF16 = mybir.dt.bfloat16
AF = mybir.ActivationFunctionType
ALU = mybir.AluOpType

S = 4096
D = 1024
DEPTH = 2
NT = 8
NB = 32
EPS = 1e-6
DMA_RING = 8

C_ID, C_TRI, C_ONE, C_BA, C_BB, C_LT, C_LE, C_MR, C_END = 0, 128, 256, 384, 512, 640, 768, 832, 1344


class Tile:
    __slots__ = ("ap", "lw", "rd", "name")

    def __init__(self, ap, name=""):
        self.ap = ap
        self.lw = None
        self.rd = {}
        self.name = name


class Eng:
    def __init__(self, name, is_dma=False):
        self.name = name
        self.is_dma = is_dma
        self.instrs = []
        self.pending = set()
        self.gfirst = None


class Instr:
    __slots__ = ("fn", "deps", "signal", "val")

    def __init__(self, fn, deps):
        self.fn = fn
        self.deps = deps
        self.signal = False
        self.val = 0


class Plan:
    def __init__(self):
        self.pe = Eng("pe")
        self.act = Eng("act")
        self.dve = Eng("dve")
        self.pool = Eng("pool")
        self.sp = Eng("sp", is_dma=True)
        self.engs = [self.sp, self.pe, self.act, self.dve, self.pool]

    def op(self, eng, fn, reads=(), writes=()):
        deps = set()
        for t in reads:
            if t.lw is not None:
                deps.add(t.lw)
        for t in writes:
            if t.lw is not None:
                deps.add(t.lw)
            for e, s in t.rd.items():
                if e.is_dma:
                    for ss in s:
                        deps.add((e, ss))
                else:
                    deps.add((e, s))
        if eng is self.pe:
            deps = {d for d in deps if d[0] is not self.pe}
        if eng.pending:
            deps |= eng.pending
            eng.pending = set()
        seq = len(eng.instrs)
        if eng.gfirst is not None:
            if eng.gfirst < 0:
                eng.gfirst = seq
            else:
                eng.instrs[eng.gfirst].deps |= deps
                deps = set()
        eng.instrs.append(Instr(fn, deps))
        for t in reads:
            if eng.is_dma:
                t.rd.setdefault(eng, []).append(seq)
            else:
                t.rd[eng] = seq
        for t in writes:
            t.lw = (eng, seq)
            t.rd = {}
        return seq

    def group(self, eng):
        plan = self

        class _G:
            def __enter__(self_g):
                eng.gfirst = -1

            def __exit__(self_g, *a):
                eng.gfirst = None
        return _G()

    def barrier(self):
        deps = set()
        for e in self.engs:
            n = len(e.instrs)
            if n == 0:
                continue
            if e.is_dma:
                for s in range(max(0, n - DMA_RING), n):
                    deps.add((e, s))
            else:
                deps.add((e, n - 1))
        for e in self.engs:
            e.pending |= {d for d in deps if d[0] is not e or e.is_dma}

    def emit(self, nc, stack):
        for e in self.engs:
            for ins in e.instrs:
                for (de, ds) in ins.deps:
                    if not de.is_dma:
                        de.instrs[ds].signal = True
        for e in self.engs:
            if e.is_dma:
                continue
            c = 0
            for ins in e.instrs:
                if ins.signal:
                    c += 1
                    ins.val = c
        sems = {}
        for e in self.engs:
            if e.is_dma:
                sems[e] = [stack.enter_context(nc.semaphore(f"s_{e.name}{i}")) for i in range(DMA_RING)]
            else:
                sems[e] = stack.enter_context(nc.semaphore(f"s_{e.name}"))
        block = stack.enter_context(nc.Block())

        def resolve(dep):
            de, ds = dep
            if de.is_dma:
                return (sems[de][ds % DMA_RING], 16 * (ds // DMA_RING + 1))
            return (sems[de], de.instrs[ds].val)

        def run(e, beng):
            known = {}
            n = len(e.instrs)
            for seq, ins in enumerate(e.instrs):
                waits = {}
                if e.is_dma and seq >= DMA_RING:
                    s, v = resolve((e, seq - DMA_RING))
                    waits[s] = max(waits.get(s, 0), v)
                for dep in ins.deps:
                    s, v = resolve(dep)
                    if v > waits.get(s, 0):
                        waits[s] = v
                for s, v in waits.items():
                    if known.get(s, 0) >= v:
                        continue
                    beng.wait_ge(s, v)
                    known[s] = v
                bi = ins.fn(beng)
                if e.is_dma:
                    bi.then_inc(sems[e][seq % DMA_RING], 16)
                elif ins.signal:
                    bi.then_inc(sems[e], 1)
            if e.is_dma:
                for seq in range(max(0, n - DMA_RING), n):
                    s, v = resolve((e, seq))
                    if known.get(s, 0) < v:
                        beng.wait_ge(s, v)
                        known[s] = v

        @block.sync
        def _(sync):
            run(self.sp, sync)

        @block.tensor
        def _(tensor):
            run(self.pe, tensor)

        @block.scalar
        def _(scalar):
            run(self.act, scalar)

        @block.vector
        def _(vector):
            run(self.dve, vector)

        @block.gpsimd
        def _(gpsimd):
            run(self.pool, gpsimd)


class RPool:
    def __init__(self, tiles):
        self.tiles = tiles
        self.i = 0

    def next(self):
        t = self.tiles[self.i % len(self.tiles)]
        self.i += 1
        return t


def build_program(debug=False, n_layers=DEPTH, stop=None):
    nc = bass.Bass("TRN2", target_bir_lowering=False)
    dram = lambda name, shape, dt=F32, kind="ExternalInput": nc.dram_tensor(name, shape, dt, kind=kind).ap()
    x = dram("x", [S, D])
    p_in = dram("p", [DEPTH, S, 256])
    w_in = dram("w_in", [DEPTH, D, 4096])
    w_out = dram("w_out", [DEPTH, D, D])
    w_pg = dram("w_pg", [DEPTH, D, D])
    w_pp = dram("w_pp", [DEPTH, 256, D])
    vecs = dram("vecs", [7, 128, D])
    colv = dram("colv", [128, 24])
    cst = dram("cst", [128, C_END])
    out = dram("out", [S, D], F32, "ExternalOutput")
    hres = dram("hres", [S, D], F32, "Internal")
    yT = dram("yT", [D, S], BF16, "Internal")

    P = Plan()
    pe, act, dve, pool, sp = P.pe, P.act, P.dve, P.pool, P.sp
    st = ExitStack()
    sbt = lambda name, shape, dt=F32: st.enter_context(nc.sbuf_tensor(name, shape, dt))

    uT_sb = sbt("uT", [128, 8, S], BF16)
    uT_t = [Tile(uT_sb[:, :, j * 512:(j + 1) * 512], f"uT{j}") for j in range(NT)]
    cst_sb = sbt("cst_sb", [128, C_END])
    cbf_sb = sbt("cbf_sb", [128, C_LE], BF16)
    colv_sb = sbt("colv_sb", [128, 24])
    lbw_sb = sbt("lbw_sb", [128, 32])
    t_cst = Tile(cst_sb[:])
    t_cbf = Tile(cbf_sb[:])
    t_colv = Tile(colv_sb[:])
    t_lbw = Tile(lbw_sb[:])
    OVW = 34800
    ov = sbt("ov", [128, OVW])
    ovp = [0]

    ovmax = [0]

    def ov_reset():
        ovmax[0] = max(ovmax[0], ovp[0])
        ovp[0] = 0

    def ov_alloc(shape, dt=F32, name=""):
        n = int(np.prod(shape[1:]))
        words = n if dt == F32 else (n + 1) // 2
        a = ov[:, ovp[0]:ovp[0] + words]
        ovp[0] += words
        assert ovp[0] <= OVW, ("overlay overflow", ovp[0])
        if dt != F32:
            a = a.bitcast(dt)
        if len(shape) == 3:
            a = a.rearrange("p (a b) -> p a b", a=shape[1])
        return Tile(a, name)

    def ov_pool(n, shape, dt=F32, name=""):
        return RPool([ov_alloc(shape, dt, f"{name}{i}") for i in range(n)])

    psbig = [st.enter_context(nc.psum_tensor(f"ps{i}", [128, 1024], F32)) for i in range(4)]
    bank = [Tile(psbig[i // 2][:, (i % 2) * 512:(i % 2) * 512 + 512], f"bank{i}") for i in range(8)]

    ident_bf = cbf_sb[:, C_ID:C_ID + 128]
    tri_bf = cbf_sb[:, C_TRI:C_TRI + 128]
    ones_bf = cbf_sb[:, C_ONE:C_ONE + 128]
    blkA_bf = cbf_sb[:, C_BA:C_BA + 128]
    blkB_bf = cbf_sb[:, C_BB:C_BB + 128]
    mlt_bf = cbf_sb[:, C_LT:C_LT + 128]
    mle_f = cst_sb[:, C_LE:C_LE + 64]
    mreset_f = cst_sb[:, C_MR:C_MR + 512]

    P.op(sp, lambda e: e.dma_start(out=cst_sb[:], in_=cst), [], [t_cst])
    P.op(sp, lambda e: e.dma_start(out=colv_sb[:], in_=colv), [], [t_colv])
    P.op(dve, lambda e: e.tensor_copy(out=cbf_sb[:], in_=cst_sb[:, 0:C_LE]), [t_cst], [t_cbf])
    L0 = colv_sb[:, 16:20]
    L1 = colv_sb[:, 20:24]
    w = lambda a, b: lbw_sb[:, a:b]
    P.op(dve, lambda e: e.tensor_tensor(out=w(0, 4), in0=L0, in1=L1, op=ALU.max), [t_colv], [t_lbw])
    P.op(dve, lambda e: e.tensor_tensor(out=w(4, 8), in0=L0, in1=w(0, 4), op=ALU.subtract), [t_colv, t_lbw], [t_lbw])
    P.op(dve, lambda e: e.tensor_tensor(out=w(8, 12), in0=L1, in1=w(0, 4), op=ALU.subtract), [t_colv, t_lbw], [t_lbw])
    P.op(act, lambda e: e.activation(out=w(4, 12), in_=w(4, 12), func=AF.Exp), [t_lbw], [t_lbw])
    P.op(dve, lambda e: e.tensor_tensor(out=w(0, 4), in0=w(4, 8), in1=w(8, 12), op=ALU.add), [t_lbw], [t_lbw])
    P.op(dve, lambda e: e.reciprocal(out=w(0, 4), in_=w(0, 4)), [t_lbw], [t_lbw])
    P.op(dve, lambda e: e.tensor_tensor(out=w(4, 8), in0=w(4, 8), in1=w(0, 4), op=ALU.mult), [t_lbw], [t_lbw])
    P.op(dve, lambda e: e.tensor_tensor(out=w(8, 12), in0=w(8, 12), in1=w(0, 4), op=ALU.mult), [t_lbw], [t_lbw])
    P.op(dve, lambda e: e.tensor_tensor(out=w(12, 16), in0=w(4, 8), in1=w(8, 12), op=ALU.add), [t_lbw], [t_lbw])
    P.op(dve, lambda e: e.tensor_tensor(out=w(0, 4), in0=w(4, 8), in1=w(4, 8), op=ALU.subtract), [t_lbw], [t_lbw])
    P.op(dve, lambda e: e.tensor_tensor(out=w(12, 16), in0=w(12, 16), in1=w(4, 8), op=ALU.subtract), [t_lbw], [t_lbw])
    P.op(dve, lambda e: e.tensor_scalar(out=w(16, 20), in0=w(0, 4), scalar1=-1.0, scalar2=1.0, op0=ALU.mult, op1=ALU.add), [t_lbw], [t_lbw])
    P.op(dve, lambda e: e.tensor_scalar(out=w(20, 24), in0=w(12, 16), scalar1=-1.0, scalar2=1.0, op0=ALU.mult, op1=ALU.add), [t_lbw], [t_lbw])

    P.op(act, lambda e: e.activation(out=w(24, 32), in_=w(16, 24), func=AF.Ln), [t_lbw], [t_lbw])

    dbg = {}
    if debug:
        dbg["uT0"] = dram("d_uT0", [128, 8, S], BF16, "ExternalOutput")
        dbg["yT0"] = dram("d_yT0", [D, S], BF16, "ExternalOutput")
        dbg["h1"] = dram("d_h1", [S, D], F32, "ExternalOutput")

    def emit_u1(hb, nm, W, on_dve=False):
        junk = W["junk"].next()
        ssq = W["small"].next()
        if on_dve:
            P.op(dve, lambda e: e.scalar_tensor_tensor(out=junk.ap, in0=hb.ap, scalar=1.0, in1=hb.ap, op0=ALU.mult, op1=ALU.mult, accum_out=ssq.ap[:, 0:1]), [hb], [junk, ssq])
        else:
            P.op(act, lambda e: e.activation(out=junk.ap, in_=hb.ap, func=AF.Square, accum_out=ssq.ap[:, 0:1]), [hb], [junk, ssq])
        P.op(act, lambda e: e.activation(out=ssq.ap[:, 1:2], in_=ssq.ap[:, 0:1], func=AF.Ln, scale=1.0 / D, bias=W["eps"].ap), [ssq, W["eps"]], [ssq])
        P.op(act, lambda e: e.activation(out=ssq.ap[:, 2:3], in_=ssq.ap[:, 1:2], func=AF.Exp, scale=-0.5), [ssq], [ssq])
        ub = W["ub"].next()
        P.op(dve, lambda e: e.scalar_tensor_tensor(out=ub.ap, in0=hb.ap, scalar=ssq.ap[:, 2:3], in1=nm.ap, op0=ALU.mult, op1=ALU.mult), [hb, ssq, nm], [ub])
        return ub

    def emit_u2(ub, tb, W):
        tp = W["tp"].next()
        tpb = tp.ap.bitcast(BF16)
        with P.group(pe):
            for c in range(8):
                P.op(pe, lambda e, c=c: e.transpose(out=tpb[:, c * 128:(c + 1) * 128], in_=ub.ap[:, c * 128:(c + 1) * 128], identity=ident_bf), [ub, t_cbf], [tp])
        ut = uT_t[tb // 4]
        P.op(act, lambda e: e.activation(out=uT_sb[:, :, tb * 128:(tb + 1) * 128], in_=tpb.rearrange("p (c t) -> p c t", c=8), func=AF.Copy), [tp], [ut])

    def emit_u(hb, tb, nm, W, on_dve=False):
        emit_u2(emit_u1(hb, nm, W, on_dve), tb, W)

    def load_w(W, src_ap, rows_c=8):
        ws = W["wst"].next()
        wb = W["wbf"].next()
        P.op(sp, lambda e: e.dma_start(out=ws.ap[:, 0:rows_c, :], in_=src_ap.rearrange("(c p) n -> p c n", p=128)), [], [ws])
        P.op(pool, lambda e: e.tensor_copy(out=wb.ap[:, 0:rows_c, :], in_=ws.ap[:, 0:rows_c, :]), [ws], [wb])
        return wb

    def proj_fm(W, wb, j, bk):
        for c in range(8):
            P.op(pe, lambda e, c=c: e.matmul(bk.ap, lhsT=wb.ap[:, c, :], rhs=uT_sb[:, c, j * 512:(j + 1) * 512], start=(c == 0), stop=(c == 7)), [wb, uT_t[j]], [bk])

    def proj_tok(W, wb, vt, vsb, j):
        bk = W["pbank"].next()
        for q in range(4):
            tb = 4 * j + q
            for c in range(8):
                P.op(pe, lambda e, c=c, q=q, tb=tb: e.matmul(bk.ap[:, q * 128:(q + 1) * 128], lhsT=uT_sb[:, c, tb * 128:(tb + 1) * 128], rhs=wb.ap[:, c, :], start=(c == 0), stop=(c == 7)), [wb, uT_t[j]], [bk])
        P.op(dve, lambda e: e.tensor_copy(out=vsb[:, 4 * j:4 * j + 4, :], in_=bk.ap.rearrange("p (q d) -> p q d", q=4)), [bk], [vt])

    def silu_from_bank(W, bk, outt, lnscale_col=None, sign=-1.0):
        en = outt if lnscale_col is not None else W["f32"].next()
        P.op(act, lambda e: e.activation(out=en.ap, in_=bk.ap, func=AF.Exp, scale=sign), [bk], [en])
        P.op(act, lambda e: e.activation(out=en.ap, in_=en.ap, func=AF.Ln, bias=1.0), [en], [en])
        if lnscale_col is None:
            P.op(act, lambda e: e.activation(out=en.ap, in_=en.ap, func=AF.Exp, scale=-1.0), [en], [en])
            P.op(dve, lambda e: e.tensor_tensor(out=outt.ap, in0=bk.ap, in1=en.ap, op=ALU.mult), [bk, en], [outt])
        else:
            P.op(act, lambda e: e.activation(out=en.ap, in_=en.ap, func=AF.Exp, scale=-1.0, bias=lnscale_col), [en, t_lbw], [en])

    def head_norm_1(W, obank, blk_bf):
        o_sb = W["f32"].next()
        P.op(dve, lambda e: e.tensor_copy(out=o_sb.ap, in_=obank.ap), [obank], [o_sb])
        osq = W["bf"].next()
        P.op(pool, lambda e: e.tensor_tensor(out=osq.ap, in0=o_sb.ap, in1=o_sb.ap, op=ALU.mult), [o_sb], [osq])
        mb = W["mbank"].next()
        P.op(pe, lambda e: e.matmul(mb.ap, lhsT=blk_bf, rhs=osq.ap, start=True, stop=True), [osq, t_cbf], [mb])
        return o_sb, mb

    def head_norm_2(W, o_sb, mb, gcol, sg, row0, j):
        rs = W["f32"].next()
        P.op(act, lambda e: e.activation(out=rs.ap, in_=mb.ap, func=AF.Ln, bias=W["eps"].ap), [mb, W["eps"]], [rs])
        P.op(act, lambda e: e.activation(out=rs.ap, in_=rs.ap, func=AF.Exp, scale=-0.5), [rs], [rs])
        P.op(dve, lambda e: e.scalar_tensor_tensor(out=o_sb.ap, in0=o_sb.ap, scalar=gcol, in1=rs.ap, op0=ALU.mult, op1=ALU.mult), [o_sb, rs, t_colv], [o_sb])
        yb = W["bf"].next()
        P.op(dve, lambda e: e.tensor_tensor(out=yb.ap, in0=o_sb.ap, in1=sg.ap, op=ALU.mult), [o_sb, sg], [yb])
        P.op(sp, lambda e: e.dma_start(out=yT[row0:row0 + 128, j * 512:(j + 1) * 512], in_=yb.ap), [yb], [W["yT_t"]])

    def head_norm_store(W, obank, blk_bf, gcol, sg, row0, j):
        o_sb, mb = head_norm_1(W, obank, blk_bf)
        head_norm_2(W, o_sb, mb, gcol, sg, row0, j)

    def phase_A0():
        ov_reset()
        W = {}
        W["junk"] = ov_pool(1, [128, D], F32, "junk")
        W["small"] = ov_pool(4, [128, 4], F32, "small")
        W["ub"] = ov_pool(2, [128, D], BF16, "ub")
        W["tp"] = RPool([bank[6], bank[7]])
        W["eps"] = ov_alloc([128, 1], F32, "eps")
        P.op(pool, lambda e: e.memset(W["eps"].ap, EPS), [], [W["eps"]])
        nm = ov_alloc([128, D], F32, "nm")
        P.op(sp, lambda e: e.dma_start(out=nm.ap, in_=vecs[0]), [], [nm])
        hbp = ov_pool(3, [128, D], F32, "hb")
        prev = None
        for tb in range(NB):
            hb = hbp.next()
            P.op(sp, lambda e, hb=hb, tb=tb: e.dma_start(out=hb.ap, in_=x[tb * 128:(tb + 1) * 128, :]), [], [hb])
            ub = emit_u1(hb, nm, W, on_dve=(tb % 2 == 1))
            if prev is not None:
                emit_u2(*prev, W)
            prev = (ub, tb)
        emit_u2(*prev, W)

    def phase_BC(l, stop=None):
        ov_reset()
        yT_tile = Tile(None, "yT")

        def common_alloc():
            ov_reset()
            W = {}
            W["eps"] = ov_alloc([128, 1], F32, "eps")
            P.op(pool, lambda e: e.memset(W["eps"].ap, EPS), [], [W["eps"]])
            W["wst"] = ov_pool(3, [128, 8, 128], F32, "wst")
            W["wbf"] = ov_pool(8, [128, 8, 128], BF16, "wbf")
            W["f32"] = ov_pool(6, [128, 512], F32, "f32")
            W["bf"] = ov_pool(4, [128, 512], BF16, "bf")
            W["yT_t"] = yT_tile
            W["pbank"] = RPool([bank[4], bank[5]])
            W["mbank"] = RPool([bank[7]])
            vsb_t = ov_alloc([128, NB, 128], BF16, "vsb")
            vsb = vsb_t.ap
            vts = [Tile(vsb[:, 4 * j:4 * j + 4, :], f"v{j}") for j in range(NT)]
            sgp = ov_pool(2, [128, 512], F32, "sg")
            return W, vsb, vts, sgp

        W, vsb, vts, sgp = common_alloc()

        kT_ts = [ov_alloc([128, S], BF16, "kT0"), ov_alloc([128, S], BF16, "kT1")]
        kts_p = [[Tile(kT_ts[par].ap[:, j * 512:(j + 1) * 512]) for j in range(NT)] for par in range(2)]
        vsb2_t = ov_alloc([128, NB, 128], BF16, "vsb2")
        vsb_p = [vsb, vsb2_t.ap]
        vts_p = [vts, [Tile(vsb2_t.ap[:, 4 * j:4 * j + 4, :], f"v2_{j}") for j in range(NT)]]
        qp = ov_pool(2, [128, 512], BF16, "q")
        ep = ov_pool(3, [128, 2, 512], F32, "e")
        spp = ov_pool(3, [128, 2, 512], BF16, "sp")
        wxp = ov_pool(2, [128, 2, 512], F32, "wx")
        wp = ov_pool(3, [128, 2, 512], BF16, "w")
        Ap = ov_pool(4, [128, 2, 512], BF16, "A")
        zps, rps = psbig[0], psbig[1]
        z3 = zps[:, :].rearrange("p (h c) -> p h c", h=2)
        r3 = rps[:, :].rearrange("p (h c) -> p h c", h=2)
        zb = [bank[0], bank[1]]
        rb = [bank[2], bank[3]]
        ob = bank[6]
        mlt3 = mlt_bf.unsqueeze(1).broadcast_to([128, 2, 128])

        items = []
        wts = {}

        def loadw_b(hp):
            wts[hp] = tuple(load_w(W, w_in[l, :, base + hp * 128:base + hp * 128 + 128]) for base in (2048, 2560, 3072, 3584))

        tiles_l = [(hp, j) for hp in range(4) for j in range(NT)]
        for idx, (hp, j) in enumerate(tiles_l):
            if idx == 0:
                items.append(("loadw", 0))
                items.append(("prep", hp, j))
            n_it = 4 * j + 4
            for m, kb in enumerate(range(4 * j + 3, -1, -1)):
                items.append(("att", hp, j, kb))
                if j == 4 and m == 0 and hp < 3:
                    items.append(("loadw", hp + 1))
                if m == n_it // 2 - 1 and idx + 1 < len(tiles_l):
                    items.append(("prep",) + tiles_l[idx + 1])

        state = {}
        deferred = []

        def proj_half(wb, j, bk, half):
            with P.group(pe):
                for c in range(4 * half, 4 * half + 4):
                    P.op(pe, lambda e, c=c: e.matmul(bk.ap, lhsT=wb.ap[:, c, :], rhs=uT_sb[:, c, j * 512:(j + 1) * 512], start=(c == 0), stop=(c == 7)), [wb, uT_t[j]], [bk])

        def prep(hp, j):
            wq, wk, wv, wg = wts[hp]
            kts = kts_p[hp % 2]
            vt, vs = vts_p[hp % 2][j], vsb_p[hp % 2]
            bk = W["pbank"].next()
            for q in range(4):
                tb = 4 * j + q
                with P.group(pe):
                    for c in range(8):
                        P.op(pe, lambda e, c=c, q=q, tb=tb, bk=bk: e.matmul(bk.ap[:, q * 128:(q + 1) * 128], lhsT=uT_sb[:, c, tb * 128:(tb + 1) * 128], rhs=wv.ap[:, c, :], start=(c == 0), stop=(c == 7)), [wv, uT_t[j]], [bk])
                yield
            bk1 = W["pbank"].next()
            proj_half(wq, j, bk1, 0)
            P.op(dve, lambda e, bk=bk: e.tensor_copy(out=vs[:, 4 * j:4 * j + 4, :], in_=bk.ap.rearrange("p (q d) -> p q d", q=4)), [bk], [vt])
            yield
            proj_half(wq, j, bk1, 1)
            yield
            bk2 = W["pbank"].next()
            proj_half(wk, j, bk2, 0)
            qt = qp.next()
            P.op(dve, lambda e: e.tensor_copy(out=qt.ap, in_=bk1.ap), [bk1], [qt])
            yield
            proj_half(wk, j, bk2, 1)
            yield
            bk3 = W["pbank"].next()
            proj_half(wg, j, bk3, 0)
            P.op(dve, lambda e: e.tensor_copy(out=kts[j].ap, in_=bk2.ap), [bk2], [kts[j]])
            yield
            proj_half(wg, j, bk3, 1)
            yield
            sg = sgp.next()
            silu_from_bank(W, bk3, sg)
            state[(hp, j)] = dict(q=qt, sg=sg)

        def s1(it):
            _, hp, j, kb = it
            qt = state[(hp, j)]["q"]
            if kb == 4 * j + 3:
                a0, a1 = Ap.next(), Ap.next()
                P.op(pool, lambda e: e.memset(a0.ap, 0.0), [], [a0])
                P.op(pool, lambda e: e.memset(a1.ap, 0.0), [], [a1])
                state["A"] = [a0, a1]
            c0 = max(0, kb - 4 * j) * 128
            kj = kb // 4
            kT = kT_ts[hp % 2].ap
            kts = kts_p[hp % 2]
            with P.group(pe):
                for hh in range(2):
                    r = slice(hh * 64, hh * 64 + 64)
                    P.op(pe, lambda e, hh=hh, r=r: e.matmul(zps[:, hh * 512 + c0:hh * 512 + 512], lhsT=kT[r, kb * 128:(kb + 1) * 128], rhs=qt.ap[r, c0:512], start=True, stop=True), [kts[kj], qt], [zb[hh]])
            et = ep.next()
            P.op(act, lambda e: e.activation(out=et.ap[:, :, c0:512], in_=z3[:, :, c0:512], func=AF.Exp, scale=0.125), zb, [et])
            spt = spp.next()
            P.op(act, lambda e: e.activation(out=spt.ap[:, :, c0:512], in_=et.ap[:, :, c0:512], func=AF.Ln, bias=1.0), [et], [spt])
            if kb >= 4 * j:
                P.op(pool, lambda e: e.tensor_tensor(out=spt.ap[:, :, c0:c0 + 128], in0=spt.ap[:, :, c0:c0 + 128], in1=mlt3, op=ALU.mult), [spt, t_cbf], [spt])
            return dict(c0=c0, sp=spt, e=et, qt=qt, A=state["A"])

        def s2(it, ctx):
            _, hp, j, kb = it
            c0, spt, et = ctx["c0"], ctx["sp"], ctx["e"]
            n = (4 * j + 3) - kb
            acur, anxt = ctx["A"][n % 2], ctx["A"][(n + 1) % 2]
            with P.group(pe):
                for hh in range(2):
                    P.op(pe, lambda e, hh=hh: e.matmul(rps[:, hh * 512 + c0:hh * 512 + 512], lhsT=tri_bf, rhs=spt.ap[:, hh, c0:512], start=True, stop=(n == 0)), [spt, t_cbf], [rb[hh]])
                    if n > 0:
                        P.op(pe, lambda e, hh=hh: e.matmul(rps[:, hh * 512 + c0:hh * 512 + 512], lhsT=ones_bf, rhs=acur.ap[:, hh, c0:512], start=False, stop=True), [acur, t_cbf], [rb[hh]])
            if kb > 0:
                P.op(dve, lambda e: e.tensor_tensor(out=anxt.ap[:, :, c0:512], in0=acur.ap[:, :, c0:512], in1=spt.ap[:, :, c0:512], op=ALU.add), [acur, spt], [anxt])
            wx = wxp.next()
            P.op(act, lambda e: e.activation(out=wx.ap[:, :, c0:512], in_=r3[:, :, c0:512], func=AF.Exp, scale=-1.0), rb, [wx])
            wt = wp.next()
            P.op(dve, lambda e: e.tensor_tensor(out=wt.ap[:, :, c0:512], in0=et.ap[:, :, c0:512], in1=wx.ap[:, :, c0:512], op=ALU.mult), [et, wx], [wt])
            if kb >= 4 * j:
                P.op(pool, lambda e: e.tensor_tensor(out=wt.ap[:, :, c0:c0 + 128], in0=wt.ap[:, :, c0:c0 + 128], in1=mlt3, op=ALU.mult), [wt, t_cbf], [wt])
            ctx["w"] = wt

        def s3(it, ctx):
            _, hp, j, kb = it
            c0, wt = ctx["c0"], ctx["w"]
            with P.group(pe):
                for hh in range(2):
                    P.op(pe, lambda e, hh=hh: e.matmul(ob.ap[hh * 64:hh * 64 + 64, c0:512], lhsT=vsb_p[hp % 2][:, kb, hh * 64:hh * 64 + 64], rhs=wt.ap[:, hh, c0:512], start=(kb == 4 * j + 3), stop=(kb == 0), skip_group_check=True), [vts_p[hp % 2][kb // 4], wt], [ob])
            if kb == 0:
                o_sb, mb = head_norm_1(W, ob, blkB_bf)
                gcol = colv_sb[:, 8 + l * 4 + hp:8 + l * 4 + hp + 1]
                sgt = state[(hp, j)]["sg"]
                deferred.append([2, lambda: head_norm_2(W, o_sb, mb, gcol, sgt, 512 + hp * 128, j)])

        pend1 = None
        pend2 = None

        def flush():
            nonlocal pend1, pend2
            if pend2 is not None:
                s3(*pend2)
                pend2 = None
            if pend1 is not None:
                s2(*pend1)
                s3(*pend1)
                pend1 = None

        gen = [None, 0]

        def run_deferred(force=False):
            for d in list(deferred):
                d[0] -= 1
                if d[0] <= 0 or force:
                    deferred.remove(d)
                    d[1]()

        def advance(k):
            for _ in range(k):
                if gen[0] is None:
                    return
                try:
                    next(gen[0])
                except StopIteration:
                    gen[0] = None

        for ii, it in enumerate(items):
            if it[0] == "loadw":
                loadw_b(it[1])
            elif it[0] == "prep":
                advance(100)
                gen[0] = prep(*it[1:])
                n_left = 0
                for it2 in items[ii + 1:]:
                    if it2[0] == "att":
                        if (it2[1], it2[2]) == (it[1], it[2]):
                            break
                        n_left += 1
                gen[1] = 100 if n_left == 0 else -(-11 // n_left)
                if n_left == 0:
                    advance(100)
            else:
                if (it[1], it[2]) not in state:
                    advance(100)
                ctx = s1(it)
                advance(gen[1])
                if pend1 is not None:
                    s2(*pend1)
                if pend2 is not None:
                    s3(*pend2)
                pend2 = pend1
                pend1 = (it, ctx)
                run_deferred()
        flush()
        run_deferred(force=True)

        if stop == "B":
            return W["yT_t"]
        P.barrier()
        ov_reset()
        Wc = {}
        Wc["eps"] = ov_alloc([128, 1], F32, "eps")
        P.op(pool, lambda e: e.memset(Wc["eps"].ap, EPS), [], [Wc["eps"]])
        Wc["wst"] = ov_pool(3, [128, 8, 128], F32, "wst")
        Wc["wbf"] = ov_pool(8, [128, 8, 128], BF16, "wbf")
        Wc["yT_t"] = yT_tile

        def a_stream(si, heads):
            Ws = dict(Wc)
            bs = bank[4 * si:4 * si + 4]
            Ws["pbank"] = RPool([bs[0]])
            Ws["mbank"] = RPool([bs[0]])
            Ws["f32"] = ov_pool(4, [128, 512], F32, f"f32_{si}")
            Ws["bf"] = ov_pool(3, [128, 512], BF16, f"bf_{si}")
            mixb, oab, stbk = bs[1], bs[2], bs[3]
            vsb_t = ov_alloc([128, NB, 128], BF16, f"vsb_{si}")
            vsb = vsb_t.ap
            vts = [Tile(vsb[:, 4 * j:4 * j + 4, :], f"v{si}_{j}") for j in range(NT)]
            sgp = ov_pool(2, [128, 512], F32, f"sg_{si}")
            f32b = ov_pool(9, [128, 512], F32, f"fa_{si}")
            bfb = ov_pool(8, [128, 512], BF16, f"ba_{si}")
            khp = ov_pool(2, [128, 4, 128], BF16, f"kht_{si}")
            atp = ov_pool(2, [128, 4, 64], BF16, f"at_{si}")
            gdp = ov_pool(2, [128, 8], F32, f"gd_{si}")
            s32p = ov_pool(2, [128, 128], F32, f"s32_{si}")
            sbfp = ov_pool(2, [128, 128], BF16, f"sbf_{si}")
            atv = mixb.ap[:, 0:256]
            tpv = stbk.ap[:, 256:512].bitcast(BF16)
            for ha in heads:
                wq = load_w(Ws, w_in[l, :, ha * 128:ha * 128 + 128])
                wf = load_w(Ws, w_in[l, :, 512 + ha * 128:512 + ha * 128 + 128])
                wi = load_w(Ws, w_in[l, :, 1024 + ha * 128:1024 + ha * 128 + 128])
                wg = load_w(Ws, w_in[l, :, 1536 + ha * 128:1536 + ha * 128 + 128])
                omlc = lbw_sb[:, 24 + l * 4 + ha:24 + l * 4 + ha + 1]
                s32 = s32p.next()
                sbf = sbfp.next()
                P.op(pool, lambda e, s32=s32: e.memset(s32.ap, 0.0), [], [s32])
                P.op(pool, lambda e, sbf=sbf: e.memset(sbf.ap, 0.0), [], [sbf])
                yield
                for j in range(NT):
                    with P.group(pe):
                        bk = Ws["pbank"].next()
                        proj_fm(Ws, wq, j, bk)
                    qa = f32b.next()
                    silu_from_bank(Ws, bk, qa)
                    yield
                    with P.group(pe):
                        bk = Ws["pbank"].next()
                        proj_fm(Ws, wf, j, bk)
                    ka = f32b.next()
                    silu_from_bank(Ws, bk, ka, lnscale_col=omlc, sign=1.0)
                    lf = f32b.next()
                    P.op(act, lambda e, lf=lf, ka=ka: e.activation(out=lf.ap, in_=ka.ap, func=AF.Ln, scale=-1.0, bias=1.0), [ka], [lf])
                    yield
                    yield
                    with P.group(pe):
                        bk = Ws["pbank"].next()
                        proj_fm(Ws, wg, j, bk)
                    sg = sgp.next()
                    silu_from_bank(Ws, bk, sg)
                    yield
                    bk = Ws["pbank"].next()
                    for q in range(4):
                        tb = 4 * j + q
                        with P.group(pe):
                            for c in range(8):
                                P.op(pe, lambda e, c=c, q=q, tb=tb, bk=bk, wi=wi: e.matmul(bk.ap[:, q * 128:(q + 1) * 128], lhsT=uT_sb[:, c, tb * 128:(tb + 1) * 128], rhs=wi.ap[:, c, :], start=(c == 0), stop=(c == 7)), [wi, uT_t[j]], [bk])
                    P.op(dve, lambda e, bk=bk, j=j: e.tensor_copy(out=vsb[:, 4 * j:4 * j + 4, :], in_=bk.ap.rearrange("p (q d) -> p q d", q=4)), [bk], [vts[j]])
                    yield
                    bt = f32b.next()
                    P.op(dve, lambda e, bt=bt, lf=lf: e.tensor_tensor_scan(out=bt.ap, data0=mreset_f, data1=lf.ap, initial=0.0, op0=ALU.mult, op1=ALU.add), [lf, t_cst], [bt])
                    yield
                    yield
                    b3 = bt.ap.rearrange("p (c s) -> p c s", s=64)
                    d1 = f32b.next()
                    d13 = d1.ap.rearrange("p (c s) -> p c s", s=64)
                    P.op(dve, lambda e, d13=d13, b3=b3: e.tensor_tensor(out=d13, in0=b3, in1=b3[:, :, 31:32].broadcast_to([128, 8, 64]), op=ALU.subtract), [bt], [d1])
                    yield
                    d2 = f32b.next()
                    d23 = d2.ap.rearrange("p (c s) -> p c s", s=64)
                    P.op(dve, lambda e, d23=d23, b3=b3: e.tensor_tensor(out=d23, in0=b3[:, :, 63:64].broadcast_to([128, 8, 64]), in1=b3, op=ALU.subtract), [bt], [d2])
                    yield
                    yield
                    ex = f32b.next()
                    ex2 = f32b.next()
                    ex3 = f32b.next()
                    P.op(act, lambda e, ex=ex, d1=d1: e.activation(out=ex.ap, in_=d1.ap, func=AF.Exp), [d1], [ex])
                    yield
                    P.op(act, lambda e, ex2=ex2, d1=d1: e.activation(out=ex2.ap, in_=d1.ap, func=AF.Exp, scale=-1.0), [d1], [ex2])
                    yield
                    P.op(act, lambda e, ex3=ex3, bt=bt: e.activation(out=ex3.ap, in_=bt.ap, func=AF.Exp), [bt], [ex3])
                    yield
                    P.op(act, lambda e, d2=d2: e.activation(out=d2.ap, in_=d2.ap, func=AF.Exp), [d2], [d2])
                    yield
                    gd = gdp.next()
                    P.op(act, lambda e, gd=gd, bt=bt: e.activation(out=gd.ap, in_=bt.ap[:, 63:512:64], func=AF.Exp), [bt], [gd])
                    yield
                    yield
                    qin, kin, qdec, khat = bfb.next(), bfb.next(), bfb.next(), bfb.next()
                    P.op(dve, lambda e, qin=qin, qa=qa, ex=ex: e.tensor_tensor(out=qin.ap, in0=qa.ap, in1=ex.ap, op=ALU.mult), [qa, ex], [qin])
                    yield
                    P.op(dve, lambda e, kin=kin, ka=ka, ex2=ex2: e.tensor_tensor(out=kin.ap, in0=ka.ap, in1=ex2.ap, op=ALU.mult), [ka, ex2], [kin])
                    yield
                    P.op(dve, lambda e, khat=khat, ka=ka, d2=d2: e.tensor_tensor(out=khat.ap, in0=ka.ap, in1=d2.ap, op=ALU.mult), [ka, d2], [khat])
                    yield
                    P.op(dve, lambda e, qdec=qdec, qa=qa, ex3=ex3: e.tensor_tensor(out=qdec.ap, in0=qa.ap, in1=ex3.ap, op=ALU.mult), [qa, ex3], [qdec])
                    yield
                    yield
                    with P.group(pe):
                        for q in range(4):
                            P.op(pe, lambda e, q=q, khat=khat: e.transpose(out=tpv[:, q * 128:(q + 1) * 128], in_=khat.ap[:, q * 128:(q + 1) * 128], identity=ident_bf), [khat, t_cbf], [stbk])
                    kht = khp.next()
                    P.op(dve, lambda e, kht=kht: e.tensor_copy(out=kht.ap, in_=tpv.rearrange("p (q k) -> p q k", q=4)), [stbk], [kht])
                    yield
                    with P.group(pe):
                        for c in range(8):
                            hf = (c % 2) * 64
                            P.op(pe, lambda e, c=c, hf=hf, kin=kin, qin=qin: e.matmul(mixb.ap[hf:hf + 64, (c // 2) * 64:(c // 2) * 64 + 64], lhsT=kin.ap[:, c * 64:(c + 1) * 64], rhs=qin.ap[:, c * 64:(c + 1) * 64], start=True, stop=True), [kin, qin], [mixb])
                    at = atp.next()
                    P.op(dve, lambda e, at=at: e.tensor_tensor(out=at.ap, in0=atv.rearrange("p (q t) -> p q t", q=4), in1=mle_f.unsqueeze(1).broadcast_to([128, 4, 64]), op=ALU.mult), [mixb, t_cst], [at])
                    yield
                    yield
                    mode[si] = "chunk"
                    for c in range(8):
                        tb = 4 * j + c // 2
                        rows = slice((c % 2) * 64, (c % 2) * 64 + 64)
                        with P.group(pe):
                            P.op(pe, lambda e, c=c, sbf=sbf, qdec=qdec: e.matmul(oab.ap[:, c * 64:(c + 1) * 64], lhsT=sbf.ap, rhs=qdec.ap[:, c * 64:(c + 1) * 64], start=True, stop=False), [sbf, qdec], [oab])
                            P.op(pe, lambda e, c=c, rows=rows, tb=tb, at=at: e.matmul(oab.ap[:, c * 64:(c + 1) * 64], lhsT=vsb[rows, tb, :], rhs=at.ap[rows, c // 2, :], start=False, stop=True), [vts[j], at], [oab])
                            P.op(pe, lambda e, c=c, rows=rows, tb=tb, kht=kht: e.matmul(stbk.ap[:, (c % 2) * 128:(c % 2) * 128 + 128], lhsT=kht.ap[rows, c // 2, :], rhs=vsb[rows, tb, :], start=True, stop=True), [kht, vts[j]], [stbk])
                        s32n = s32p.next()
                        sbfn = sbfp.next()
                        sv = stbk.ap[:, (c % 2) * 128:(c % 2) * 128 + 128]
                        P.op(dve, lambda e, sbfn=sbfn, s32=s32, gd=gd, c=c, sv=sv: e.scalar_tensor_tensor(out=sbfn.ap, in0=s32.ap, scalar=gd.ap[:, c:c + 1], in1=sv, op0=ALU.mult, op1=ALU.add), [s32, gd, stbk], [sbfn])
                        P.op(dve, lambda e, s32n=s32n, s32=s32, gd=gd, c=c, sv=sv: e.scalar_tensor_tensor(out=s32n.ap, in0=s32.ap, scalar=gd.ap[:, c:c + 1], in1=sv, op0=ALU.mult, op1=ALU.add), [s32, gd, stbk], [s32n])
                        s32, sbf = s32n, sbfn
                        yield
                    mode[si] = "op"
                    head_norm_store(Ws, oab, blkA_bf, colv_sb[:, l * 4 + ha:l * 4 + ha + 1], sg, ha * 128, j)
                    yield

        mode = {0: "op", 1: "op"}
        gens = {0: a_stream(0, [0, 1]), 1: a_stream(1, [2, 3])}

        def adv(si, k):
            for _ in range(k):
                if si not in gens:
                    return
                try:
                    next(gens[si])
                except StopIteration:
                    del gens[si]

        adv(0, 14)
        while gens:
            for si in (0, 1):
                if si not in gens:
                    continue
                other = 1 - si
                if mode[si] == "chunk":
                    adv(si, 1)
                elif other in gens and mode[other] == "chunk":
                    adv(si, 3)
                else:
                    adv(si, 1)
        W = Wc
        return W["yT_t"]

    def phase_D(l, yT_tile, last):
        ov_reset()
        W = {}
        W["eps"] = ov_alloc([128, 1], F32, "eps")
        P.op(pool, lambda e: e.memset(W["eps"].ap, EPS), [], [W["eps"]])
        W["wst"] = ov_pool(2, [128, 8, 128], F32, "wst")
        W["junk"] = ov_pool(1, [128, D], BF16, "junk")
        W["small"] = ov_pool(12, [128, 4], F32, "small")
        W["ub"] = ov_pool(2, [128, D], BF16, "ub")
        W["tp"] = RPool([bank[6], bank[7]])
        wo = ov_alloc([128, 8, D], BF16, "wo")
        wg = ov_alloc([128, 8, D], BF16, "wg")
        wpj = ov_alloc([128, 2, D], BF16, "wpj")
        for g in range(8):
            for (dst, src, rc) in ((wo, w_out, 8), (wg, w_pg, 8), (wpj, w_pp, 2)):
                ws = W["wst"].next()
                P.op(sp, lambda e, ws=ws, src=src, g=g, rc=rc: e.dma_start(out=ws.ap[:, 0:rc, :], in_=src[l, :, g * 128:(g + 1) * 128].rearrange("(c p) n -> p c n", p=128)), [], [ws])
                P.op(pool, lambda e, ws=ws, dst=dst, g=g, rc=rc: e.tensor_copy(out=dst.ap[:, :, g * 128:(g + 1) * 128], in_=ws.ap[:, 0:rc, :]), [ws], [dst])
        gn = ov_alloc([128, D], F32, "gn")
        pn = ov_alloc([128, D], F32, "pn")
        nmn = ov_alloc([128, D], F32, "nmn")
        P.op(sp, lambda e: e.dma_start(out=gn.ap, in_=vecs[2 + l]), [], [gn])
        P.op(sp, lambda e: e.dma_start(out=pn.ap, in_=vecs[4 + l]), [], [pn])
        P.op(sp, lambda e: e.dma_start(out=nmn.ap, in_=vecs[6] if last else vecs[l + 1]), [], [nmn])
        hbp = ov_pool(2, [128, D], F32, "hb")
        hmp = ov_pool(3, [128, D], F32, "hm")
        gtp = ov_pool(1, [128, D], F32, "gt")
        pep = ov_pool(3, [128, D], F32, "pe")
        otp = ov_pool(1 if last else 0, [128, D], F32, "ot")
        hnp = ov_pool(2, [128, D], F32, "hn")
        ybp = ov_pool(2, [128, 8, 512], BF16, "yb")
        pbp = ov_pool(2, [128, 256], F32, "pb")
        pbfp = ov_pool(2, [128, 256], BF16, "pbf")
        ptp = ov_pool(2, [128, 2, 128], BF16, "pT")
        ugp = ov_pool(2, [128, 8, 128], BF16, "ugT")
        h_src = x if l == 0 else hres
        hres_t = Tile(None, "hres")
        mixP = (psbig[0], bank[0], bank[1])
        gP = (psbig[1], bank[2], bank[3])
        pP = (psbig[2], bank[4], bank[5])
        ybs = {}

        def rstd_chain(src_ap, src_tiles, on_dve=False):
            junk = W["junk"].next()
            sm = W["small"].next()
            if on_dve:
                P.op(dve, lambda e: e.scalar_tensor_tensor(out=junk.ap, in0=src_ap, scalar=1.0, in1=src_ap, op0=ALU.mult, op1=ALU.mult, accum_out=sm.ap[:, 0:1]), src_tiles, [junk, sm])
            else:
                P.op(act, lambda e: e.activation(out=junk.ap, in_=src_ap, func=AF.Square, accum_out=sm.ap[:, 0:1]), src_tiles, [junk, sm])
            P.op(act, lambda e: e.activation(out=sm.ap[:, 1:2], in_=sm.ap[:, 0:1], func=AF.Ln, scale=1.0 / D, bias=W["eps"].ap), [sm, W["eps"]], [sm])
            P.op(act, lambda e: e.activation(out=sm.ap[:, 2:3], in_=sm.ap[:, 1:2], func=AF.Exp, scale=-0.5), [sm], [sm])
            return sm

        ctxs = {}

        def st_a(tb):
            j, q = tb // 4, tb % 4
            if q == 0:
                yb = ybp.next()
                P.op(sp, lambda e: e.dma_start(out=yb.ap, in_=yT[:, j * 512:(j + 1) * 512].rearrange("(c p) t -> p c t", p=128)), [yT_tile], [yb])
                ybs[j] = yb
            yb = ybs[j]
            hb = hbp.next()
            P.op(sp, lambda e: e.dma_start(out=hb.ap, in_=h_src[tb * 128:(tb + 1) * 128, :]), [hres_t] if l > 0 else [], [hb])
            pb = pbp.next()
            P.op(sp, lambda e: e.dma_start(out=pb.ap, in_=p_in[l, tb * 128:(tb + 1) * 128, :]), [], [pb])
            with P.group(pe):
                for half in range(2):
                    bk = mixP[1 + half]
                    for c in range(8):
                        P.op(pe, lambda e, c=c, half=half, bk=bk: e.matmul(bk.ap, lhsT=yb.ap[:, c, q * 128:(q + 1) * 128], rhs=wo.ap[:, c, half * 512:(half + 1) * 512], start=(c == 0), stop=(c == 7)), [yb, wo], [bk])
            hm = hmp.next()
            P.op(dve, lambda e: e.tensor_tensor(out=hm.ap, in0=mixP[0][:, :], in1=hb.ap, op=ALU.add), [mixP[1], mixP[2], hb], [hm])
            pbf = pbfp.next()
            P.op(pool, lambda e: e.tensor_copy(out=pbf.ap, in_=pb.ap), [pb], [pbf])
            tp2 = W["tp"].next()
            tpv2 = tp2.ap.bitcast(BF16)
            with P.group(pe):
                for c in range(2):
                    P.op(pe, lambda e, c=c: e.transpose(out=tpv2[:, c * 128:(c + 1) * 128], in_=pbf.ap[:, c * 128:(c + 1) * 128], identity=ident_bf), [pbf, t_cbf], [tp2])
            pT = ptp.next()
            P.op(act, lambda e: e.activation(out=pT.ap, in_=tpv2[:, 0:256].rearrange("p (c t) -> p c t", c=2), func=AF.Copy), [tp2], [pT])
            ctxs[tb] = dict(hm=hm, pT=pT)

        def st_b1(tb):
            cx = ctxs[tb]
            hm = cx["hm"]
            sm = rstd_chain(hm.ap, [hm], on_dve=True)
            ug = W["ub"].next()
            P.op(dve, lambda e: e.scalar_tensor_tensor(out=ug.ap, in0=hm.ap, scalar=sm.ap[:, 2:3], in1=gn.ap, op0=ALU.mult, op1=ALU.mult), [hm, sm, gn], [ug])
            cx["ug"] = ug

        def st_b2(tb):
            cx = ctxs[tb]
            pT, ug = cx["pT"], cx["ug"]
            with P.group(pe):
                for half in range(2):
                    bk = pP[1 + half]
                    for c in range(2):
                        P.op(pe, lambda e, c=c, half=half, bk=bk: e.matmul(bk.ap, lhsT=pT.ap[:, c, :], rhs=wpj.ap[:, c, half * 512:(half + 1) * 512], start=(c == 0), stop=(c == 1)), [pT, wpj], [bk])
            sm2 = rstd_chain(pP[0][:, :], [pP[1], pP[2]])
            pet = pep.next()
            P.op(dve, lambda e: e.scalar_tensor_tensor(out=pet.ap, in0=pP[0][:, :], scalar=sm2.ap[:, 2:3], in1=pn.ap, op0=ALU.mult, op1=ALU.mult), [pP[1], pP[2], sm2, pn], [pet])
            tp = W["tp"].next()
            tpv = tp.ap.bitcast(BF16)
            with P.group(pe):
                for c in range(8):
                    P.op(pe, lambda e, c=c: e.transpose(out=tpv[:, c * 128:(c + 1) * 128], in_=ug.ap[:, c * 128:(c + 1) * 128], identity=ident_bf), [ug, t_cbf], [tp])
            ugT = ugp.next()
            P.op(act, lambda e: e.activation(out=ugT.ap, in_=tpv.rearrange("p (c t) -> p c t", c=8), func=AF.Copy), [tp], [ugT])
            cx["pet"] = pet
            cx["ugT"] = ugT

        def st_c(tb):
            cx = ctxs[tb]
            hm, pet, ugT = cx["hm"], cx["pet"], cx["ugT"]
            with P.group(pe):
                for half in range(2):
                    bk = gP[1 + half]
                    for c in range(8):
                        P.op(pe, lambda e, c=c, half=half, bk=bk: e.matmul(bk.ap, lhsT=ugT.ap[:, c, :], rhs=wg.ap[:, c, half * 512:(half + 1) * 512], start=(c == 0), stop=(c == 7)), [ugT, wg], [bk])
            gt = gtp.next()
            P.op(act, lambda e: e.activation(out=gt.ap, in_=gP[0][:, :], func=AF.Exp, scale=-1.0), [gP[1], gP[2]], [gt])
            P.op(act, lambda e: e.activation(out=gt.ap, in_=gt.ap, func=AF.Ln, bias=1.0), [gt], [gt])
            P.op(act, lambda e: e.activation(out=gt.ap, in_=gt.ap, func=AF.Exp, scale=-1.0), [gt], [gt])
            P.op(pool, lambda e: e.tensor_tensor(out=gt.ap, in0=gt.ap, in1=pet.ap, op=ALU.mult), [pet, gt], [gt])
            hn = hnp.next()
            P.op(pool, lambda e: e.tensor_tensor(out=hn.ap, in0=gt.ap, in1=hm.ap, op=ALU.add), [gt, hm], [hn])
            if not last:
                P.op(sp, lambda e: e.dma_start(out=hres[tb * 128:(tb + 1) * 128, :], in_=hn.ap), [hn], [hres_t])
                if debug and l == 0:
                    P.op(sp, lambda e: e.dma_start(out=dbg["h1"][tb * 128:(tb + 1) * 128, :], in_=hn.ap), [hn], [])
            cx["hn"] = hn

        def st_d1(tb):
            cx = ctxs[tb]
            hn = cx["hn"]
            if not last:
                cx["ub"] = emit_u1(hn, nmn, W, on_dve=True)
            else:
                sm3 = rstd_chain(hn.ap, [hn], on_dve=True)
                ot = otp.next()
                P.op(dve, lambda e: e.scalar_tensor_tensor(out=ot.ap, in0=hn.ap, scalar=sm3.ap[:, 2:3], in1=nmn.ap, op0=ALU.mult, op1=ALU.mult), [hn, sm3, nmn], [ot])
                P.op(sp, lambda e: e.dma_start(out=out[tb * 128:(tb + 1) * 128, :], in_=ot.ap), [ot], [])

        def st_d2(tb):
            if not last:
                emit_u2(ctxs[tb]["ub"], tb, W)
            del ctxs[tb]

        ok = lambda t: 0 <= t < NB
        for i in range(NB + 3):
            if ok(i - 1):
                st_b1(i - 1)
            if ok(i - 3):
                st_d1(i - 3)
            if ok(i):
                st_a(i)
            if ok(i - 2):
                st_c(i - 2)
            if ok(i - 1):
                st_b2(i - 1)
            if ok(i - 3):
                st_d2(i - 3)

    phase_A0()
    for l in range(n_layers):
        P.barrier()
        if debug and l == 0:
            for c in range(8):
                P.op(sp, lambda e, c=c: e.dma_start(out=dbg["uT0"][:, c, :], in_=uT_sb[:, c, :]), uT_t, [])
        if stop == "A0":
            break
        yt = phase_BC(l, stop)
        P.barrier()
        if debug and l == 0:
            for c in range(8):
                P.op(sp, lambda e, c=c: e.dma_start(out=dbg["yT0"][c * 128:(c + 1) * 128, :], in_=yT[c * 128:(c + 1) * 128, :]), [yt], [])
        if stop in ("B", "BC"):
            break
        phase_D(l, yt, last=(l == n_layers - 1))
    P.emit(nc, st)
    st.close()
    return nc


def make_consts():
    c = np.zeros((128, C_END), np.float32)
    i = np.arange(128)
    c[:, C_ID:C_ID + 128] = np.eye(128)
    c[:, C_TRI:C_TRI + 128] = (i[:, None] >= i[None, :])
    c[:, C_ONE:C_ONE + 128] = 1.0
    c[:, C_BA:C_BA + 128] = 1.0 / 128
    c[:, C_BB:C_BB + 128] = ((i[:, None] // 64) == (i[None, :] // 64)) / 64.0
    c[:, C_LT:C_LT + 128] = (i[:, None] < i[None, :])
    c[:, C_LE:C_LE + 64] = ((i[:, None] % 64) <= np.arange(64)[None, :])
    m = np.ones((128, 512), np.float32)
    m[:, ::64] = 0.0
    c[:, C_MR:C_MR + 512] = m
    return c


_NC_CACHE = {}


def kernel(x, p, norm_mix, w_in, a_out_norm, b_out_norm, w_out, lb_logits,
           ple_gate_norm, w_ple_gate, w_ple_proj, ple_post_norm, final_norm):
    f = lambda a: np.ascontiguousarray(np.asarray(a, dtype=np.float32))
    x, p = f(x), f(p)
    B = x.shape[0]
    rep = lambda v: np.broadcast_to(f(v)[None, :], (128, D))
    vecs = np.ascontiguousarray(np.stack([rep(norm_mix[0]), rep(norm_mix[1]), rep(ple_gate_norm[0]), rep(ple_gate_norm[1]),
                                          rep(ple_post_norm[0]), rep(ple_post_norm[1]), rep(final_norm)], axis=0))
    colv = np.zeros((128, 24), np.float32)
    aon, bon, lbl = f(a_out_norm), f(b_out_norm), f(lb_logits)
    for l in range(DEPTH):
        for h in range(4):
            colv[:, l * 4 + h] = aon[l, h * 128:(h + 1) * 128]
            colv[:, 8 + l * 4 + h] = bon[l, h * 128:(h + 1) * 128]
            colv[:, 16 + l * 4 + h] = lbl[l, h * 128:(h + 1) * 128]
    cst = make_consts()
    if "nc" not in _NC_CACHE:
        _NC_CACHE["nc"] = build_program()
    nc = _NC_CACHE["nc"]
    shared = dict(w_in=f(w_in), w_out=f(w_out), w_pg=f(w_ple_gate), w_pp=f(w_ple_proj), vecs=vecs, colv=colv, cst=cst)
    in_maps = []
    for b in range(B):
        d = dict(shared)
        d["x"] = np.ascontiguousarray(x[b])
        d["p"] = np.ascontiguousarray(p[:, b])
        in_maps.append(d)
    res = run_bass_kernel_spmd(nc, in_maps, core_ids=list(range(B)))
    return np.stack([np.asarray(r["out"], dtype=np.float32) for r in res.results], axis=0)
```

```python
from contextlib import ExitStack
import numpy as np
import concourse.bass as bass
import concourse.mybir as mybir
from concourse.bass_utils import run_bass_kernel_spmd

F32 = mybir.dt.float32
BF16 = mybir.dt.bfloat16
AF = mybir.ActivationFunctionType
ALU = mybir.AluOpType

S = 4096
D = 1024
DEPTH = 2
NT = 8
NB = 32
EPS = 1e-6
DMA_RING = 8

C_ID, C_TRI, C_ONE, C_BA, C_BB, C_LT, C_LE, C_MR, C_END = 0, 128, 256, 384, 512, 640, 768, 832, 1344


class Tile:
    __slots__ = ("ap", "lw", "rd", "name")

    def __init__(self, ap, name=""):
        self.ap = ap
        self.lw = None
        self.rd = {}
        self.name = name


class Eng:
    def __init__(self, name, is_dma=False):
        self.name = name
        self.is_dma = is_dma
        self.instrs = []
        self.pending = set()
        self.gfirst = None


class Instr:
    __slots__ = ("fn", "deps", "signal", "val")

    def __init__(self, fn, deps):
        self.fn = fn
        self.deps = deps
        self.signal = False
        self.val = 0


class Plan:
    def __init__(self):
        self.pe = Eng("pe")
        self.act = Eng("act")
        self.dve = Eng("dve")
        self.pool = Eng("pool")
        self.sp = Eng("sp", is_dma=True)
        self.engs = [self.sp, self.pe, self.act, self.dve, self.pool]

    def op(self, eng, fn, reads=(), writes=()):
        deps = set()
        for t in reads:
            if t.lw is not None:
                deps.add(t.lw)
        for t in writes:
            if t.lw is not None:
                deps.add(t.lw)
            for e, s in t.rd.items():
                if e.is_dma:
                    for ss in s:
                        deps.add((e, ss))
                else:
                    deps.add((e, s))
        if eng is self.pe:
            deps = {d for d in deps if d[0] is not self.pe}
        if eng.pending:
            deps |= eng.pending
            eng.pending = set()
        seq = len(eng.instrs)
        if eng.gfirst is not None:
            if eng.gfirst < 0:
                eng.gfirst = seq
            else:
                eng.instrs[eng.gfirst].deps |= deps
                deps = set()
        eng.instrs.append(Instr(fn, deps))
        for t in reads:
            if eng.is_dma:
                t.rd.setdefault(eng, []).append(seq)
            else:
                t.rd[eng] = seq
        for t in writes:
            t.lw = (eng, seq)
            t.rd = {}
        return seq

    def group(self, eng):
        plan = self

        class _G:
            def __enter__(self_g):
                eng.gfirst = -1

            def __exit__(self_g, *a):
                eng.gfirst = None
        return _G()

    def barrier(self):
        deps = set()
        for e in self.engs:
            n = len(e.instrs)
            if n == 0:
                continue
            if e.is_dma:
                for s in range(max(0, n - DMA_RING), n):
                    deps.add((e, s))
            else:
                deps.add((e, n - 1))
        for e in self.engs:
            e.pending |= {d for d in deps if d[0] is not e or e.is_dma}

    def emit(self, nc, stack):
        for e in self.engs:
            for ins in e.instrs:
                for (de, ds) in ins.deps:
                    if not de.is_dma:
                        de.instrs[ds].signal = True
        for e in self.engs:
            if e.is_dma:
                continue
            c = 0
            for ins in e.instrs:
                if ins.signal:
                    c += 1
                    ins.val = c
        sems = {}
        for e in self.engs:
            if e.is_dma:
                sems[e] = [stack.enter_context(nc.semaphore(f"s_{e.name}{i}")) for i in range(DMA_RING)]
            else:
                sems[e] = stack.enter_context(nc.semaphore(f"s_{e.name}"))
        block = stack.enter_context(nc.Block())

        def resolve(dep):
            de, ds = dep
            if de.is_dma:
                return (sems[de][ds % DMA_RING], 16 * (ds // DMA_RING + 1))
            return (sems[de], de.instrs[ds].val)

        def run(e, beng):
            known = {}
            n = len(e.instrs)
            for seq, ins in enumerate(e.instrs):
                waits = {}
                if e.is_dma and seq >= DMA_RING:
                    s, v = resolve((e, seq - DMA_RING))
                    waits[s] = max(waits.get(s, 0), v)
                for dep in ins.deps:
                    s, v = resolve(dep)
                    if v > waits.get(s, 0):
                        waits[s] = v
                for s, v in waits.items():
                    if known.get(s, 0) >= v:
                        continue
                    beng.wait_ge(s, v)
                    known[s] = v
                bi = ins.fn(beng)
                if e.is_dma:
                    bi.then_inc(sems[e][seq % DMA_RING], 16)
                elif ins.signal:
                    bi.then_inc(sems[e], 1)
            if e.is_dma:
                for seq in range(max(0, n - DMA_RING), n):
                    s, v = resolve((e, seq))
                    if known.get(s, 0) < v:
                        beng.wait_ge(s, v)
                        known[s] = v

        @block.sync
        def _(sync):
            run(self.sp, sync)

        @block.tensor
        def _(tensor):
            run(self.pe, tensor)

        @block.scalar
        def _(scalar):
            run(self.act, scalar)

        @block.vector
        def _(vector):
            run(self.dve, vector)

        @block.gpsimd
        def _(gpsimd):
            run(self.pool, gpsimd)


class RPool:
    def __init__(self, tiles):
        self.tiles = tiles
        self.i = 0

    def next(self):
        t = self.tiles[self.i % len(self.tiles)]
        self.i += 1
        return t


def build_program(debug=False, n_layers=DEPTH, stop=None):
    nc = bass.Bass("TRN2", target_bir_lowering=False)
    dram = lambda name, shape, dt=F32, kind="ExternalInput": nc.dram_tensor(name, shape, dt, kind=kind).ap()
    x = dram("x", [S, D])
    p_in = dram("p", [DEPTH, S, 256])
    w_in = dram("w_in", [DEPTH, D, 4096])
    w_out = dram("w_out", [DEPTH, D, D])
    w_pg = dram("w_pg", [DEPTH, D, D])
    w_pp = dram("w_pp", [DEPTH, 256, D])
    vecs = dram("vecs", [7, 128, D])
    colv = dram("colv", [128, 24])
    cst = dram("cst", [128, C_END])
    out = dram("out", [S, D], F32, "ExternalOutput")
    hres = dram("hres", [S, D], F32, "Internal")
    yT = dram("yT", [D, S], BF16, "Internal")

    P = Plan()
    pe, act, dve, pool, sp = P.pe, P.act, P.dve, P.pool, P.sp
    st = ExitStack()
    sbt = lambda name, shape, dt=F32: st.enter_context(nc.sbuf_tensor(name, shape, dt))

    uT_sb = sbt("uT", [128, 8, S], BF16)
    uT_t = [Tile(uT_sb[:, :, j * 512:(j + 1) * 512], f"uT{j}") for j in range(NT)]
    cst_sb = sbt("cst_sb", [128, C_END])
    cbf_sb = sbt("cbf_sb", [128, C_LE], BF16)
    colv_sb = sbt("colv_sb", [128, 24])
    lbw_sb = sbt("lbw_sb", [128, 32])
    t_cst = Tile(cst_sb[:])
    t_cbf = Tile(cbf_sb[:])
    t_colv = Tile(colv_sb[:])
    t_lbw = Tile(lbw_sb[:])
    OVW = 34800
    ov = sbt("ov", [128, OVW])
    ovp = [0]

    ovmax = [0]

    def ov_reset():
        ovmax[0] = max(ovmax[0], ovp[0])
        ovp[0] = 0

    def ov_alloc(shape, dt=F32, name=""):
        n = int(np.prod(shape[1:]))
        words = n if dt == F32 else (n + 1) // 2
        a = ov[:, ovp[0]:ovp[0] + words]
        ovp[0] += words
        assert ovp[0] <= OVW, ("overlay overflow", ovp[0])
        if dt != F32:
            a = a.bitcast(dt)
        if len(shape) == 3:
            a = a.rearrange("p (a b) -> p a b", a=shape[1])
        return Tile(a, name)

    def ov_pool(n, shape, dt=F32, name=""):
        return RPool([ov_alloc(shape, dt, f"{name}{i}") for i in range(n)])

    psbig = [st.enter_context(nc.psum_tensor(f"ps{i}", [128, 1024], F32)) for i in range(4)]
    bank = [Tile(psbig[i // 2][:, (i % 2) * 512:(i % 2) * 512 + 512], f"bank{i}") for i in range(8)]

    ident_bf = cbf_sb[:, C_ID:C_ID + 128]
    tri_bf = cbf_sb[:, C_TRI:C_TRI + 128]
    ones_bf = cbf_sb[:, C_ONE:C_ONE + 128]
    blkA_bf = cbf_sb[:, C_BA:C_BA + 128]
    blkB_bf = cbf_sb[:, C_BB:C_BB + 128]
    mlt_bf = cbf_sb[:, C_LT:C_LT + 128]
    mle_f = cst_sb[:, C_LE:C_LE + 64]
    mreset_f = cst_sb[:, C_MR:C_MR + 512]

    P.op(sp, lambda e: e.dma_start(out=cst_sb[:], in_=cst), [], [t_cst])
    P.op(sp, lambda e: e.dma_start(out=colv_sb[:], in_=colv), [], [t_colv])
    P.op(dve, lambda e: e.tensor_copy(out=cbf_sb[:], in_=cst_sb[:, 0:C_LE]), [t_cst], [t_cbf])
    L0 = colv_sb[:, 16:20]
    L1 = colv_sb[:, 20:24]
    w = lambda a, b: lbw_sb[:, a:b]
    P.op(dve, lambda e: e.tensor_tensor(out=w(0, 4), in0=L0, in1=L1, op=ALU.max), [t_colv], [t_lbw])
    P.op(dve, lambda e: e.tensor_tensor(out=w(4, 8), in0=L0, in1=w(0, 4), op=ALU.subtract), [t_colv, t_lbw], [t_lbw])
    P.op(dve, lambda e: e.tensor_tensor(out=w(8, 12), in0=L1, in1=w(0, 4), op=ALU.subtract), [t_colv, t_lbw], [t_lbw])
    P.op(act, lambda e: e.activation(out=w(4, 12), in_=w(4, 12), func=AF.Exp), [t_lbw], [t_lbw])
    P.op(dve, lambda e: e.tensor_tensor(out=w(0, 4), in0=w(4, 8), in1=w(8, 12), op=ALU.add), [t_lbw], [t_lbw])
    P.op(dve, lambda e: e.reciprocal(out=w(0, 4), in_=w(0, 4)), [t_lbw], [t_lbw])
    P.op(dve, lambda e: e.tensor_tensor(out=w(4, 8), in0=w(4, 8), in1=w(0, 4), op=ALU.mult), [t_lbw], [t_lbw])
    P.op(dve, lambda e: e.tensor_tensor(out=w(8, 12), in0=w(8, 12), in1=w(0, 4), op=ALU.mult), [t_lbw], [t_lbw])
    P.op(dve, lambda e: e.tensor_tensor(out=w(12, 16), in0=w(4, 8), in1=w(8, 12), op=ALU.add), [t_lbw], [t_lbw])
    P.op(dve, lambda e: e.tensor_tensor(out=w(0, 4), in0=w(4, 8), in1=w(4, 8), op=ALU.subtract), [t_lbw], [t_lbw])
    P.op(dve, lambda e: e.tensor_tensor(out=w(12, 16), in0=w(12, 16), in1=w(4, 8), op=ALU.subtract), [t_lbw], [t_lbw])
    P.op(dve, lambda e: e.tensor_scalar(out=w(16, 20), in0=w(0, 4), scalar1=-1.0, scalar2=1.0, op0=ALU.mult, op1=ALU.add), [t_lbw], [t_lbw])
    P.op(dve, lambda e: e.tensor_scalar(out=w(20, 24), in0=w(12, 16), scalar1=-1.0, scalar2=1.0, op0=ALU.mult, op1=ALU.add), [t_lbw], [t_lbw])

    P.op(act, lambda e: e.activation(out=w(24, 32), in_=w(16, 24), func=AF.Ln), [t_lbw], [t_lbw])

    dbg = {}
    if debug:
        dbg["uT0"] = dram("d_uT0", [128, 8, S], BF16, "ExternalOutput")
        dbg["yT0"] = dram("d_yT0", [D, S], BF16, "ExternalOutput")
        dbg["h1"] = dram("d_h1", [S, D], F32, "ExternalOutput")

    def emit_u1(hb, nm, W, on_dve=False):
        junk = W["junk"].next()
        ssq = W["small"].next()
        if on_dve:
            P.op(dve, lambda e: e.scalar_tensor_tensor(out=junk.ap, in0=hb.ap, scalar=1.0, in1=hb.ap, op0=ALU.mult, op1=ALU.mult, accum_out=ssq.ap[:, 0:1]), [hb], [junk, ssq])
        else:
            P.op(act, lambda e: e.activation(out=junk.ap, in_=hb.ap, func=AF.Square, accum_out=ssq.ap[:, 0:1]), [hb], [junk, ssq])
        P.op(act, lambda e: e.activation(out=ssq.ap[:, 1:2], in_=ssq.ap[:, 0:1], func=AF.Ln, scale=1.0 / D, bias=W["eps"].ap), [ssq, W["eps"]], [ssq])
        P.op(act, lambda e: e.activation(out=ssq.ap[:, 2:3], in_=ssq.ap[:, 1:2], func=AF.Exp, scale=-0.5), [ssq], [ssq])
        ub = W["ub"].next()
        P.op(dve, lambda e: e.scalar_tensor_tensor(out=ub.ap, in0=hb.ap, scalar=ssq.ap[:, 2:3], in1=nm.ap, op0=ALU.mult, op1=ALU.mult), [hb, ssq, nm], [ub])
        return ub

    def emit_u2(ub, tb, W):
        tp = W["tp"].next()
        tpb = tp.ap.bitcast(BF16)
        with P.group(pe):
            for c in range(8):
                P.op(pe, lambda e, c=c: e.transpose(out=tpb[:, c * 128:(c + 1) * 128], in_=ub.ap[:, c * 128:(c + 1) * 128], identity=ident_bf), [ub, t_cbf], [tp])
        ut = uT_t[tb // 4]
        P.op(act, lambda e: e.activation(out=uT_sb[:, :, tb * 128:(tb + 1) * 128], in_=tpb.rearrange("p (c t) -> p c t", c=8), func=AF.Copy), [tp], [ut])

    def emit_u(hb, tb, nm, W, on_dve=False):
        emit_u2(emit_u1(hb, nm, W, on_dve), tb, W)

    def load_w(W, src_ap, rows_c=8):
        ws = W["wst"].next()
        wb = W["wbf"].next()
        P.op(sp, lambda e: e.dma_start(out=ws.ap[:, 0:rows_c, :], in_=src_ap.rearrange("(c p) n -> p c n", p=128)), [], [ws])
        P.op(pool, lambda e: e.tensor_copy(out=wb.ap[:, 0:rows_c, :], in_=ws.ap[:, 0:rows_c, :]), [ws], [wb])
        return wb

    def proj_fm(W, wb, j, bk):
        for c in range(8):
            P.op(pe, lambda e, c=c: e.matmul(bk.ap, lhsT=wb.ap[:, c, :], rhs=uT_sb[:, c, j * 512:(j + 1) * 512], start=(c == 0), stop=(c == 7)), [wb, uT_t[j]], [bk])

    def proj_tok(W, wb, vt, vsb, j):
        bk = W["pbank"].next()
        for q in range(4):
            tb = 4 * j + q
            for c in range(8):
                P.op(pe, lambda e, c=c, q=q, tb=tb: e.matmul(bk.ap[:, q * 128:(q + 1) * 128], lhsT=uT_sb[:, c, tb * 128:(tb + 1) * 128], rhs=wb.ap[:, c, :], start=(c == 0), stop=(c == 7)), [wb, uT_t[j]], [bk])
        P.op(dve, lambda e: e.tensor_copy(out=vsb[:, 4 * j:4 * j + 4, :], in_=bk.ap.rearrange("p (q d) -> p q d", q=4)), [bk], [vt])

    def silu_from_bank(W, bk, outt, lnscale_col=None, sign=-1.0):
        en = outt if lnscale_col is not None else W["f32"].next()
        P.op(act, lambda e: e.activation(out=en.ap, in_=bk.ap, func=AF.Exp, scale=sign), [bk], [en])
        P.op(act, lambda e: e.activation(out=en.ap, in_=en.ap, func=AF.Ln, bias=1.0), [en], [en])
        if lnscale_col is None:
            P.op(act, lambda e: e.activation(out=en.ap, in_=en.ap, func=AF.Exp, scale=-1.0), [en], [en])
            P.op(dve, lambda e: e.tensor_tensor(out=outt.ap, in0=bk.ap, in1=en.ap, op=ALU.mult), [bk, en], [outt])
        else:
            P.op(act, lambda e: e.activation(out=en.ap, in_=en.ap, func=AF.Exp, scale=-1.0, bias=lnscale_col), [en, t_lbw], [en])

    def head_norm_1(W, obank, blk_bf):
        o_sb = W["f32"].next()
        P.op(dve, lambda e: e.tensor_copy(out=o_sb.ap, in_=obank.ap), [obank], [o_sb])
        osq = W["bf"].next()
        P.op(pool, lambda e: e.tensor_tensor(out=osq.ap, in0=o_sb.ap, in1=o_sb.ap, op=ALU.mult), [o_sb], [osq])
        mb = W["mbank"].next()
        P.op(pe, lambda e: e.matmul(mb.ap, lhsT=blk_bf, rhs=osq.ap, start=True, stop=True), [osq, t_cbf], [mb])
        return o_sb, mb

    def head_norm_2(W, o_sb, mb, gcol, sg, row0, j):
        rs = W["f32"].next()
        P.op(act, lambda e: e.activation(out=rs.ap, in_=mb.ap, func=AF.Ln, bias=W["eps"].ap), [mb, W["eps"]], [rs])
        P.op(act, lambda e: e.activation(out=rs.ap, in_=rs.ap, func=AF.Exp, scale=-0.5), [rs], [rs])
        P.op(dve, lambda e: e.scalar_tensor_tensor(out=o_sb.ap, in0=o_sb.ap, scalar=gcol, in1=rs.ap, op0=ALU.mult, op1=ALU.mult), [o_sb, rs, t_colv], [o_sb])
        yb = W["bf"].next()
        P.op(dve, lambda e: e.tensor_tensor(out=yb.ap, in0=o_sb.ap, in1=sg.ap, op=ALU.mult), [o_sb, sg], [yb])
        P.op(sp, lambda e: e.dma_start(out=yT[row0:row0 + 128, j * 512:(j + 1) * 512], in_=yb.ap), [yb], [W["yT_t"]])

    def head_norm_store(W, obank, blk_bf, gcol, sg, row0, j):
        o_sb, mb = head_norm_1(W, obank, blk_bf)
        head_norm_2(W, o_sb, mb, gcol, sg, row0, j)

    def phase_A0():
        ov_reset()
        W = {}
        W["junk"] = ov_pool(1, [128, D], F32, "junk")
        W["small"] = ov_pool(4, [128, 4], F32, "small")
        W["ub"] = ov_pool(2, [128, D], BF16, "ub")
        W["tp"] = RPool([bank[6], bank[7]])
        W["eps"] = ov_alloc([128, 1], F32, "eps")
        P.op(pool, lambda e: e.memset(W["eps"].ap, EPS), [], [W["eps"]])
        nm = ov_alloc([128, D], F32, "nm")
        P.op(sp, lambda e: e.dma_start(out=nm.ap, in_=vecs[0]), [], [nm])
        hbp = ov_pool(3, [128, D], F32, "hb")
        prev = None
        for tb in range(NB):
            hb = hbp.next()
            P.op(sp, lambda e, hb=hb, tb=tb: e.dma_start(out=hb.ap, in_=x[tb * 128:(tb + 1) * 128, :]), [], [hb])
            ub = emit_u1(hb, nm, W, on_dve=(tb % 2 == 1))
            if prev is not None:
                emit_u2(*prev, W)
            prev = (ub, tb)
        emit_u2(*prev, W)

    def phase_BC(l, stop=None):
        ov_reset()
        yT_tile = Tile(None, "yT")

        def common_alloc():
            ov_reset()
            W = {}
            W["eps"] = ov_alloc([128, 1], F32, "eps")
            P.op(pool, lambda e: e.memset(W["eps"].ap, EPS), [], [W["eps"]])
            W["wst"] = ov_pool(3, [128, 8, 128], F32, "wst")
            W["wbf"] = ov_pool(8, [128, 8, 128], BF16, "wbf")
            W["f32"] = ov_pool(6, [128, 512], F32, "f32")
            W["bf"] = ov_pool(4, [128, 512], BF16, "bf")
            W["yT_t"] = yT_tile
            W["pbank"] = RPool([bank[4], bank[5]])
            W["mbank"] = RPool([bank[7]])
            vsb_t = ov_alloc([128, NB, 128], BF16, "vsb")
            vsb = vsb_t.ap
            vts = [Tile(vsb[:, 4 * j:4 * j + 4, :], f"v{j}") for j in range(NT)]
            sgp = ov_pool(2, [128, 512], F32, "sg")
            return W, vsb, vts, sgp

        W, vsb, vts, sgp = common_alloc()

        kT_ts = [ov_alloc([128, S], BF16, "kT0"), ov_alloc([128, S], BF16, "kT1")]
        kts_p = [[Tile(kT_ts[par].ap[:, j * 512:(j + 1) * 512]) for j in range(NT)] for par in range(2)]
        vsb2_t = ov_alloc([128, NB, 128], BF16, "vsb2")
        vsb_p = [vsb, vsb2_t.ap]
        vts_p = [vts, [Tile(vsb2_t.ap[:, 4 * j:4 * j + 4, :], f"v2_{j}") for j in range(NT)]]
        qp = ov_pool(2, [128, 512], BF16, "q")
        ep = ov_pool(3, [128, 2, 512], F32, "e")
        spp = ov_pool(3, [128, 2, 512], BF16, "sp")
        wxp = ov_pool(2, [128, 2, 512], F32, "wx")
        wp = ov_pool(3, [128, 2, 512], BF16, "w")
        Ap = ov_pool(4, [128, 2, 512], BF16, "A")
        zps, rps = psbig[0], psbig[1]
        z3 = zps[:, :].rearrange("p (h c) -> p h c", h=2)
        r3 = rps[:, :].rearrange("p (h c) -> p h c", h=2)
        zb = [bank[0], bank[1]]
        rb = [bank[2], bank[3]]
        ob = bank[6]
        mlt3 = mlt_bf.unsqueeze(1).broadcast_to([128, 2, 128])

        items = []
        wts = {}

        def loadw_b(hp):
            wts[hp] = tuple(load_w(W, w_in[l, :, base + hp * 128:base + hp * 128 + 128]) for base in (2048, 2560, 3072, 3584))

        tiles_l = [(hp, j) for hp in range(4) for j in range(NT)]
        for idx, (hp, j) in enumerate(tiles_l):
            if idx == 0:
                items.append(("loadw", 0))
                items.append(("prep", hp, j))
            n_it = 4 * j + 4
            for m, kb in enumerate(range(4 * j + 3, -1, -1)):
                items.append(("att", hp, j, kb))
                if j == 4 and m == 0 and hp < 3:
                    items.append(("loadw", hp + 1))
                if m == n_it // 2 - 1 and idx + 1 < len(tiles_l):
                    items.append(("prep",) + tiles_l[idx + 1])

        state = {}
        deferred = []

        def proj_half(wb, j, bk, half):
            with P.group(pe):
                for c in range(4 * half, 4 * half + 4):
                    P.op(pe, lambda e, c=c: e.matmul(bk.ap, lhsT=wb.ap[:, c, :], rhs=uT_sb[:, c, j * 512:(j + 1) * 512], start=(c == 0), stop=(c == 7)), [wb, uT_t[j]], [bk])

        def prep(hp, j):
            wq, wk, wv, wg = wts[hp]
            kts = kts_p[hp % 2]
            vt, vs = vts_p[hp % 2][j], vsb_p[hp % 2]
            bk = W["pbank"].next()
            for q in range(4):
                tb = 4 * j + q
                with P.group(pe):
                    for c in range(8):
                        P.op(pe, lambda e, c=c, q=q, tb=tb, bk=bk: e.matmul(bk.ap[:, q * 128:(q + 1) * 128], lhsT=uT_sb[:, c, tb * 128:(tb + 1) * 128], rhs=wv.ap[:, c, :], start=(c == 0), stop=(c == 7)), [wv, uT_t[j]], [bk])
                yield
            bk1 = W["pbank"].next()
            proj_half(wq, j, bk1, 0)
            P.op(dve, lambda e, bk=bk: e.tensor_copy(out=vs[:, 4 * j:4 * j + 4, :], in_=bk.ap.rearrange("p (q d) -> p q d", q=4)), [bk], [vt])
            yield
            proj_half(wq, j, bk1, 1)
            yield
            bk2 = W["pbank"].next()
            proj_half(wk, j, bk2, 0)
            qt = qp.next()
            P.op(dve, lambda e: e.tensor_copy(out=qt.ap, in_=bk1.ap), [bk1], [qt])
            yield
            proj_half(wk, j, bk2, 1)
            yield
            bk3 = W["pbank"].next()
            proj_half(wg, j, bk3, 0)
            P.op(dve, lambda e: e.tensor_copy(out=kts[j].ap, in_=bk2.ap), [bk2], [kts[j]])
            yield
            proj_half(wg, j, bk3, 1)
            yield
            sg = sgp.next()
            silu_from_bank(W, bk3, sg)
            state[(hp, j)] = dict(q=qt, sg=sg)

        def s1(it):
            _, hp, j, kb = it
            qt = state[(hp, j)]["q"]
            if kb == 4 * j + 3:
                a0, a1 = Ap.next(), Ap.next()
                P.op(pool, lambda e: e.memset(a0.ap, 0.0), [], [a0])
                P.op(pool, lambda e: e.memset(a1.ap, 0.0), [], [a1])
                state["A"] = [a0, a1]
            c0 = max(0, kb - 4 * j) * 128
            kj = kb // 4
            kT = kT_ts[hp % 2].ap
            kts = kts_p[hp % 2]
            with P.group(pe):
                for hh in range(2):
                    r = slice(hh * 64, hh * 64 + 64)
                    P.op(pe, lambda e, hh=hh, r=r: e.matmul(zps[:, hh * 512 + c0:hh * 512 + 512], lhsT=kT[r, kb * 128:(kb + 1) * 128], rhs=qt.ap[r, c0:512], start=True, stop=True), [kts[kj], qt], [zb[hh]])
            et = ep.next()
            P.op(act, lambda e: e.activation(out=et.ap[:, :, c0:512], in_=z3[:, :, c0:512], func=AF.Exp, scale=0.125), zb, [et])
            spt = spp.next()
            P.op(act, lambda e: e.activation(out=spt.ap[:, :, c0:512], in_=et.ap[:, :, c0:512], func=AF.Ln, bias=1.0), [et], [spt])
            if kb >= 4 * j:
                P.op(pool, lambda e: e.tensor_tensor(out=spt.ap[:, :, c0:c0 + 128], in0=spt.ap[:, :, c0:c0 + 128], in1=mlt3, op=ALU.mult), [spt, t_cbf], [spt])
            return dict(c0=c0, sp=spt, e=et, qt=qt, A=state["A"])

        def s2(it, ctx):
            _, hp, j, kb = it
            c0, spt, et = ctx["c0"], ctx["sp"], ctx["e"]
            n = (4 * j + 3) - kb
            acur, anxt = ctx["A"][n % 2], ctx["A"][(n + 1) % 2]
            with P.group(pe):
                for hh in range(2):
                    P.op(pe, lambda e, hh=hh: e.matmul(rps[:, hh * 512 + c0:hh * 512 + 512], lhsT=tri_bf, rhs=spt.ap[:, hh, c0:512], start=True, stop=(n == 0)), [spt, t_cbf], [rb[hh]])
                    if n > 0:
                        P.op(pe, lambda e, hh=hh: e.matmul(rps[:, hh * 512 + c0:hh * 512 + 512], lhsT=ones_bf, rhs=acur.ap[:, hh, c0:512], start=False, stop=True), [acur, t_cbf], [rb[hh]])
            if kb > 0:
                P.op(dve, lambda e: e.tensor_tensor(out=anxt.ap[:, :, c0:512], in0=acur.ap[:, :, c0:512], in1=spt.ap[:, :, c0:512], op=ALU.add), [acur, spt], [anxt])
            wx = wxp.next()
            P.op(act, lambda e: e.activation(out=wx.ap[:, :, c0:512], in_=r3[:, :, c0:512], func=AF.Exp, scale=-1.0), rb, [wx])
            wt = wp.next()
            P.op(dve, lambda e: e.tensor_tensor(out=wt.ap[:, :, c0:512], in0=et.ap[:, :, c0:512], in1=wx.ap[:, :, c0:512], op=ALU.mult), [et, wx], [wt])
            if kb >= 4 * j:
                P.op(pool, lambda e: e.tensor_tensor(out=wt.ap[:, :, c0:c0 + 128], in0=wt.ap[:, :, c0:c0 + 128], in1=mlt3, op=ALU.mult), [wt, t_cbf], [wt])
            ctx["w"] = wt

        def s3(it, ctx):
            _, hp, j, kb = it
            c0, wt = ctx["c0"], ctx["w"]
            with P.group(pe):
                for hh in range(2):
                    P.op(pe, lambda e, hh=hh: e.matmul(ob.ap[hh * 64:hh * 64 + 64, c0:512], lhsT=vsb_p[hp % 2][:, kb, hh * 64:hh * 64 + 64], rhs=wt.ap[:, hh, c0:512], start=(kb == 4 * j + 3), stop=(kb == 0), skip_group_check=True), [vts_p[hp % 2][kb // 4], wt], [ob])
            if kb == 0:
                o_sb, mb = head_norm_1(W, ob, blkB_bf)
                gcol = colv_sb[:, 8 + l * 4 + hp:8 + l * 4 + hp + 1]
                sgt = state[(hp, j)]["sg"]
                deferred.append([2, lambda: head_norm_2(W, o_sb, mb, gcol, sgt, 512 + hp * 128, j)])

        pend1 = None
        pend2 = None

        def flush():
            nonlocal pend1, pend2
            if pend2 is not None:
                s3(*pend2)
                pend2 = None
            if pend1 is not None:
                s2(*pend1)
                s3(*pend1)
                pend1 = None

        gen = [None, 0]

        def run_deferred(force=False):
            for d in list(deferred):
                d[0] -= 1
                if d[0] <= 0 or force:
                    deferred.remove(d)
                    d[1]()

        def advance(k):
            for _ in range(k):
                if gen[0] is None:
                    return
                try:
                    next(gen[0])
                except StopIteration:
                    gen[0] = None

        for ii, it in enumerate(items):
            if it[0] == "loadw":
                loadw_b(it[1])
            elif it[0] == "prep":
                advance(100)
                gen[0] = prep(*it[1:])
                n_left = 0
                for it2 in items[ii + 1:]:
                    if it2[0] == "att":
                        if (it2[1], it2[2]) == (it[1], it[2]):
                            break
                        n_left += 1
                gen[1] = 100 if n_left == 0 else -(-11 // n_left)
                if n_left == 0:
                    advance(100)
            else:
                if (it[1], it[2]) not in state:
                    advance(100)
                ctx = s1(it)
                advance(gen[1])
                if pend1 is not None:
                    s2(*pend1)
                if pend2 is not None:
                    s3(*pend2)
                pend2 = pend1
                pend1 = (it, ctx)
                run_deferred()
        flush()
        run_deferred(force=True)

        if stop == "B":
            return W["yT_t"]
        P.barrier()
        ov_reset()
        Wc = {}
        Wc["eps"] = ov_alloc([128, 1], F32, "eps")
        P.op(pool, lambda e: e.memset(Wc["eps"].ap, EPS), [], [Wc["eps"]])
        Wc["wst"] = ov_pool(3, [128, 8, 128], F32, "wst")
        Wc["wbf"] = ov_pool(8, [128, 8, 128], BF16, "wbf")
        Wc["yT_t"] = yT_tile

        def a_stream(si, heads):
            Ws = dict(Wc)
            bs = bank[4 * si:4 * si + 4]
            Ws["pbank"] = RPool([bs[0]])
            Ws["mbank"] = RPool([bs[0]])
            Ws["f32"] = ov_pool(4, [128, 512], F32, f"f32_{si}")
            Ws["bf"] = ov_pool(3, [128, 512], BF16, f"bf_{si}")
            mixb, oab, stbk = bs[1], bs[2], bs[3]
            vsb_t = ov_alloc([128, NB, 128], BF16, f"vsb_{si}")
            vsb = vsb_t.ap
            vts = [Tile(vsb[:, 4 * j:4 * j + 4, :], f"v{si}_{j}") for j in range(NT)]
            sgp = ov_pool(2, [128, 512], F32, f"sg_{si}")
            f32b = ov_pool(9, [128, 512], F32, f"fa_{si}")
            bfb = ov_pool(8, [128, 512], BF16, f"ba_{si}")
            khp = ov_pool(2, [128, 4, 128], BF16, f"kht_{si}")
            atp = ov_pool(2, [128, 4, 64], BF16, f"at_{si}")
            gdp = ov_pool(2, [128, 8], F32, f"gd_{si}")
            s32p = ov_pool(2, [128, 128], F32, f"s32_{si}")
            sbfp = ov_pool(2, [128, 128], BF16, f"sbf_{si}")
            atv = mixb.ap[:, 0:256]
            tpv = stbk.ap[:, 256:512].bitcast(BF16)
            for ha in heads:
                wq = load_w(Ws, w_in[l, :, ha * 128:ha * 128 + 128])
                wf = load_w(Ws, w_in[l, :, 512 + ha * 128:512 + ha * 128 + 128])
                wi = load_w(Ws, w_in[l, :, 1024 + ha * 128:1024 + ha * 128 + 128])
                wg = load_w(Ws, w_in[l, :, 1536 + ha * 128:1536 + ha * 128 + 128])
                omlc = lbw_sb[:, 24 + l * 4 + ha:24 + l * 4 + ha + 1]
                s32 = s32p.next()
                sbf = sbfp.next()
                P.op(pool, lambda e, s32=s32: e.memset(s32.ap, 0.0), [], [s32])
                P.op(pool, lambda e, sbf=sbf: e.memset(sbf.ap, 0.0), [], [sbf])
                yield
                for j in range(NT):
                    with P.group(pe):
                        bk = Ws["pbank"].next()
                        proj_fm(Ws, wq, j, bk)
                    yield
                    qa = f32b.next()
                    silu_from_bank(Ws, bk, qa)
                    yield
                    with P.group(pe):
                        bk = Ws["pbank"].next()
                        proj_fm(Ws, wf, j, bk)
                    yield
                    ka = f32b.next()
                    silu_from_bank(Ws, bk, ka, lnscale_col=omlc, sign=1.0)
                    lf = f32b.next()
                    P.op(act, lambda e, lf=lf, ka=ka: e.activation(out=lf.ap, in_=ka.ap, func=AF.Ln, scale=-1.0, bias=1.0), [ka], [lf])
                    yield
                    yield
                    with P.group(pe):
                        bk = Ws["pbank"].next()
                        proj_fm(Ws, wg, j, bk)
                    yield
                    sg = sgp.next()
                    silu_from_bank(Ws, bk, sg)
                    yield
                    bk = Ws["pbank"].next()
                    for q in range(4):
                        tb = 4 * j + q
                        with P.group(pe):
                            for c in range(8):
                                P.op(pe, lambda e, c=c, q=q, tb=tb, bk=bk, wi=wi: e.matmul(bk.ap[:, q * 128:(q + 1) * 128], lhsT=uT_sb[:, c, tb * 128:(tb + 1) * 128], rhs=wi.ap[:, c, :], start=(c == 0), stop=(c == 7)), [wi, uT_t[j]], [bk])
                    P.op(dve, lambda e, bk=bk, j=j: e.tensor_copy(out=vsb[:, 4 * j:4 * j + 4, :], in_=bk.ap.rearrange("p (q d) -> p q d", q=4)), [bk], [vts[j]])
                    yield
                    bt = f32b.next()
                    P.op(dve, lambda e, bt=bt, lf=lf: e.tensor_tensor_scan(out=bt.ap, data0=mreset_f, data1=lf.ap, initial=0.0, op0=ALU.mult, op1=ALU.add), [lf, t_cst], [bt])
                    yield
                    yield
                    b3 = bt.ap.rearrange("p (c s) -> p c s", s=64)
                    d1 = f32b.next()
                    d13 = d1.ap.rearrange("p (c s) -> p c s", s=64)
                    P.op(dve, lambda e, d13=d13, b3=b3: e.tensor_tensor(out=d13, in0=b3, in1=b3[:, :, 31:32].broadcast_to([128, 8, 64]), op=ALU.subtract), [bt], [d1])
                    yield
                    d2 = f32b.next()
                    d23 = d2.ap.rearrange("p (c s) -> p c s", s=64)
                    P.op(dve, lambda e, d23=d23, b3=b3: e.tensor_tensor(out=d23, in0=b3[:, :, 63:64].broadcast_to([128, 8, 64]), in1=b3, op=ALU.subtract), [bt], [d2])
                    yield
                    yield
                    ex = f32b.next()
                    ex2 = f32b.next()
                    ex3 = f32b.next()
                    P.op(act, lambda e, ex=ex, d1=d1: e.activation(out=ex.ap, in_=d1.ap, func=AF.Exp), [d1], [ex])
                    yield
                    P.op(act, lambda e, ex2=ex2, d1=d1: e.activation(out=ex2.ap, in_=d1.ap, func=AF.Exp, scale=-1.0), [d1], [ex2])
                    yield
                    P.op(act, lambda e, ex3=ex3, bt=bt: e.activation(out=ex3.ap, in_=bt.ap, func=AF.Exp), [bt], [ex3])
                    yield
                    P.op(act, lambda e, d2=d2: e.activation(out=d2.ap, in_=d2.ap, func=AF.Exp), [d2], [d2])
                    yield
                    gd = gdp.next()
                    P.op(act, lambda e, gd=gd, bt=bt: e.activation(out=gd.ap, in_=bt.ap[:, 63:512:64], func=AF.Exp), [bt], [gd])
                    yield
                    yield
                    qin, kin, qdec, khat = bfb.next(), bfb.next(), bfb.next(), bfb.next()
                    P.op(dve, lambda e, qin=qin, qa=qa, ex=ex: e.tensor_tensor(out=qin.ap, in0=qa.ap, in1=ex.ap, op=ALU.mult), [qa, ex], [qin])
                    yield
                    P.op(dve, lambda e, kin=kin, ka=ka, ex2=ex2: e.tensor_tensor(out=kin.ap, in0=ka.ap, in1=ex2.ap, op=ALU.mult), [ka, ex2], [kin])
                    yield
                    P.op(dve, lambda e, khat=khat, ka=ka, d2=d2: e.tensor_tensor(out=khat.ap, in0=ka.ap, in1=d2.ap, op=ALU.mult), [ka, d2], [khat])
                    yield
                    P.op(dve, lambda e, qdec=qdec, qa=qa, ex3=ex3: e.tensor_tensor(out=qdec.ap, in0=qa.ap, in1=ex3.ap, op=ALU.mult), [qa, ex3], [qdec])
                    yield
                    yield
                    with P.group(pe):
                        for q in range(4):
                            P.op(pe, lambda e, q=q, khat=khat: e.transpose(out=tpv[:, q * 128:(q + 1) * 128], in_=khat.ap[:, q * 128:(q + 1) * 128], identity=ident_bf), [khat, t_cbf], [stbk])
                    yield
                    kht = khp.next()
                    P.op(dve, lambda e, kht=kht: e.tensor_copy(out=kht.ap, in_=tpv.rearrange("p (q k) -> p q k", q=4)), [stbk], [kht])
                    yield
                    with P.group(pe):
                        for c in range(8):
                            hf = (c % 2) * 64
                            P.op(pe, lambda e, c=c, hf=hf, kin=kin, qin=qin: e.matmul(mixb.ap[hf:hf + 64, (c // 2) * 64:(c // 2) * 64 + 64], lhsT=kin.ap[:, c * 64:(c + 1) * 64], rhs=qin.ap[:, c * 64:(c + 1) * 64], start=True, stop=True), [kin, qin], [mixb])
                    yield
                    at = atp.next()
                    P.op(dve, lambda e, at=at: e.tensor_tensor(out=at.ap, in0=atv.rearrange("p (q t) -> p q t", q=4), in1=mle_f.unsqueeze(1).broadcast_to([128, 4, 64]), op=ALU.mult), [mixb, t_cst], [at])
                    yield
                    yield
                    mode[si] = "chunk"
                    for c in range(8):
                        tb = 4 * j + c // 2
                        rows = slice((c % 2) * 64, (c % 2) * 64 + 64)
                        with P.group(pe):
                            P.op(pe, lambda e, c=c, sbf=sbf, qdec=qdec: e.matmul(oab.ap[:, c * 64:(c + 1) * 64], lhsT=sbf.ap, rhs=qdec.ap[:, c * 64:(c + 1) * 64], start=True, stop=False), [sbf, qdec], [oab])
                            P.op(pe, lambda e, c=c, rows=rows, tb=tb, at=at: e.matmul(oab.ap[:, c * 64:(c + 1) * 64], lhsT=vsb[rows, tb, :], rhs=at.ap[rows, c // 2, :], start=False, stop=True), [vts[j], at], [oab])
                            P.op(pe, lambda e, c=c, rows=rows, tb=tb, kht=kht: e.matmul(stbk.ap[:, (c % 2) * 128:(c % 2) * 128 + 128], lhsT=kht.ap[rows, c // 2, :], rhs=vsb[rows, tb, :], start=True, stop=True), [kht, vts[j]], [stbk])
                        s32n = s32p.next()
                        sbfn = sbfp.next()
                        sv = stbk.ap[:, (c % 2) * 128:(c % 2) * 128 + 128]
                        P.op(dve, lambda e, sbfn=sbfn, s32=s32, gd=gd, c=c, sv=sv: e.scalar_tensor_tensor(out=sbfn.ap, in0=s32.ap, scalar=gd.ap[:, c:c + 1], in1=sv, op0=ALU.mult, op1=ALU.add), [s32, gd, stbk], [sbfn])
                        P.op(dve, lambda e, s32n=s32n, s32=s32, gd=gd, c=c, sv=sv: e.scalar_tensor_tensor(out=s32n.ap, in0=s32.ap, scalar=gd.ap[:, c:c + 1], in1=sv, op0=ALU.mult, op1=ALU.add), [s32, gd, stbk], [s32n])
                        s32, sbf = s32n, sbfn
                        yield
                    mode[si] = "op"
                    head_norm_store(Ws, oab, blkA_bf, colv_sb[:, l * 4 + ha:l * 4 + ha + 1], sg, ha * 128, j)
                    yield

        mode = {0: "op", 1: "op"}
        gens = {0: a_stream(0, [0, 1]), 1: a_stream(1, [2, 3])}

        def adv(si, k):
            for _ in range(k):
                if si not in gens:
                    return
                try:
                    next(gens[si])
                except StopIteration:
                    del gens[si]

        adv(0, 14)
        while gens:
            for si in (0, 1):
                if si not in gens:
                    continue
                other = 1 - si
                if mode[si] == "chunk":
                    adv(si, 1)
                elif other in gens and mode[other] == "chunk":
                    adv(si, 3)
                else:
                    adv(si, 1)
        W = Wc
        return W["yT_t"]

    def phase_D(l, yT_tile, last):
        ov_reset()
        W = {}
        W["eps"] = ov_alloc([128, 1], F32, "eps")
        P.op(pool, lambda e: e.memset(W["eps"].ap, EPS), [], [W["eps"]])
        W["wst"] = ov_pool(2, [128, 8, 128], F32, "wst")
        W["junk"] = ov_pool(1, [128, D], BF16, "junk")
        W["small"] = ov_pool(12, [128, 4], F32, "small")
        W["ub"] = ov_pool(2, [128, D], BF16, "ub")
        W["tp"] = RPool([bank[6], bank[7]])
        wo = ov_alloc([128, 8, D], BF16, "wo")
        wg = ov_alloc([128, 8, D], BF16, "wg")
        wpj = ov_alloc([128, 2, D], BF16, "wpj")
        for g in range(8):
            for (dst, src, rc) in ((wo, w_out, 8), (wg, w_pg, 8), (wpj, w_pp, 2)):
                ws = W["wst"].next()
                P.op(sp, lambda e, ws=ws, src=src, g=g, rc=rc: e.dma_start(out=ws.ap[:, 0:rc, :], in_=src[l, :, g * 128:(g + 1) * 128].rearrange("(c p) n -> p c n", p=128)), [], [ws])
                P.op(pool, lambda e, ws=ws, dst=dst, g=g, rc=rc: e.tensor_copy(out=dst.ap[:, :, g * 128:(g + 1) * 128], in_=ws.ap[:, 0:rc, :]), [ws], [dst])
        gn = ov_alloc([128, D], F32, "gn")
        pn = ov_alloc([128, D], F32, "pn")
        nmn = ov_alloc([128, D], F32, "nmn")
        P.op(sp, lambda e: e.dma_start(out=gn.ap, in_=vecs[2 + l]), [], [gn])
        P.op(sp, lambda e: e.dma_start(out=pn.ap, in_=vecs[4 + l]), [], [pn])
        P.op(sp, lambda e: e.dma_start(out=nmn.ap, in_=vecs[6] if last else vecs[l + 1]), [], [nmn])
        hbp = ov_pool(2, [128, D], F32, "hb")
        hmp = ov_pool(3, [128, D], F32, "hm")
        gtp = ov_pool(1, [128, D], F32, "gt")
        pep = ov_pool(3, [128, D], F32, "pe")
        otp = ov_pool(1 if last else 0, [128, D], F32, "ot")
        hnp = ov_pool(2, [128, D], F32, "hn")
        ybp = ov_pool(2, [128, 8, 512], BF16, "yb")
        pbp = ov_pool(2, [128, 256], F32, "pb")
        pbfp = ov_pool(2, [128, 256], BF16, "pbf")
        ptp = ov_pool(2, [128, 2, 128], BF16, "pT")
        ugp = ov_pool(2, [128, 8, 128], BF16, "ugT")
        h_src = x if l == 0 else hres
        hres_t = Tile(None, "hres")
        mixP = (psbig[0], bank[0], bank[1])
        gP = (psbig[1], bank[2], bank[3])
        pP = (psbig[2], bank[4], bank[5])
        ybs = {}

        def rstd_chain(src_ap, src_tiles, on_dve=False):
            junk = W["junk"].next()
            sm = W["small"].next()
            if on_dve:
                P.op(dve, lambda e: e.scalar_tensor_tensor(out=junk.ap, in0=src_ap, scalar=1.0, in1=src_ap, op0=ALU.mult, op1=ALU.mult, accum_out=sm.ap[:, 0:1]), src_tiles, [junk, sm])
            else:
                P.op(act, lambda e: e.activation(out=junk.ap, in_=src_ap, func=AF.Square, accum_out=sm.ap[:, 0:1]), src_tiles, [junk, sm])
            P.op(act, lambda e: e.activation(out=sm.ap[:, 1:2], in_=sm.ap[:, 0:1], func=AF.Ln, scale=1.0 / D, bias=W["eps"].ap), [sm, W["eps"]], [sm])
            P.op(act, lambda e: e.activation(out=sm.ap[:, 2:3], in_=sm.ap[:, 1:2], func=AF.Exp, scale=-0.5), [sm], [sm])
            return sm

        ctxs = {}

        def st_a(tb):
            j, q = tb // 4, tb % 4
            if q == 0:
                yb = ybp.next()
                P.op(sp, lambda e: e.dma_start(out=yb.ap, in_=yT[:, j * 512:(j + 1) * 512].rearrange("(c p) t -> p c t", p=128)), [yT_tile], [yb])
                ybs[j] = yb
            yb = ybs[j]
            hb = hbp.next()
            P.op(sp, lambda e: e.dma_start(out=hb.ap, in_=h_src[tb * 128:(tb + 1) * 128, :]), [hres_t] if l > 0 else [], [hb])
            pb = pbp.next()
            P.op(sp, lambda e: e.dma_start(out=pb.ap, in_=p_in[l, tb * 128:(tb + 1) * 128, :]), [], [pb])
            with P.group(pe):
                for half in range(2):
                    bk = mixP[1 + half]
                    for c in range(8):
                        P.op(pe, lambda e, c=c, half=half, bk=bk: e.matmul(bk.ap, lhsT=yb.ap[:, c, q * 128:(q + 1) * 128], rhs=wo.ap[:, c, half * 512:(half + 1) * 512], start=(c == 0), stop=(c == 7)), [yb, wo], [bk])
            hm = hmp.next()
            P.op(dve, lambda e: e.tensor_tensor(out=hm.ap, in0=mixP[0][:, :], in1=hb.ap, op=ALU.add), [mixP[1], mixP[2], hb], [hm])
            pbf = pbfp.next()
            P.op(pool, lambda e: e.tensor_copy(out=pbf.ap, in_=pb.ap), [pb], [pbf])
            ctxs[tb] = dict(hm=hm, pbf=pbf)

        def st_b1(tb):
            cx = ctxs[tb]
            hm = cx["hm"]
            sm = rstd_chain(hm.ap, [hm], on_dve=True)
            ug = W["ub"].next()
            P.op(dve, lambda e: e.scalar_tensor_tensor(out=ug.ap, in0=hm.ap, scalar=sm.ap[:, 2:3], in1=gn.ap, op0=ALU.mult, op1=ALU.mult), [hm, sm, gn], [ug])
            cx["ug"] = ug

        def st_b2(tb):
            cx = ctxs[tb]
            ug = cx["ug"]
            tp = W["tp"].next()
            tpv = tp.ap.bitcast(BF16)
            with P.group(pe):
                for c in range(8):
                    P.op(pe, lambda e, c=c: e.transpose(out=tpv[:, c * 128:(c + 1) * 128], in_=ug.ap[:, c * 128:(c + 1) * 128], identity=ident_bf), [ug, t_cbf], [tp])
            ugT = ugp.next()
            P.op(act, lambda e: e.activation(out=ugT.ap, in_=tpv.rearrange("p (c t) -> p c t", c=8), func=AF.Copy), [tp], [ugT])
            cx["ugT"] = ugT

        def st_b3(tb):
            cx = ctxs[tb]
            pbf = cx["pbf"]
            tp2 = W["tp"].next()
            tpv2 = tp2.ap.bitcast(BF16)
            with P.group(pe):
                for c in range(2):
                    P.op(pe, lambda e, c=c: e.transpose(out=tpv2[:, c * 128:(c + 1) * 128], in_=pbf.ap[:, c * 128:(c + 1) * 128], identity=ident_bf), [pbf, t_cbf], [tp2])
            pT = ptp.next()
            P.op(act, lambda e: e.activation(out=pT.ap, in_=tpv2[:, 0:256].rearrange("p (c t) -> p c t", c=2), func=AF.Copy), [tp2], [pT])
            cx["pT"] = pT

        def st_c(tb):
            cx = ctxs[tb]
            hm, ugT, pT = cx["hm"], cx["ugT"], cx["pT"]
            with P.group(pe):
                for half in range(2):
                    bk = pP[1 + half]
                    for c in range(2):
                        P.op(pe, lambda e, c=c, half=half, bk=bk: e.matmul(bk.ap, lhsT=pT.ap[:, c, :], rhs=wpj.ap[:, c, half * 512:(half + 1) * 512], start=(c == 0), stop=(c == 1)), [pT, wpj], [bk])
                for half in range(2):
                    bk = gP[1 + half]
                    for c in range(8):
                        P.op(pe, lambda e, c=c, half=half, bk=bk: e.matmul(bk.ap, lhsT=ugT.ap[:, c, :], rhs=wg.ap[:, c, half * 512:(half + 1) * 512], start=(c == 0), stop=(c == 7)), [ugT, wg], [bk])
            sm2 = rstd_chain(pP[0][:, :], [pP[1], pP[2]])
            pet = pep.next()
            P.op(dve, lambda e: e.scalar_tensor_tensor(out=pet.ap, in0=pP[0][:, :], scalar=sm2.ap[:, 2:3], in1=pn.ap, op0=ALU.mult, op1=ALU.mult), [pP[1], pP[2], sm2, pn], [pet])
            gt = gtp.next()
            P.op(act, lambda e: e.activation(out=gt.ap, in_=gP[0][:, :], func=AF.Exp, scale=-1.0), [gP[1], gP[2]], [gt])
            P.op(act, lambda e: e.activation(out=gt.ap, in_=gt.ap, func=AF.Ln, bias=1.0), [gt], [gt])
            P.op(act, lambda e: e.activation(out=gt.ap, in_=gt.ap, func=AF.Exp, scale=-1.0), [gt], [gt])
            P.op(pool, lambda e: e.tensor_tensor(out=gt.ap, in0=gt.ap, in1=pet.ap, op=ALU.mult), [pet, gt], [gt])
            hn = hnp.next()
            P.op(pool, lambda e: e.tensor_tensor(out=hn.ap, in0=gt.ap, in1=hm.ap, op=ALU.add), [gt, hm], [hn])
            if not last:
                P.op(sp, lambda e: e.dma_start(out=hres[tb * 128:(tb + 1) * 128, :], in_=hn.ap), [hn], [hres_t])
                if debug and l == 0:
                    P.op(sp, lambda e: e.dma_start(out=dbg["h1"][tb * 128:(tb + 1) * 128, :], in_=hn.ap), [hn], [])
            cx["hn"] = hn

        def st_d1(tb):
            cx = ctxs[tb]
            hn = cx["hn"]
            if not last:
                cx["ub"] = emit_u1(hn, nmn, W, on_dve=True)
            else:
                sm3 = rstd_chain(hn.ap, [hn], on_dve=True)
                ot = otp.next()
                P.op(dve, lambda e: e.scalar_tensor_tensor(out=ot.ap, in0=hn.ap, scalar=sm3.ap[:, 2:3], in1=nmn.ap, op0=ALU.mult, op1=ALU.mult), [hn, sm3, nmn], [ot])
                P.op(sp, lambda e: e.dma_start(out=out[tb * 128:(tb + 1) * 128, :], in_=ot.ap), [ot], [])

        def st_d2(tb):
            if not last:
                emit_u2(ctxs[tb]["ub"], tb, W)
            del ctxs[tb]

        ok = lambda t: 0 <= t < NB
        for i in range(NB + 3):
            if ok(i - 1):
                st_b1(i - 1)
            if ok(i - 3):
                st_d1(i - 3)
            if ok(i):
                st_a(i)
            if ok(i - 2):
                st_c(i - 2)
            if ok(i - 1):
                st_b2(i - 1)
            if ok(i - 3):
                st_d2(i - 3)
            if ok(i - 1):
                st_b3(i - 1)

    phase_A0()
    for l in range(n_layers):
        P.barrier()
        if debug and l == 0:
            for c in range(8):
                P.op(sp, lambda e, c=c: e.dma_start(out=dbg["uT0"][:, c, :], in_=uT_sb[:, c, :]), uT_t, [])
        if stop == "A0":
            break
        yt = phase_BC(l, stop)
        P.barrier()
        if debug and l == 0:
            for c in range(8):
                P.op(sp, lambda e, c=c: e.dma_start(out=dbg["yT0"][c * 128:(c + 1) * 128, :], in_=yT[c * 128:(c + 1) * 128, :]), [yt], [])
        if stop in ("B", "BC"):
            break
        phase_D(l, yt, last=(l == n_layers - 1))
    P.emit(nc, st)
    st.close()
    return nc


def make_consts():
    c = np.zeros((128, C_END), np.float32)
    i = np.arange(128)
    c[:, C_ID:C_ID + 128] = np.eye(128)
    c[:, C_TRI:C_TRI + 128] = (i[:, None] >= i[None, :])
    c[:, C_ONE:C_ONE + 128] = 1.0
    c[:, C_BA:C_BA + 128] = 1.0 / 128
    c[:, C_BB:C_BB + 128] = ((i[:, None] // 64) == (i[None, :] // 64)) / 64.0
    c[:, C_LT:C_LT + 128] = (i[:, None] < i[None, :])
    c[:, C_LE:C_LE + 64] = ((i[:, None] % 64) <= np.arange(64)[None, :])
    m = np.ones((128, 512), np.float32)
    m[:, ::64] = 0.0
    c[:, C_MR:C_MR + 512] = m
    return c


_NC_CACHE = {}


def kernel(x, p, norm_mix, w_in, a_out_norm, b_out_norm, w_out, lb_logits,
           ple_gate_norm, w_ple_gate, w_ple_proj, ple_post_norm, final_norm):
    f = lambda a: np.ascontiguousarray(np.asarray(a, dtype=np.float32))
    x, p = f(x), f(p)
    B = x.shape[0]
    rep = lambda v: np.broadcast_to(f(v)[None, :], (128, D))
    vecs = np.ascontiguousarray(np.stack([rep(norm_mix[0]), rep(norm_mix[1]), rep(ple_gate_norm[0]), rep(ple_gate_norm[1]),
                                          rep(ple_post_norm[0]), rep(ple_post_norm[1]), rep(final_norm)], axis=0))
    colv = np.zeros((128, 24), np.float32)
    aon, bon, lbl = f(a_out_norm), f(b_out_norm), f(lb_logits)
    for l in range(DEPTH):
        for h in range(4):
            colv[:, l * 4 + h] = aon[l, h * 128:(h + 1) * 128]
            colv[:, 8 + l * 4 + h] = bon[l, h * 128:(h + 1) * 128]
            colv[:, 16 + l * 4 + h] = lbl[l, h * 128:(h + 1) * 128]
    cst = make_consts()
    if "nc" not in _NC_CACHE:
        _NC_CACHE["nc"] = build_program()
    nc = _NC_CACHE["nc"]
    shared = dict(w_in=f(w_in), w_out=f(w_out), w_pg=f(w_ple_gate), w_pp=f(w_ple_proj), vecs=vecs, colv=colv, cst=cst)
    in_maps = []
    for b in range(B):
        d = dict(shared)
        d["x"] = np.ascontiguousarray(x[b])
        d["p"] = np.ascontiguousarray(p[:, b])
        in_maps.append(d)
    res = run_bass_kernel_spmd(nc, in_maps, core_ids=list(range(B)))
    return np.stack([np.asarray(r["out"], dtype=np.float32) for r in res.results], axis=0)
```

```python
from contextlib import ExitStack
import numpy as np
import concourse.bass as bass
import concourse.mybir as mybir
from concourse.bass_utils import run_bass_kernel_spmd

F32 = mybir.dt.float32
BF16 = mybir.dt.bfloat16
AF = mybir.ActivationFunctionType
ALU = mybir.AluOpType

S = 4096
D = 1024
DEPTH = 2
NT = 8
NB = 32
EPS = 1e-6
DMA_RING = 8

C_ID, C_TRI, C_ONE, C_BA, C_BB, C_LT, C_LE, C_MR, C_END = 0, 128, 256, 384, 512, 640, 768, 832, 1344


class Tile:
    __slots__ = ("ap", "lw", "rd", "name")

    def __init__(self, ap, name=""):
        self.ap = ap
        self.lw = None
        self.rd = {}
        self.name = name


class Eng:
    def __init__(self, name, is_dma=False):
        self.name = name
        self.is_dma = is_dma
        self.instrs = []
        self.pending = set()
        self.gfirst = None


class Instr:
    __slots__ = ("fn", "deps", "signal", "val")

    def __init__(self, fn, deps):
        self.fn = fn
        self.deps = deps
        self.signal = False
        self.val = 0


class Plan:
    def __init__(self):
        self.pe = Eng("pe")
        self.act = Eng("act")
        self.dve = Eng("dve")
        self.pool = Eng("pool")
        self.sp = Eng("sp", is_dma=True)
        self.engs = [self.sp, self.pe, self.act, self.dve, self.pool]

    def op(self, eng, fn, reads=(), writes=()):
        deps = set()
        for t in reads:
            if t.lw is not None:
                deps.add(t.lw)
        for t in writes:
            if t.lw is not None:
                deps.add(t.lw)
            for e, s in t.rd.items():
                if e.is_dma:
                    for ss in s:
                        deps.add((e, ss))
                else:
                    deps.add((e, s))
        if eng is self.pe:
            deps = {d for d in deps if d[0] is not self.pe}
        if eng.pending:
            deps |= eng.pending
            eng.pending = set()
        seq = len(eng.instrs)
        if eng.gfirst is not None:
            if eng.gfirst < 0:
                eng.gfirst = seq
            else:
                eng.instrs[eng.gfirst].deps |= deps
                deps = set()
        eng.instrs.append(Instr(fn, deps))
        for t in reads:
            if eng.is_dma:
                t.rd.setdefault(eng, []).append(seq)
            else:
                t.rd[eng] = seq
        for t in writes:
            t.lw = (eng, seq)
            t.rd = {}
        return seq

    def group(self, eng):
        plan = self

        class _G:
            def __enter__(self_g):
                eng.gfirst = -1

            def __exit__(self_g, *a):
                eng.gfirst = None
        return _G()

    def barrier(self):
        deps = set()
        for e in self.engs:
            n = len(e.instrs)
            if n == 0:
                continue
            if e.is_dma:
                for s in range(max(0, n - DMA_RING), n):
                    deps.add((e, s))
            else:
                deps.add((e, n - 1))
        for e in self.engs:
            e.pending |= {d for d in deps if d[0] is not e or e.is_dma}

    def emit(self, nc, stack):
        for e in self.engs:
            for ins in e.instrs:
                for (de, ds) in ins.deps:
                    if not de.is_dma:
                        de.instrs[ds].signal = True
        for e in self.engs:
            if e.is_dma:
                continue
            c = 0
            for ins in e.instrs:
                if ins.signal:
                    c += 1
                    ins.val = c
        sems = {}
        for e in self.engs:
            if e.is_dma:
                sems[e] = [stack.enter_context(nc.semaphore(f"s_{e.name}{i}")) for i in range(DMA_RING)]
            else:
                sems[e] = stack.enter_context(nc.semaphore(f"s_{e.name}"))
        block = stack.enter_context(nc.Block())

        def resolve(dep):
            de, ds = dep
            if de.is_dma:
                return (sems[de][ds % DMA_RING], 16 * (ds // DMA_RING + 1))
            return (sems[de], de.instrs[ds].val)

        def run(e, beng):
            known = {}
            n = len(e.instrs)
            for seq, ins in enumerate(e.instrs):
                waits = {}
                if e.is_dma and seq >= DMA_RING:
                    s, v = resolve((e, seq - DMA_RING))
                    waits[s] = max(waits.get(s, 0), v)
                for dep in ins.deps:
                    s, v = resolve(dep)
                    if v > waits.get(s, 0):
                        waits[s] = v
                for s, v in waits.items():
                    if known.get(s, 0) >= v:
                        continue
                    beng.wait_ge(s, v)
                    known[s] = v
                bi = ins.fn(beng)
                if e.is_dma:
                    bi.then_inc(sems[e][seq % DMA_RING], 16)
                elif ins.signal:
                    bi.then_inc(sems[e], 1)
            if e.is_dma:
                for seq in range(max(0, n - DMA_RING), n):
                    s, v = resolve((e, seq))
                    if known.get(s, 0) < v:
                        beng.wait_ge(s, v)
                        known[s] = v

        @block.sync
        def _(sync):
            run(self.sp, sync)

        @block.tensor
        def _(tensor):
            run(self.pe, tensor)

        @block.scalar
        def _(scalar):
            run(self.act, scalar)

        @block.vector
        def _(vector):
            run(self.dve, vector)

        @block.gpsimd
        def _(gpsimd):
            run(self.pool, gpsimd)


class RPool:
    def __init__(self, tiles):
        self.tiles = tiles
        self.i = 0

    def next(self):
        t = self.tiles[self.i % len(self.tiles)]
        self.i += 1
        return t


def build_program(debug=False, n_layers=DEPTH, stop=None):
    nc = bass.Bass("TRN2", target_bir_lowering=False)
    dram = lambda name, shape, dt=F32, kind="ExternalInput": nc.dram_tensor(name, shape, dt, kind=kind).ap()
    x = dram("x", [S, D])
    p_in = dram("p", [DEPTH, S, 256])
    w_in = dram("w_in", [DEPTH, D, 4096])
    w_out = dram("w_out", [DEPTH, D, D])
    w_pg = dram("w_pg", [DEPTH, D, D])
    w_pp = dram("w_pp", [DEPTH, 256, D])
    vecs = dram("vecs", [7, 128, D])
    colv = dram("colv", [128, 24])
    cst = dram("cst", [128, C_END])
    out = dram("out", [S, D], F32, "ExternalOutput")
    hres = dram("hres", [S, D], F32, "Internal")
    yT = dram("yT", [D, S], BF16, "Internal")

    P = Plan()
    pe, act, dve, pool, sp = P.pe, P.act, P.dve, P.pool, P.sp
    st = ExitStack()
    sbt = lambda name, shape, dt=F32: st.enter_context(nc.sbuf_tensor(name, shape, dt))

    uT_sb = sbt("uT", [128, 8, S], BF16)
    uT_t = [Tile(uT_sb[:, :, j * 512:(j + 1) * 512], f"uT{j}") for j in range(NT)]
    cst_sb = sbt("cst_sb", [128, C_END])
    cbf_sb = sbt("cbf_sb", [128, C_LE], BF16)
    colv_sb = sbt("colv_sb", [128, 24])
    lbw_sb = sbt("lbw_sb", [128, 32])
    t_cst = Tile(cst_sb[:])
    t_cbf = Tile(cbf_sb[:])
    t_colv = Tile(colv_sb[:])
    t_lbw = Tile(lbw_sb[:])
    OVW = 34800
    ov = sbt("ov", [128, OVW])
    ovp = [0]

    ovmax = [0]

    def ov_reset():
        ovmax[0] = max(ovmax[0], ovp[0])
        ovp[0] = 0

    def ov_alloc(shape, dt=F32, name=""):
        n = int(np.prod(shape[1:]))
        words = n if dt == F32 else (n + 1) // 2
        a = ov[:, ovp[0]:ovp[0] + words]
        ovp[0] += words
        assert ovp[0] <= OVW, ("overlay overflow", ovp[0])
        if dt != F32:
            a = a.bitcast(dt)
        if len(shape) == 3:
            a = a.rearrange("p (a b) -> p a b", a=shape[1])
        return Tile(a, name)

    def ov_pool(n, shape, dt=F32, name=""):
        return RPool([ov_alloc(shape, dt, f"{name}{i}") for i in range(n)])

    psbig = [st.enter_context(nc.psum_tensor(f"ps{i}", [128, 1024], F32)) for i in range(4)]
    bank = [Tile(psbig[i // 2][:, (i % 2) * 512:(i % 2) * 512 + 512], f"bank{i}") for i in range(8)]

    ident_bf = cbf_sb[:, C_ID:C_ID + 128]
    tri_bf = cbf_sb[:, C_TRI:C_TRI + 128]
    ones_bf = cbf_sb[:, C_ONE:C_ONE + 128]
    blkA_bf = cbf_sb[:, C_BA:C_BA + 128]
    blkB_bf = cbf_sb[:, C_BB:C_BB + 128]
    mlt_bf = cbf_sb[:, C_LT:C_LT + 128]
    mle_f = cst_sb[:, C_LE:C_LE + 64]
    mreset_f = cst_sb[:, C_MR:C_MR + 512]

    P.op(sp, lambda e: e.dma_start(out=cst_sb[:], in_=cst), [], [t_cst])
    P.op(sp, lambda e: e.dma_start(out=colv_sb[:], in_=colv), [], [t_colv])
    P.op(dve, lambda e: e.tensor_copy(out=cbf_sb[:], in_=cst_sb[:, 0:C_LE]), [t_cst], [t_cbf])
    L0 = colv_sb[:, 16:20]
    L1 = colv_sb[:, 20:24]
    w = lambda a, b: lbw_sb[:, a:b]
    P.op(dve, lambda e: e.tensor_tensor(out=w(0, 4), in0=L0, in1=L1, op=ALU.max), [t_colv], [t_lbw])
    P.op(dve, lambda e: e.tensor_tensor(out=w(4, 8), in0=L0, in1=w(0, 4), op=ALU.subtract), [t_colv, t_lbw], [t_lbw])
    P.op(dve, lambda e: e.tensor_tensor(out=w(8, 12), in0=L1, in1=w(0, 4), op=ALU.subtract), [t_colv, t_lbw], [t_lbw])
    P.op(act, lambda e: e.activation(out=w(4, 12), in_=w(4, 12), func=AF.Exp), [t_lbw], [t_lbw])
    P.op(dve, lambda e: e.tensor_tensor(out=w(0, 4), in0=w(4, 8), in1=w(8, 12), op=ALU.add), [t_lbw], [t_lbw])
    P.op(dve, lambda e: e.reciprocal(out=w(0, 4), in_=w(0, 4)), [t_lbw], [t_lbw])
    P.op(dve, lambda e: e.tensor_tensor(out=w(4, 8), in0=w(4, 8), in1=w(0, 4), op=ALU.mult), [t_lbw], [t_lbw])
    P.op(dve, lambda e: e.tensor_tensor(out=w(8, 12), in0=w(8, 12), in1=w(0, 4), op=ALU.mult), [t_lbw], [t_lbw])
    P.op(dve, lambda e: e.tensor_tensor(out=w(12, 16), in0=w(4, 8), in1=w(8, 12), op=ALU.add), [t_lbw], [t_lbw])
    P.op(dve, lambda e: e.tensor_tensor(out=w(0, 4), in0=w(4, 8), in1=w(4, 8), op=ALU.subtract), [t_lbw], [t_lbw])
    P.op(dve, lambda e: e.tensor_tensor(out=w(12, 16), in0=w(12, 16), in1=w(4, 8), op=ALU.subtract), [t_lbw], [t_lbw])
    P.op(dve, lambda e: e.tensor_scalar(out=w(16, 20), in0=w(0, 4), scalar1=-1.0, scalar2=1.0, op0=ALU.mult, op1=ALU.add), [t_lbw], [t_lbw])
    P.op(dve, lambda e: e.tensor_scalar(out=w(20, 24), in0=w(12, 16), scalar1=-1.0, scalar2=1.0, op0=ALU.mult, op1=ALU.add), [t_lbw], [t_lbw])

    P.op(act, lambda e: e.activation(out=w(24, 32), in_=w(16, 24), func=AF.Ln), [t_lbw], [t_lbw])

    dbg = {}
    if debug:
        dbg["uT0"] = dram("d_uT0", [128, 8, S], BF16, "ExternalOutput")
        dbg["yT0"] = dram("d_yT0", [D, S], BF16, "ExternalOutput")
        dbg["h1"] = dram("d_h1", [S, D], F32, "ExternalOutput")

    def emit_u1(hb, nm, W, on_dve=False):
        junk = W["junk"].next()
        ssq = W["small"].next()
        if on_dve:
            P.op(dve, lambda e: e.scalar_tensor_tensor(out=junk.ap, in0=hb.ap, scalar=1.0, in1=hb.ap, op0=ALU.mult, op1=ALU.mult, accum_out=ssq.ap[:, 0:1]), [hb], [junk, ssq])
        else:
            P.op(act, lambda e: e.activation(out=junk.ap, in_=hb.ap, func=AF.Square, accum_out=ssq.ap[:, 0:1]), [hb], [junk, ssq])
        P.op(act, lambda e: e.activation(out=ssq.ap[:, 1:2], in_=ssq.ap[:, 0:1], func=AF.Ln, scale=1.0 / D, bias=W["eps"].ap), [ssq, W["eps"]], [ssq])
        P.op(act, lambda e: e.activation(out=ssq.ap[:, 2:3], in_=ssq.ap[:, 1:2], func=AF.Exp, scale=-0.5), [ssq], [ssq])
        ub = W["ub"].next()
        P.op(dve, lambda e: e.scalar_tensor_tensor(out=ub.ap, in0=hb.ap, scalar=ssq.ap[:, 2:3], in1=nm.ap, op0=ALU.mult, op1=ALU.mult), [hb, ssq, nm], [ub])
        return ub

    def emit_u2(ub, tb, W):
        tp = W["tp"].next()
        tpb = tp.ap.bitcast(BF16)
        with P.group(pe):
            for c in range(8):
                P.op(pe, lambda e, c=c: e.transpose(out=tpb[:, c * 128:(c + 1) * 128], in_=ub.ap[:, c * 128:(c + 1) * 128], identity=ident_bf), [ub, t_cbf], [tp])
        ut = uT_t[tb // 4]
        P.op(act, lambda e: e.activation(out=uT_sb[:, :, tb * 128:(tb + 1) * 128], in_=tpb.rearrange("p (c t) -> p c t", c=8), func=AF.Copy), [tp], [ut])

    def emit_u(hb, tb, nm, W, on_dve=False):
        emit_u2(emit_u1(hb, nm, W, on_dve), tb, W)

    def load_w(W, src_ap, rows_c=8):
        ws = W["wst"].next()
        wb = W["wbf"].next()
        P.op(sp, lambda e: e.dma_start(out=ws.ap[:, 0:rows_c, :], in_=src_ap.rearrange("(c p) n -> p c n", p=128)), [], [ws])
        P.op(pool, lambda e: e.tensor_copy(out=wb.ap[:, 0:rows_c, :], in_=ws.ap[:, 0:rows_c, :]), [ws], [wb])
        return wb

    def proj_fm(W, wb, j, bk):
        for c in range(8):
            P.op(pe, lambda e, c=c: e.matmul(bk.ap, lhsT=wb.ap[:, c, :], rhs=uT_sb[:, c, j * 512:(j + 1) * 512], start=(c == 0), stop=(c == 7)), [wb, uT_t[j]], [bk])

    def proj_tok(W, wb, vt, vsb, j):
        bk = W["pbank"].next()
        for q in range(4):
            tb = 4 * j + q
            for c in range(8):
                P.op(pe, lambda e, c=c, q=q, tb=tb: e.matmul(bk.ap[:, q * 128:(q + 1) * 128], lhsT=uT_sb[:, c, tb * 128:(tb + 1) * 128], rhs=wb.ap[:, c, :], start=(c == 0), stop=(c == 7)), [wb, uT_t[j]], [bk])
        P.op(dve, lambda e: e.tensor_copy(out=vsb[:, 4 * j:4 * j + 4, :], in_=bk.ap.rearrange("p (q d) -> p q d", q=4)), [bk], [vt])

    def silu_from_bank(W, bk, outt, lnscale_col=None, sign=-1.0):
        en = outt if lnscale_col is not None else W["f32"].next()
        P.op(act, lambda e: e.activation(out=en.ap, in_=bk.ap, func=AF.Exp, scale=sign), [bk], [en])
        P.op(act, lambda e: e.activation(out=en.ap, in_=en.ap, func=AF.Ln, bias=1.0), [en], [en])
        if lnscale_col is None:
            P.op(act, lambda e: e.activation(out=en.ap, in_=en.ap, func=AF.Exp, scale=-1.0), [en], [en])
            P.op(dve, lambda e: e.tensor_tensor(out=outt.ap, in0=bk.ap, in1=en.ap, op=ALU.mult), [bk, en], [outt])
        else:
            P.op(act, lambda e: e.activation(out=en.ap, in_=en.ap, func=AF.Exp, scale=-1.0, bias=lnscale_col), [en, t_lbw], [en])

    def head_norm_1(W, obank, blk_bf):
        o_sb = W["f32"].next()
        P.op(dve, lambda e: e.tensor_copy(out=o_sb.ap, in_=obank.ap), [obank], [o_sb])
        osq = W["bf"].next()
        P.op(pool, lambda e: e.tensor_tensor(out=osq.ap, in0=o_sb.ap, in1=o_sb.ap, op=ALU.mult), [o_sb], [osq])
        mb = W["mbank"].next()
        P.op(pe, lambda e: e.matmul(mb.ap, lhsT=blk_bf, rhs=osq.ap, start=True, stop=True), [osq, t_cbf], [mb])
        return o_sb, mb

    def head_norm_2(W, o_sb, mb, gcol, sg, row0, j):
        rs = W["f32"].next()
        P.op(act, lambda e: e.activation(out=rs.ap, in_=mb.ap, func=AF.Ln, bias=W["eps"].ap), [mb, W["eps"]], [rs])
        P.op(act, lambda e: e.activation(out=rs.ap, in_=rs.ap, func=AF.Exp, scale=-0.5), [rs], [rs])
        P.op(dve, lambda e: e.scalar_tensor_tensor(out=o_sb.ap, in0=o_sb.ap, scalar=gcol, in1=rs.ap, op0=ALU.mult, op1=ALU.mult), [o_sb, rs, t_colv], [o_sb])
        yb = W["bf"].next()
        P.op(dve, lambda e: e.tensor_tensor(out=yb.ap, in0=o_sb.ap, in1=sg.ap, op=ALU.mult), [o_sb, sg], [yb])
        P.op(sp, lambda e: e.dma_start(out=yT[row0:row0 + 128, j * 512:(j + 1) * 512], in_=yb.ap), [yb], [W["yT_t"]])

    def head_norm_store(W, obank, blk_bf, gcol, sg, row0, j):
        o_sb, mb = head_norm_1(W, obank, blk_bf)
        head_norm_2(W, o_sb, mb, gcol, sg, row0, j)

    def phase_A0():
        ov_reset()
        W = {}
        W["junk"] = ov_pool(1, [128, D], F32, "junk")
        W["small"] = ov_pool(4, [128, 4], F32, "small")
        W["ub"] = ov_pool(2, [128, D], BF16, "ub")
        W["tp"] = RPool([bank[6], bank[7]])
        W["eps"] = ov_alloc([128, 1], F32, "eps")
        P.op(pool, lambda e: e.memset(W["eps"].ap, EPS), [], [W["eps"]])
        nm = ov_alloc([128, D], F32, "nm")
        P.op(sp, lambda e: e.dma_start(out=nm.ap, in_=vecs[0]), [], [nm])
        hbp = ov_pool(3, [128, D], F32, "hb")
        prev = None
        for tb in range(NB):
            hb = hbp.next()
            P.op(sp, lambda e, hb=hb, tb=tb: e.dma_start(out=hb.ap, in_=x[tb * 128:(tb + 1) * 128, :]), [], [hb])
            ub = emit_u1(hb, nm, W, on_dve=(tb % 2 == 1))
            if prev is not None:
                emit_u2(*prev, W)
            prev = (ub, tb)
        emit_u2(*prev, W)

    def phase_BC(l, stop=None):
        ov_reset()
        yT_tile = Tile(None, "yT")

        def common_alloc():
            ov_reset()
            W = {}
            W["eps"] = ov_alloc([128, 1], F32, "eps")
            P.op(pool, lambda e: e.memset(W["eps"].ap, EPS), [], [W["eps"]])
            W["wst"] = ov_pool(3, [128, 8, 128], F32, "wst")
            W["wbf"] = ov_pool(8, [128, 8, 128], BF16, "wbf")
            W["f32"] = ov_pool(6, [128, 512], F32, "f32")
            W["bf"] = ov_pool(4, [128, 512], BF16, "bf")
            W["yT_t"] = yT_tile
            W["pbank"] = RPool([bank[4], bank[5]])
            W["mbank"] = RPool([bank[7]])
            vsb_t = ov_alloc([128, NB, 128], BF16, "vsb")
            vsb = vsb_t.ap
            vts = [Tile(vsb[:, 4 * j:4 * j + 4, :], f"v{j}") for j in range(NT)]
            sgp = ov_pool(2, [128, 512], F32, "sg")
            return W, vsb, vts, sgp

        W, vsb, vts, sgp = common_alloc()

        kT_ts = [ov_alloc([128, S], BF16, "kT0"), ov_alloc([128, S], BF16, "kT1")]
        kts_p = [[Tile(kT_ts[par].ap[:, j * 512:(j + 1) * 512]) for j in range(NT)] for par in range(2)]
        vsb2_t = ov_alloc([128, NB, 128], BF16, "vsb2")
        vsb_p = [vsb, vsb2_t.ap]
        vts_p = [vts, [Tile(vsb2_t.ap[:, 4 * j:4 * j + 4, :], f"v2_{j}") for j in range(NT)]]
        qp = ov_pool(2, [128, 512], BF16, "q")
        ep = ov_pool(3, [128, 2, 512], F32, "e")
        spp = ov_pool(3, [128, 2, 512], BF16, "sp")
        wxp = ov_pool(2, [128, 2, 512], F32, "wx")
        wp = ov_pool(3, [128, 2, 512], BF16, "w")
        Ap = ov_pool(4, [128, 2, 512], BF16, "A")
        zps, rps = psbig[0], psbig[1]
        z3 = zps[:, :].rearrange("p (h c) -> p h c", h=2)
        r3 = rps[:, :].rearrange("p (h c) -> p h c", h=2)
        zb = [bank[0], bank[1]]
        rb = [bank[2], bank[3]]
        ob = bank[6]
        mlt3 = mlt_bf.unsqueeze(1).broadcast_to([128, 2, 128])

        items = []
        wts = {}

        def loadw_b(hp):
            wts[hp] = tuple(load_w(W, w_in[l, :, base + hp * 128:base + hp * 128 + 128]) for base in (2048, 2560, 3072, 3584))

        tiles_l = [(hp, j) for hp in range(4) for j in range(NT)]
        for idx, (hp, j) in enumerate(tiles_l):
            if idx == 0:
                items.append(("loadw", 0))
                items.append(("prep", hp, j))
            n_it = 4 * j + 4
            for m, kb in enumerate(range(4 * j + 3, -1, -1)):
                items.append(("att", hp, j, kb))
                if j == 4 and m == 0 and hp < 3:
                    items.append(("loadw", hp + 1))
                if m == n_it // 2 - 1 and idx + 1 < len(tiles_l):
                    items.append(("prep",) + tiles_l[idx + 1])

        state = {}
        deferred = []

        def proj_half(wb, j, bk, half):
            with P.group(pe):
                for c in range(4 * half, 4 * half + 4):
                    P.op(pe, lambda e, c=c: e.matmul(bk.ap, lhsT=wb.ap[:, c, :], rhs=uT_sb[:, c, j * 512:(j + 1) * 512], start=(c == 0), stop=(c == 7)), [wb, uT_t[j]], [bk])

        def prep(hp, j):
            wq, wk, wv, wg = wts[hp]
            kts = kts_p[hp % 2]
            vt, vs = vts_p[hp % 2][j], vsb_p[hp % 2]
            bk = W["pbank"].next()
            for q in range(4):
                tb = 4 * j + q
                with P.group(pe):
                    for c in range(8):
                        P.op(pe, lambda e, c=c, q=q, tb=tb, bk=bk: e.matmul(bk.ap[:, q * 128:(q + 1) * 128], lhsT=uT_sb[:, c, tb * 128:(tb + 1) * 128], rhs=wv.ap[:, c, :], start=(c == 0), stop=(c == 7)), [wv, uT_t[j]], [bk])
                yield
            bk1 = W["pbank"].next()
            proj_half(wq, j, bk1, 0)
            P.op(dve, lambda e, bk=bk: e.tensor_copy(out=vs[:, 4 * j:4 * j + 4, :], in_=bk.ap.rearrange("p (q d) -> p q d", q=4)), [bk], [vt])
            yield
            proj_half(wq, j, bk1, 1)
            yield
            bk2 = W["pbank"].next()
            proj_half(wk, j, bk2, 0)
            qt = qp.next()
            P.op(dve, lambda e: e.tensor_copy(out=qt.ap, in_=bk1.ap), [bk1], [qt])
            yield
            proj_half(wk, j, bk2, 1)
            yield
            bk3 = W["pbank"].next()
            proj_half(wg, j, bk3, 0)
            P.op(dve, lambda e: e.tensor_copy(out=kts[j].ap, in_=bk2.ap), [bk2], [kts[j]])
            yield
            proj_half(wg, j, bk3, 1)
            yield
            sg = sgp.next()
            silu_from_bank(W, bk3, sg)
            state[(hp, j)] = dict(q=qt, sg=sg)

        def s1(it):
            _, hp, j, kb = it
            qt = state[(hp, j)]["q"]
            if kb == 4 * j + 3:
                a0, a1 = Ap.next(), Ap.next()
                P.op(pool, lambda e: e.memset(a0.ap, 0.0), [], [a0])
                P.op(pool, lambda e: e.memset(a1.ap, 0.0), [], [a1])
                state["A"] = [a0, a1]
            c0 = max(0, kb - 4 * j) * 128
            kj = kb // 4
            kT = kT_ts[hp % 2].ap
            kts = kts_p[hp % 2]
            with P.group(pe):
                for hh in range(2):
                    r = slice(hh * 64, hh * 64 + 64)
                    P.op(pe, lambda e, hh=hh, r=r: e.matmul(zps[:, hh * 512 + c0:hh * 512 + 512], lhsT=kT[r, kb * 128:(kb + 1) * 128], rhs=qt.ap[r, c0:512], start=True, stop=True), [kts[kj], qt], [zb[hh]])
            et = ep.next()
            P.op(act, lambda e: e.activation(out=et.ap[:, :, c0:512], in_=z3[:, :, c0:512], func=AF.Exp, scale=0.125), zb, [et])
            spt = spp.next()
            P.op(act, lambda e: e.activation(out=spt.ap[:, :, c0:512], in_=et.ap[:, :, c0:512], func=AF.Ln, bias=1.0), [et], [spt])
            if kb >= 4 * j:
                P.op(pool, lambda e: e.tensor_tensor(out=spt.ap[:, :, c0:c0 + 128], in0=spt.ap[:, :, c0:c0 + 128], in1=mlt3, op=ALU.mult), [spt, t_cbf], [spt])
            return dict(c0=c0, sp=spt, e=et, qt=qt, A=state["A"])

        def s2(it, ctx):
            _, hp, j, kb = it
            c0, spt, et = ctx["c0"], ctx["sp"], ctx["e"]
            n = (4 * j + 3) - kb
            acur, anxt = ctx["A"][n % 2], ctx["A"][(n + 1) % 2]
            with P.group(pe):
                for hh in range(2):
                    P.op(pe, lambda e, hh=hh: e.matmul(rps[:, hh * 512 + c0:hh * 512 + 512], lhsT=tri_bf, rhs=spt.ap[:, hh, c0:512], start=True, stop=(n == 0)), [spt, t_cbf], [rb[hh]])
                    if n > 0:
                        P.op(pe, lambda e, hh=hh: e.matmul(rps[:, hh * 512 + c0:hh * 512 + 512], lhsT=ones_bf, rhs=acur.ap[:, hh, c0:512], start=False, stop=True), [acur, t_cbf], [rb[hh]])
            if kb > 0:
                P.op(dve, lambda e: e.tensor_tensor(out=anxt.ap[:, :, c0:512], in0=acur.ap[:, :, c0:512], in1=spt.ap[:, :, c0:512], op=ALU.add), [acur, spt], [anxt])
            wx = wxp.next()
            P.op(act, lambda e: e.activation(out=wx.ap[:, :, c0:512], in_=r3[:, :, c0:512], func=AF.Exp, scale=-1.0), rb, [wx])
            wt = wp.next()
            P.op(dve, lambda e: e.tensor_tensor(out=wt.ap[:, :, c0:512], in0=et.ap[:, :, c0:512], in1=wx.ap[:, :, c0:512], op=ALU.mult), [et, wx], [wt])
            if kb >= 4 * j:
                P.op(pool, lambda e: e.tensor_tensor(out=wt.ap[:, :, c0:c0 + 128], in0=wt.ap[:, :, c0:c0 + 128], in1=mlt3, op=ALU.mult), [wt, t_cbf], [wt])
            ctx["w"] = wt

        def s3(it, ctx):
            _, hp, j, kb = it
            c0, wt = ctx["c0"], ctx["w"]
            with P.group(pe):
                for hh in range(2):
                    P.op(pe, lambda e, hh=hh: e.matmul(ob.ap[hh * 64:hh * 64 + 64, c0:512], lhsT=vsb_p[hp % 2][:, kb, hh * 64:hh * 64 + 64], rhs=wt.ap[:, hh, c0:512], start=(kb == 4 * j + 3), stop=(kb == 0), skip_group_check=True), [vts_p[hp % 2][kb // 4], wt], [ob])
            if kb == 0:
                o_sb, mb = head_norm_1(W, ob, blkB_bf)
                gcol = colv_sb[:, 8 + l * 4 + hp:8 + l * 4 + hp + 1]
                sgt = state[(hp, j)]["sg"]
                deferred.append([2, lambda: head_norm_2(W, o_sb, mb, gcol, sgt, 512 + hp * 128, j)])

        pend1 = None
        pend2 = None

        def flush():
            nonlocal pend1, pend2
            if pend2 is not None:
                s3(*pend2)
                pend2 = None
            if pend1 is not None:
                s2(*pend1)
                s3(*pend1)
                pend1 = None

        gen = [None, 0]

        def run_deferred(force=False):
            for d in list(deferred):
                d[0] -= 1
                if d[0] <= 0 or force:
                    deferred.remove(d)
                    d[1]()

        def advance(k):
            for _ in range(k):
                if gen[0] is None:
                    return
                try:
                    next(gen[0])
                except StopIteration:
                    gen[0] = None

        for ii, it in enumerate(items):
            if it[0] == "loadw":
                loadw_b(it[1])
            elif it[0] == "prep":
                advance(100)
                gen[0] = prep(*it[1:])
                n_left = 0
                for it2 in items[ii + 1:]:
                    if it2[0] == "att":
                        if (it2[1], it2[2]) == (it[1], it[2]):
                            break
                        n_left += 1
                gen[1] = 100 if n_left == 0 else -(-11 // n_left)
                if n_left == 0:
                    advance(100)
            else:
                if (it[1], it[2]) not in state:
                    advance(100)
                ctx = s1(it)
                advance(gen[1])
                if pend1 is not None:
                    s2(*pend1)
                if pend2 is not None:
                    s3(*pend2)
                pend2 = pend1
                pend1 = (it, ctx)
                run_deferred()
        flush()
        run_deferred(force=True)

        if stop == "B":
            return W["yT_t"]
        P.barrier()
        ov_reset()
        Wc = {}
        Wc["eps"] = ov_alloc([128, 1], F32, "eps")
        P.op(pool, lambda e: e.memset(Wc["eps"].ap, EPS), [], [Wc["eps"]])
        Wc["wst"] = ov_pool(3, [128, 8, 128], F32, "wst")
        Wc["wbf"] = ov_pool(8, [128, 8, 128], BF16, "wbf")
        Wc["yT_t"] = yT_tile

        def a_stream(si, heads):
            Ws = dict(Wc)
            bs = bank[4 * si:4 * si + 4]
            Ws["pbank"] = RPool([bs[0]])
            Ws["mbank"] = RPool([bs[0]])
            Ws["f32"] = ov_pool(4, [128, 512], F32, f"f32_{si}")
            Ws["bf"] = ov_pool(2, [128, 512], BF16, f"bf_{si}")
            mixb, oab, stbk = bs[1], bs[2], bs[3]
            vsb_t = ov_alloc([128, NB, 128], BF16, f"vsb_{si}")
            vsb = vsb_t.ap
            vts = [Tile(vsb[:, 4 * j:4 * j + 4, :], f"v{si}_{j}") for j in range(NT)]
            sgp = ov_pool(2, [128, 512], F32, f"sg_{si}")
            f32b = ov_pool(9, [128, 512], F32, f"fa_{si}")
            bfb = ov_pool(8, [128, 512], BF16, f"ba_{si}")
            khp = ov_pool(2, [128, 4, 128], BF16, f"kht_{si}")
            atp = ov_pool(2, [128, 4, 64], BF16, f"at_{si}")
            gdp = ov_pool(2, [128, 8], F32, f"gd_{si}")
            s32p = ov_pool(2, [128, 128], F32, f"s32_{si}")
            sbfp = ov_pool(5, [128, 128], BF16, f"sbf_{si}")
            atv = mixb.ap[:, 0:256]
            tpv = stbk.ap[:, 256:512].bitcast(BF16)
            for ha in heads:
                wq = load_w(Ws, w_in[l, :, ha * 128:ha * 128 + 128])
                wf = load_w(Ws, w_in[l, :, 512 + ha * 128:512 + ha * 128 + 128])
                wi = load_w(Ws, w_in[l, :, 1024 + ha * 128:1024 + ha * 128 + 128])
                wg = load_w(Ws, w_in[l, :, 1536 + ha * 128:1536 + ha * 128 + 128])
                omlc = lbw_sb[:, 24 + l * 4 + ha:24 + l * 4 + ha + 1]
                s32 = s32p.next()
                sbf = sbfp.next()
                P.op(pool, lambda e, s32=s32: e.memset(s32.ap, 0.0), [], [s32])
                P.op(pool, lambda e, sbf=sbf: e.memset(sbf.ap, 0.0), [], [sbf])
                yield
                for j in range(NT):
                    with P.group(pe):
                        bk = Ws["pbank"].next()
                        proj_fm(Ws, wq, j, bk)
                    yield
                    qa = f32b.next()
                    silu_from_bank(Ws, bk, qa)
                    yield
                    with P.group(pe):
                        bk = Ws["pbank"].next()
                        proj_fm(Ws, wf, j, bk)
                    yield
                    ka = f32b.next()
                    silu_from_bank(Ws, bk, ka, lnscale_col=omlc, sign=1.0)
                    lf = f32b.next()
                    P.op(act, lambda e, lf=lf, ka=ka: e.activation(out=lf.ap, in_=ka.ap, func=AF.Ln, scale=-1.0, bias=1.0), [ka], [lf])
                    yield
                    yield
                    with P.group(pe):
                        bk = Ws["pbank"].next()
                        proj_fm(Ws, wg, j, bk)
                    yield
                    sg = sgp.next()
                    silu_from_bank(Ws, bk, sg)
                    yield
                    bk = Ws["pbank"].next()
                    for q in range(4):
                        tb = 4 * j + q
                        with P.group(pe):
                            for c in range(8):
                                P.op(pe, lambda e, c=c, q=q, tb=tb, bk=bk, wi=wi: e.matmul(bk.ap[:, q * 128:(q + 1) * 128], lhsT=uT_sb[:, c, tb * 128:(tb + 1) * 128], rhs=wi.ap[:, c, :], start=(c == 0), stop=(c == 7)), [wi, uT_t[j]], [bk])
                    P.op(dve, lambda e, bk=bk, j=j: e.tensor_copy(out=vsb[:, 4 * j:4 * j + 4, :], in_=bk.ap.rearrange("p (q d) -> p q d", q=4)), [bk], [vts[j]])
                    yield
                    bt = f32b.next()
                    P.op(dve, lambda e, bt=bt, lf=lf: e.tensor_tensor_scan(out=bt.ap, data0=mreset_f, data1=lf.ap, initial=0.0, op0=ALU.mult, op1=ALU.add), [lf, t_cst], [bt])
                    yield
                    yield
                    b3 = bt.ap.rearrange("p (c s) -> p c s", s=64)
                    d1 = f32b.next()
                    d13 = d1.ap.rearrange("p (c s) -> p c s", s=64)
                    P.op(dve, lambda e, d13=d13, b3=b3: e.tensor_tensor(out=d13, in0=b3, in1=b3[:, :, 31:32].broadcast_to([128, 8, 64]), op=ALU.subtract), [bt], [d1])
                    yield
                    d2 = f32b.next()
                    d23 = d2.ap.rearrange("p (c s) -> p c s", s=64)
                    P.op(dve, lambda e, d23=d23, b3=b3: e.tensor_tensor(out=d23, in0=b3[:, :, 63:64].broadcast_to([128, 8, 64]), in1=b3, op=ALU.subtract), [bt], [d2])
                    yield
                    yield
                    ex = f32b.next()
                    ex2 = f32b.next()
                    ex3 = f32b.next()
                    P.op(act, lambda e, ex=ex, d1=d1: e.activation(out=ex.ap, in_=d1.ap, func=AF.Exp), [d1], [ex])
                    yield
                    P.op(act, lambda e, ex2=ex2, d1=d1: e.activation(out=ex2.ap, in_=d1.ap, func=AF.Exp, scale=-1.0), [d1], [ex2])
                    yield
                    P.op(act, lambda e, ex3=ex3, bt=bt: e.activation(out=ex3.ap, in_=bt.ap, func=AF.Exp), [bt], [ex3])
                    yield
                    P.op(act, lambda e, d2=d2: e.activation(out=d2.ap, in_=d2.ap, func=AF.Exp), [d2], [d2])
                    yield
                    gd = gdp.next()
                    P.op(act, lambda e, gd=gd, bt=bt: e.activation(out=gd.ap, in_=bt.ap[:, 63:512:64], func=AF.Exp), [bt], [gd])
                    yield
                    yield
                    qin, kin, qdec, khat = bfb.next(), bfb.next(), bfb.next(), bfb.next()
                    P.op(dve, lambda e, qin=qin, qa=qa, ex=ex: e.tensor_tensor(out=qin.ap, in0=qa.ap, in1=ex.ap, op=ALU.mult), [qa, ex], [qin])
                    yield
                    P.op(dve, lambda e, kin=kin, ka=ka, ex2=ex2: e.tensor_tensor(out=kin.ap, in0=ka.ap, in1=ex2.ap, op=ALU.mult), [ka, ex2], [kin])
                    yield
                    P.op(dve, lambda e, khat=khat, ka=ka, d2=d2: e.tensor_tensor(out=khat.ap, in0=ka.ap, in1=d2.ap, op=ALU.mult), [ka, d2], [khat])
                    yield
                    P.op(dve, lambda e, qdec=qdec, qa=qa, ex3=ex3: e.tensor_tensor(out=qdec.ap, in0=qa.ap, in1=ex3.ap, op=ALU.mult), [qa, ex3], [qdec])
                    yield
                    yield
                    with P.group(pe):
                        for q in range(4):
                            P.op(pe, lambda e, q=q, khat=khat: e.transpose(out=tpv[:, q * 128:(q + 1) * 128], in_=khat.ap[:, q * 128:(q + 1) * 128], identity=ident_bf), [khat, t_cbf], [stbk])
                    yield
                    kht = khp.next()
                    P.op(dve, lambda e, kht=kht: e.tensor_copy(out=kht.ap, in_=tpv.rearrange("p (q k) -> p q k", q=4)), [stbk], [kht])
                    yield
                    with P.group(pe):
                        for c in range(8):
                            hf = (c % 2) * 64
                            P.op(pe, lambda e, c=c, hf=hf, kin=kin, qin=qin: e.matmul(mixb.ap[hf:hf + 64, (c // 2) * 64:(c // 2) * 64 + 64], lhsT=kin.ap[:, c * 64:(c + 1) * 64], rhs=qin.ap[:, c * 64:(c + 1) * 64], start=True, stop=True), [kin, qin], [mixb])
                    yield
                    at = atp.next()
                    P.op(dve, lambda e, at=at: e.tensor_tensor(out=at.ap, in0=atv.rearrange("p (q t) -> p q t", q=4), in1=mle_f.unsqueeze(1).broadcast_to([128, 4, 64]), op=ALU.mult), [mixb, t_cst], [at])
                    yield
                    yield
                    mode[si] = "chunk"
                    for half in range(2):
                        cs = list(range(4 * half, 4 * half + 4))
                        with P.group(pe):
                            for c in cs:
                                tb = 4 * j + c // 2
                                rows = slice((c % 2) * 64, (c % 2) * 64 + 64)
                                sbank = stbk if c % 2 == 0 else mixb
                                scol = ((c % 4) // 2) * 128
                                P.op(pe, lambda e, c=c, rows=rows, tb=tb, kht=kht, sbank=sbank, scol=scol: e.matmul(sbank.ap[:, scol:scol + 128], lhsT=kht.ap[rows, c // 2, :], rhs=vsb[rows, tb, :], start=True, stop=True), [kht, vts[j]], [sbank])
                        yield
                        states = [sbf]
                        for c in cs:
                            sbank = stbk if c % 2 == 0 else mixb
                            scol = ((c % 4) // 2) * 128
                            sv = sbank.ap[:, scol:scol + 128]
                            s32n = s32p.next()
                            sbfn = sbfp.next()
                            P.op(dve, lambda e, sbfn=sbfn, s32=s32, gd=gd, c=c, sv=sv: e.scalar_tensor_tensor(out=sbfn.ap, in0=s32.ap, scalar=gd.ap[:, c:c + 1], in1=sv, op0=ALU.mult, op1=ALU.add), [s32, gd, sbank], [sbfn])
                            P.op(dve, lambda e, s32n=s32n, s32=s32, gd=gd, c=c, sv=sv: e.scalar_tensor_tensor(out=s32n.ap, in0=s32.ap, scalar=gd.ap[:, c:c + 1], in1=sv, op0=ALU.mult, op1=ALU.add), [s32, gd, sbank], [s32n])
                            s32 = s32n
                            states.append(sbfn)
                        yield
                        with P.group(pe):
                            for k, c in enumerate(cs):
                                tb = 4 * j + c // 2
                                rows = slice((c % 2) * 64, (c % 2) * 64 + 64)
                                sprev = states[k]
                                P.op(pe, lambda e, c=c, sprev=sprev, qdec=qdec: e.matmul(oab.ap[:, c * 64:(c + 1) * 64], lhsT=sprev.ap, rhs=qdec.ap[:, c * 64:(c + 1) * 64], start=True, stop=False), [sprev, qdec], [oab])
                                P.op(pe, lambda e, c=c, rows=rows, tb=tb, at=at: e.matmul(oab.ap[:, c * 64:(c + 1) * 64], lhsT=vsb[rows, tb, :], rhs=at.ap[rows, c // 2, :], start=False, stop=True), [vts[j], at], [oab])
                        sbf = states[-1]
                        yield
                    mode[si] = "op"
                    head_norm_store(Ws, oab, blkA_bf, colv_sb[:, l * 4 + ha:l * 4 + ha + 1], sg, ha * 128, j)
                    yield

        mode = {0: "op", 1: "op"}
        gens = {0: a_stream(0, [0, 1]), 1: a_stream(1, [2, 3])}

        def adv(si, k):
            for _ in range(k):
                if si not in gens:
                    return
                try:
                    next(gens[si])
                except StopIteration:
                    del gens[si]

        adv(0, 14)
        while gens:
            for si in (0, 1):
                if si not in gens:
                    continue
                other = 1 - si
                if mode[si] == "chunk":
                    adv(si, 1)
                elif other in gens and mode[other] == "chunk":
                    adv(si, 3)
                else:
                    adv(si, 1)
        W = Wc
        return W["yT_t"]

    def phase_D(l, yT_tile, last):
        ov_reset()
        W = {}
        W["eps"] = ov_alloc([128, 1], F32, "eps")
        P.op(pool, lambda e: e.memset(W["eps"].ap, EPS), [], [W["eps"]])
        W["wst"] = ov_pool(2, [128, 8, 128], F32, "wst")
        W["junk"] = ov_pool(1, [128, D], BF16, "junk")
        W["small"] = ov_pool(12, [128, 4], F32, "small")
        W["ub"] = ov_pool(2, [128, D], BF16, "ub")
        W["tp"] = RPool([bank[6], bank[7]])
        wo = ov_alloc([128, 8, D], BF16, "wo")
        wg = ov_alloc([128, 8, D], BF16, "wg")
        wpj = ov_alloc([128, 2, D], BF16, "wpj")
        for g in range(8):
            for (dst, src, rc) in ((wo, w_out, 8), (wg, w_pg, 8), (wpj, w_pp, 2)):
                ws = W["wst"].next()
                P.op(sp, lambda e, ws=ws, src=src, g=g, rc=rc: e.dma_start(out=ws.ap[:, 0:rc, :], in_=src[l, :, g * 128:(g + 1) * 128].rearrange("(c p) n -> p c n", p=128)), [], [ws])
                P.op(pool, lambda e, ws=ws, dst=dst, g=g, rc=rc: e.tensor_copy(out=dst.ap[:, :, g * 128:(g + 1) * 128], in_=ws.ap[:, 0:rc, :]), [ws], [dst])
        gn = ov_alloc([128, D], F32, "gn")
        pn = ov_alloc([128, D], F32, "pn")
        nmn = ov_alloc([128, D], F32, "nmn")
        P.op(sp, lambda e: e.dma_start(out=gn.ap, in_=vecs[2 + l]), [], [gn])
        P.op(sp, lambda e: e.dma_start(out=pn.ap, in_=vecs[4 + l]), [], [pn])
        P.op(sp, lambda e: e.dma_start(out=nmn.ap, in_=vecs[6] if last else vecs[l + 1]), [], [nmn])
        hbp = ov_pool(2, [128, D], F32, "hb")
        hmp = ov_pool(3, [128, D], F32, "hm")
        gtp = ov_pool(1, [128, D], F32, "gt")
        pep = ov_pool(3, [128, D], F32, "pe")
        otp = ov_pool(1 if last else 0, [128, D], F32, "ot")
        hnp = ov_pool(2, [128, D], F32, "hn")
        ybp = ov_pool(2, [128, 8, 512], BF16, "yb")
        pbp = ov_pool(2, [128, 256], F32, "pb")
        pbfp = ov_pool(2, [128, 256], BF16, "pbf")
        ptp = ov_pool(2, [128, 2, 128], BF16, "pT")
        ugp = ov_pool(2, [128, 8, 128], BF16, "ugT")
        h_src = x if l == 0 else hres
        hres_t = Tile(None, "hres")
        mixP = (psbig[0], bank[0], bank[1])
        gP = (psbig[1], bank[2], bank[3])
        pP = (psbig[2], bank[4], bank[5])
        ybs = {}

        def rstd_chain(src_ap, src_tiles, on_dve=False):
            junk = W["junk"].next()
            sm = W["small"].next()
            if on_dve:
                P.op(dve, lambda e: e.scalar_tensor_tensor(out=junk.ap, in0=src_ap, scalar=1.0, in1=src_ap, op0=ALU.mult, op1=ALU.mult, accum_out=sm.ap[:, 0:1]), src_tiles, [junk, sm])
            else:
                P.op(act, lambda e: e.activation(out=junk.ap, in_=src_ap, func=AF.Square, accum_out=sm.ap[:, 0:1]), src_tiles, [junk, sm])
            P.op(act, lambda e: e.activation(out=sm.ap[:, 1:2], in_=sm.ap[:, 0:1], func=AF.Ln, scale=1.0 / D, bias=W["eps"].ap), [sm, W["eps"]], [sm])
            P.op(act, lambda e: e.activation(out=sm.ap[:, 2:3], in_=sm.ap[:, 1:2], func=AF.Exp, scale=-0.5), [sm], [sm])
            return sm

        ctxs = {}

        def st_a(tb):
            j, q = tb // 4, tb % 4
            if q == 0:
                yb = ybp.next()
                P.op(sp, lambda e: e.dma_start(out=yb.ap, in_=yT[:, j * 512:(j + 1) * 512].rearrange("(c p) t -> p c t", p=128)), [yT_tile], [yb])
                ybs[j] = yb
            yb = ybs[j]
            hb = hbp.next()
            P.op(sp, lambda e: e.dma_start(out=hb.ap, in_=h_src[tb * 128:(tb + 1) * 128, :]), [hres_t] if l > 0 else [], [hb])
            pb = pbp.next()
            P.op(sp, lambda e: e.dma_start(out=pb.ap, in_=p_in[l, tb * 128:(tb + 1) * 128, :]), [], [pb])
            with P.group(pe):
                for half in range(2):
                    bk = mixP[1 + half]
                    for c in range(8):
                        P.op(pe, lambda e, c=c, half=half, bk=bk: e.matmul(bk.ap, lhsT=yb.ap[:, c, q * 128:(q + 1) * 128], rhs=wo.ap[:, c, half * 512:(half + 1) * 512], start=(c == 0), stop=(c == 7)), [yb, wo], [bk])
            hm = hmp.next()
            P.op(dve, lambda e: e.tensor_tensor(out=hm.ap, in0=mixP[0][:, :], in1=hb.ap, op=ALU.add), [mixP[1], mixP[2], hb], [hm])
            pbf = pbfp.next()
            P.op(pool, lambda e: e.tensor_copy(out=pbf.ap, in_=pb.ap), [pb], [pbf])
            ctxs[tb] = dict(hm=hm, pbf=pbf)

        def st_b1(tb):
            cx = ctxs[tb]
            hm = cx["hm"]
            sm = rstd_chain(hm.ap, [hm], on_dve=True)
            ug = W["ub"].next()
            P.op(dve, lambda e: e.scalar_tensor_tensor(out=ug.ap, in0=hm.ap, scalar=sm.ap[:, 2:3], in1=gn.ap, op0=ALU.mult, op1=ALU.mult), [hm, sm, gn], [ug])
            cx["ug"] = ug

        def st_b2(tb):
            cx = ctxs[tb]
            ug = cx["ug"]
            tp = W["tp"].next()
            tpv = tp.ap.bitcast(BF16)
            with P.group(pe):
                for c in range(8):
                    P.op(pe, lambda e, c=c: e.transpose(out=tpv[:, c * 128:(c + 1) * 128], in_=ug.ap[:, c * 128:(c + 1) * 128], identity=ident_bf), [ug, t_cbf], [tp])
            ugT = ugp.next()
            P.op(act, lambda e: e.activation(out=ugT.ap, in_=tpv.rearrange("p (c t) -> p c t", c=8), func=AF.Copy), [tp], [ugT])
            cx["ugT"] = ugT

        def st_b3(tb):
            cx = ctxs[tb]
            pbf = cx["pbf"]
            tp2 = W["tp"].next()
            tpv2 = tp2.ap.bitcast(BF16)
            with P.group(pe):
                for c in range(2):
                    P.op(pe, lambda e, c=c: e.transpose(out=tpv2[:, c * 128:(c + 1) * 128], in_=pbf.ap[:, c * 128:(c + 1) * 128], identity=ident_bf), [pbf, t_cbf], [tp2])
            pT = ptp.next()
            P.op(act, lambda e: e.activation(out=pT.ap, in_=tpv2[:, 0:256].rearrange("p (c t) -> p c t", c=2), func=AF.Copy), [tp2], [pT])
            cx["pT"] = pT

        def st_c(tb):
            cx = ctxs[tb]
            hm, ugT, pT = cx["hm"], cx["ugT"], cx["pT"]
            with P.group(pe):
                for half in range(2):
                    bk = pP[1 + half]
                    for c in range(2):
                        P.op(pe, lambda e, c=c, half=half, bk=bk: e.matmul(bk.ap, lhsT=pT.ap[:, c, :], rhs=wpj.ap[:, c, half * 512:(half + 1) * 512], start=(c == 0), stop=(c == 1)), [pT, wpj], [bk])
                for half in range(2):
                    bk = gP[1 + half]
                    for c in range(8):
                        P.op(pe, lambda e, c=c, half=half, bk=bk: e.matmul(bk.ap, lhsT=ugT.ap[:, c, :], rhs=wg.ap[:, c, half * 512:(half + 1) * 512], start=(c == 0), stop=(c == 7)), [ugT, wg], [bk])
            sm2 = rstd_chain(pP[0][:, :], [pP[1], pP[2]])
            pet = pep.next()
            P.op(dve, lambda e: e.scalar_tensor_tensor(out=pet.ap, in0=pP[0][:, :], scalar=sm2.ap[:, 2:3], in1=pn.ap, op0=ALU.mult, op1=ALU.mult), [pP[1], pP[2], sm2, pn], [pet])
            gt = gtp.next()
            P.op(act, lambda e: e.activation(out=gt.ap, in_=gP[0][:, :], func=AF.Exp, scale=-1.0), [gP[1], gP[2]], [gt])
            P.op(act, lambda e: e.activation(out=gt.ap, in_=gt.ap, func=AF.Ln, bias=1.0), [gt], [gt])
            P.op(act, lambda e: e.activation(out=gt.ap, in_=gt.ap, func=AF.Exp, scale=-1.0), [gt], [gt])
            P.op(pool, lambda e: e.tensor_tensor(out=gt.ap, in0=gt.ap, in1=pet.ap, op=ALU.mult), [pet, gt], [gt])
            hn = hnp.next()
            P.op(pool, lambda e: e.tensor_tensor(out=hn.ap, in0=gt.ap, in1=hm.ap, op=ALU.add), [gt, hm], [hn])
            if not last:
                P.op(sp, lambda e: e.dma_start(out=hres[tb * 128:(tb + 1) * 128, :], in_=hn.ap), [hn], [hres_t])
                if debug and l == 0:
                    P.op(sp, lambda e: e.dma_start(out=dbg["h1"][tb * 128:(tb + 1) * 128, :], in_=hn.ap), [hn], [])
            cx["hn"] = hn

        def st_d1(tb):
            cx = ctxs[tb]
            hn = cx["hn"]
            if not last:
                cx["ub"] = emit_u1(hn, nmn, W, on_dve=True)
            else:
                sm3 = rstd_chain(hn.ap, [hn], on_dve=True)
                ot = otp.next()
                P.op(dve, lambda e: e.scalar_tensor_tensor(out=ot.ap, in0=hn.ap, scalar=sm3.ap[:, 2:3], in1=nmn.ap, op0=ALU.mult, op1=ALU.mult), [hn, sm3, nmn], [ot])
                P.op(sp, lambda e: e.dma_start(out=out[tb * 128:(tb + 1) * 128, :], in_=ot.ap), [ot], [])

        def st_d2(tb):
            if not last:
                emit_u2(ctxs[tb]["ub"], tb, W)
            del ctxs[tb]

        ok = lambda t: 0 <= t < NB
        for i in range(NB + 3):
            if ok(i - 1):
                st_b1(i - 1)
            if ok(i - 3):
                st_d1(i - 3)
            if ok(i):
                st_a(i)
            if ok(i - 2):
                st_c(i - 2)
            if ok(i - 1):
                st_b2(i - 1)
            if ok(i - 3):
                st_d2(i - 3)
            if ok(i - 1):
                st_b3(i - 1)

    phase_A0()
    for l in range(n_layers):
        P.barrier()
        if debug and l == 0:
            for c in range(8):
                P.op(sp, lambda e, c=c: e.dma_start(out=dbg["uT0"][:, c, :], in_=uT_sb[:, c, :]), uT_t, [])
        if stop == "A0":
            break
        yt = phase_BC(l, stop)
        P.barrier()
        if debug and l == 0:
            for c in range(8):
                P.op(sp, lambda e, c=c: e.dma_start(out=dbg["yT0"][c * 128:(c + 1) * 128, :], in_=yT[c * 128:(c + 1) * 128, :]), [yt], [])
        if stop in ("B", "BC"):
            break
        phase_D(l, yt, last=(l == n_layers - 1))
    P.emit(nc, st)
    st.close()
    return nc


def make_consts():
    c = np.zeros((128, C_END), np.float32)
    i = np.arange(128)
    c[:, C_ID:C_ID + 128] = np.eye(128)
    c[:, C_TRI:C_TRI + 128] = (i[:, None] >= i[None, :])
    c[:, C_ONE:C_ONE + 128] = 1.0
    c[:, C_BA:C_BA + 128] = 1.0 / 128
    c[:, C_BB:C_BB + 128] = ((i[:, None] // 64) == (i[None, :] // 64)) / 64.0
    c[:, C_LT:C_LT + 128] = (i[:, None] < i[None, :])
    c[:, C_LE:C_LE + 64] = ((i[:, None] % 64) <= np.arange(64)[None, :])
    m = np.ones((128, 512), np.float32)
    m[:, ::64] = 0.0
    c[:, C_MR:C_MR + 512] = m
    return c


_NC_CACHE = {}


def kernel(x, p, norm_mix, w_in, a_out_norm, b_out_norm, w_out, lb_logits,
           ple_gate_norm, w_ple_gate, w_ple_proj, ple_post_norm, final_norm):
    f = lambda a: np.ascontiguousarray(np.asarray(a, dtype=np.float32))
    x, p = f(x), f(p)
    B = x.shape[0]
    rep = lambda v: np.broadcast_to(f(v)[None, :], (128, D))
    vecs = np.ascontiguousarray(np.stack([rep(norm_mix[0]), rep(norm_mix[1]), rep(ple_gate_norm[0]), rep(ple_gate_norm[1]),
                                          rep(ple_post_norm[0]), rep(ple_post_norm[1]), rep(final_norm)], axis=0))
    colv = np.zeros((128, 24), np.float32)
    aon, bon, lbl = f(a_out_norm), f(b_out_norm), f(lb_logits)
    for l in range(DEPTH):
        for h in range(4):
            colv[:, l * 4 + h] = aon[l, h * 128:(h + 1) * 128]
            colv[:, 8 + l * 4 + h] = bon[l, h * 128:(h + 1) * 128]
            colv[:, 16 + l * 4 + h] = lbl[l, h * 128:(h + 1) * 128]
    cst = make_consts()
    if "nc" not in _NC_CACHE:
        _NC_CACHE["nc"] = build_program()
    nc = _NC_CACHE["nc"]
    shared = dict(w_in=f(w_in), w_out=f(w_out), w_pg=f(w_ple_gate), w_pp=f(w_ple_proj), vecs=vecs, colv=colv, cst=cst)
    in_maps = []
    for b in range(B):
        d = dict(shared)
        d["x"] = np.ascontiguousarray(x[b])
        d["p"] = np.ascontiguousarray(p[:, b])
        in_maps.append(d)
    res = run_bass_kernel_spmd(nc, in_maps, core_ids=list(range(B)))
    return np.stack([np.asarray(r["out"], dtype=np.float32) for r in res.results], axis=0)
```
